# Optimizing a Trainium2 kernel written in Bass

```python
import math
import jax, jax.numpy as jnp
from jax import lax
import numpy as np

D_MODEL = 2048
BATCH = 2
SEQ = 8192
DEPTH = 4

N_MIXERS = 3
ATTN_HEADS = 16
ATTN_HEAD_DIM = 128
DILATION_PATTERNS = ((128, 1), (512, 4), (2048, 16))
N_DIL = len(DILATION_PATTERNS)
ATTN_BLOCK = 128
SSM_CH = 16
SSM_GROUPS = D_MODEL // SSM_CH
SSM_STATE = 64
SSM_DT_MIN = 0.001
SSM_DT_MAX = 0.1
RWKV_HEAD_DIM = 64
RWKV_HEADS = D_MODEL // RWKV_HEAD_DIM
RWKV_DECAY_LORA = 96
RWKV_AAA_LORA = 96
RWKV_GATE_LORA = 256
RWKV_GN_EPS = RWKV_HEAD_DIM * 1e-5
D_FF = 4 * D_MODEL
NORM_EPS = 1e-5
N_ATTN_LAYERS = (DEPTH + 2) // 3
N_SSM_LAYERS = (DEPTH + 1) // 3
N_RWKV_LAYERS = DEPTH // 3

kernel_name = 'hybrid_dilated_attn_s5_rwkv7_trunk'


def rms_norm(x, g):
    xf = x.astype(jnp.float32)
    y = xf * lax.rsqrt(jnp.mean(xf * xf, axis=-1, keepdims=True) + NORM_EPS)
    return (y * g.astype(jnp.float32)).astype(x.dtype)


def alibi_slopes(n):
    return 2.0 ** (-8.0 * jnp.arange(1, n + 1, dtype=jnp.float32) / n)


def dilated_window_branch(q, k, v, slopes, dilation, lookback):
    b, s, h, e = q.shape
    L = s // dilation
    n = b * dilation
    nb = -(-L // ATTN_BLOCK)
    lp = nb * ATTN_BLOCK

    def to_sub(a):
        return a.reshape(b, L, dilation, h, e).transpose(0, 2, 1, 3, 4).reshape(n, L, h, e)

    qs = jnp.pad(to_sub(q), ((0, 0), (0, lp - L), (0, 0), (0, 0))).reshape(n, nb, ATTN_BLOCK, h, e)

    def windows(a):
        a = jnp.pad(to_sub(a), ((0, 0), (ATTN_BLOCK, lp - L), (0, 0), (0, 0)))
        a = a.reshape(n, nb + 1, ATTN_BLOCK, h, e)
        return jnp.concatenate([a[:, :-1], a[:, 1:]], axis=2)

    kw, vw = windows(k), windows(v)
    qi = jnp.arange(ATTN_BLOCK)[:, None]
    kj = jnp.arange(2 * ATTN_BLOCK)[None, :]
    dist = ATTN_BLOCK + qi - kj
    key_idx = jnp.arange(nb)[:, None, None] * ATTN_BLOCK + kj[None] - ATTN_BLOCK
    valid = (dist >= 0) & (dist <= lookback) & (key_idx >= 0)
    bias = -(slopes[:, None, None] * dilation) * dist.astype(jnp.float32)
    scores = jnp.einsum('nbqhe,nbkhe->nbhqk', qs, kw).astype(jnp.float32) * (e ** -0.5)
    scores = jnp.where(valid[None, :, None], scores + bias, -jnp.inf)
    m = jnp.max(scores, axis=-1, keepdims=True)
    p = jnp.exp(scores - m)
    denom = jnp.sum(p, axis=-1, keepdims=True)
    out = jnp.einsum('nbhqk,nbkhe->nbqhe', p / denom, vw.astype(jnp.float32))
    lse = jnp.transpose((m + jnp.log(denom))[..., 0], (0, 1, 3, 2))

    def from_sub(a):
        a = a.reshape((n, lp) + a.shape[3:])[:, :L]
        a = a.reshape((b, dilation, L) + a.shape[2:])
        a = jnp.swapaxes(a, 1, 2)
        return a.reshape((b, s) + a.shape[3:])

    return from_sub(out), from_sub(lse)


def dilated_attention(h, w_qkv, w_o):
    b, s, _ = h.shape
    w = w_qkv.reshape(D_MODEL, N_DIL, 3, ATTN_HEADS, ATTN_HEAD_DIM)
    slopes = alibi_slopes(N_DIL * ATTN_HEADS).reshape(N_DIL, ATTN_HEADS)
    outs, lses = [], []
    for g, (window, dilation) in enumerate(DILATION_PATTERNS):
        qkv = jnp.einsum('bsd,dthe->tbshe', h, w[:, g])
        o, l = dilated_window_branch(qkv[0], qkv[1], qkv[2], slopes[g], dilation, window // dilation)
        outs.append(o)
        lses.append(l)
    wts = jax.nn.softmax(jnp.stack(lses), axis=0)
    o = jnp.sum(wts[..., None] * jnp.stack(outs), axis=0)
    return o.reshape(b, s, ATTN_HEADS * ATTN_HEAD_DIM).astype(h.dtype) @ w_o


def s5_mixer(h, w_in, log_dt, a_re, a_im, b_re, b_im, c_re, c_im, d_skip, w_out):
    bsz, s, _ = h.shape
    f32 = jnp.float32
    u = (h @ w_in).reshape(bsz, s, SSM_GROUPS, SSM_CH).astype(f32)
    a_re = a_re.astype(f32)
    a_im = a_im.astype(f32)
    dt = jnp.exp(log_dt.astype(f32))[:, None]
    mag = jnp.exp(dt * a_re)
    ab_re = mag * jnp.cos(dt * a_im)
    ab_im = mag * jnp.sin(dt * a_im)
    den = a_re * a_re + a_im * a_im
    zr = ab_re - 1.0
    cr = (zr * a_re + ab_im * a_im) / den
    ci = (ab_im * a_re - zr * a_im) / den
    bb_re = cr[..., None] * b_re - ci[..., None] * b_im
    bb_im = cr[..., None] * b_im + ci[..., None] * b_re
    bu_re = jnp.einsum('bsgc,gpc->bsgp', u, bb_re)
    bu_im = jnp.einsum('bsgc,gpc->bsgp', u, bb_im)
    a_seq_re = jnp.broadcast_to(ab_re, (1, s, SSM_GROUPS, SSM_STATE))
    a_seq_im = jnp.broadcast_to(ab_im, (1, s, SSM_GROUPS, SSM_STATE))

    def combine(e1, e2):
        a1r, a1i, x1r, x1i = e1
        a2r, a2i, x2r, x2i = e2
        return (a2r * a1r - a2i * a1i, a2r * a1i + a2i * a1r,
                a2r * x1r - a2i * x1i + x2r, a2r * x1i + a2i * x1r + x2i)

    _, _, st_re, st_im = lax.associative_scan(combine, (a_seq_re, a_seq_im, bu_re, bu_im), axis=1)
    y = (jnp.einsum('bsgp,gcp->bsgc', st_re, c_re) - jnp.einsum('bsgp,gcp->bsgc', st_im, c_im)
         + d_skip * u)
    y = jax.nn.gelu(y.reshape(bsz, s, SSM_GROUPS * SSM_CH)).astype(h.dtype)
    z = y @ w_out
    return z[..., :D_MODEL] * jax.nn.sigmoid(z[..., D_MODEL:])


def rwkv7_mixer(h, mu, w_rkv, w0, w1, w2, a0, a1, a2, g1, g2, k_k, k_a, r_k, ln_w, ln_b, w_o):
    b, s, d = h.shape
    f32 = jnp.float32
    H, N = RWKV_HEADS, RWKV_HEAD_DIM
    xx = jnp.pad(h, ((0, 0), (1, 0), (0, 0)))[:, :-1] - h
    x_rkv = h[:, :, None, :] + xx[:, :, None, :] * mu[:3]
    xw = h + xx * mu[3]
    xa = h + xx * mu[4]
    xg = h + xx * mu[5]
    rkv = jnp.einsum('bsjd,jde->bsje', x_rkv, w_rkv)
    r, k, v = rkv[:, :, 0], rkv[:, :, 1], rkv[:, :, 2]
    w = -jax.nn.softplus(-(w0 + jnp.tanh(xw @ w1) @ w2).astype(f32)) - 0.5
    decay = jnp.exp(-jnp.exp(w))
    a = jax.nn.sigmoid((a0 + (xa @ a1) @ a2).astype(f32))
    g = jax.nn.sigmoid(xg @ g1) @ g2
    heads = lambda t: t.astype(f32).reshape(b, s, H, N)
    r, k, v, decay, a = heads(r), heads(k), heads(v), heads(decay), heads(a)
    kk = k * k_k.astype(f32).reshape(H, N)
    kk = kk / jnp.maximum(jnp.sqrt(jnp.sum(kk * kk, axis=-1, keepdims=True)), 1e-12)
    k = k * (1.0 + (a - 1.0) * k_a.astype(f32).reshape(H, N))

    def step(state, inp):
        r_t, w_t, k_t, v_t, a_t, b_t = inp
        sa = jnp.einsum('bhvk,bhk->bhv', state, a_t)
        state = (state * w_t[:, :, None, :] + sa[..., None] * b_t[:, :, None, :]
                 + v_t[..., None] * k_t[:, :, None, :])
        return state, jnp.einsum('bhvk,bhk->bhv', state, r_t)

    tm = lambda t: jnp.moveaxis(t, 1, 0)
    s0 = jnp.zeros((b, H, N, N), f32)
    _, y = lax.scan(step, s0, (tm(r), tm(decay), tm(k), tm(v), tm(-kk), tm(kk * a)))
    y = jnp.moveaxis(y, 0, 1)
    mean = jnp.mean(y, axis=-1, keepdims=True)
    var = jnp.mean(jnp.square(y - mean), axis=-1, keepdims=True)
    y = ((y - mean) * lax.rsqrt(var + RWKV_GN_EPS)).reshape(b, s, d) * ln_w + ln_b
    bonus = jnp.sum(r * k * r_k.astype(f32), axis=-1, keepdims=True) * v
    y = y + bonus.reshape(b, s, d)
    return (y * g).astype(h.dtype) @ w_o


def squared_relu_mlp(h, w1, w2):
    return jnp.square(jax.nn.relu(h @ w1)) @ w2


def setup_inputs(seed: int = 0) -> dict:
    key = jax.random.key(seed)
    ks = jax.random.split(key, 40)
    f32 = jnp.float32

    def nrm(i, shape, scale=1.0):
        return jax.random.normal(ks[i], shape, f32) * scale

    D = D_MODEL
    HE = ATTN_HEADS * ATTN_HEAD_DIM
    G, C, P = SSM_GROUPS, SSM_CH, SSM_STATE
    NA, NB, NC = N_ATTN_LAYERS, N_SSM_LAYERS, N_RWKV_LAYERS
    return {
        'x': nrm(0, (BATCH, SEQ, D)),
        'norm_mix': 1.0 + nrm(1, (DEPTH, D), 0.02),
        'norm_mlp': 1.0 + nrm(2, (DEPTH, D), 0.02),
        'norm_f': 1.0 + nrm(3, (D,), 0.02),
        'attn_w_qkv': nrm(4, (NA, D, N_DIL * 3 * HE), D ** -0.5),
        'attn_w_o': nrm(5, (NA, HE, D), HE ** -0.5),
        'ssm_w_in': nrm(6, (NB, D, G * C), D ** -0.5),
        'ssm_log_dt': jax.random.uniform(ks[7], (NB, G), f32, math.log(SSM_DT_MIN), math.log(SSM_DT_MAX)),
        'ssm_a_re': -0.5 + nrm(8, (NB, G, P), 0.01),
        'ssm_a_im': jnp.pi * jnp.arange(P, dtype=f32) + nrm(9, (NB, G, P), 0.01),
        'ssm_b_re': nrm(10, (NB, G, P, C), (2 * C) ** -0.5),
        'ssm_b_im': nrm(11, (NB, G, P, C), (2 * C) ** -0.5),
        'ssm_c_re': nrm(12, (NB, G, C, P), 1.0),
        'ssm_c_im': nrm(13, (NB, G, C, P), 1.0),
        'ssm_d': nrm(14, (NB, G, C)),
        'ssm_w_out': nrm(15, (NB, G * C, 2 * D), (G * C) ** -0.5),
        'rwkv_mu': jax.random.uniform(ks[16], (NC, 6, D), f32),
        'rwkv_w_rkv': nrm(17, (NC, 3, D, D), D ** -0.5),
        'rwkv_w0': jax.random.uniform(ks[18], (NC, D), f32, -6.0, 1.0),
        'rwkv_w1': nrm(19, (NC, D, RWKV_DECAY_LORA), D ** -0.5),
        'rwkv_w2': nrm(20, (NC, RWKV_DECAY_LORA, D), 0.1 * RWKV_DECAY_LORA ** -0.5),
        'rwkv_a0': nrm(21, (NC, D), 0.1),
        'rwkv_a1': nrm(22, (NC, D, RWKV_AAA_LORA), D ** -0.5),
        'rwkv_a2': nrm(23, (NC, RWKV_AAA_LORA, D), 0.1 * RWKV_AAA_LORA ** -0.5),
        'rwkv_g1': nrm(24, (NC, D, RWKV_GATE_LORA), D ** -0.5),
        'rwkv_g2': nrm(25, (NC, RWKV_GATE_LORA, D), RWKV_GATE_LORA ** -0.5),
        'rwkv_k_k': 0.85 + nrm(26, (NC, D), 0.02),
        'rwkv_k_a': 1.0 + nrm(27, (NC, D), 0.02),
        'rwkv_r_k': nrm(28, (NC, RWKV_HEADS, RWKV_HEAD_DIM), 0.1),
        'rwkv_ln_w': 1.0 + nrm(29, (NC, D), 0.02),
        'rwkv_ln_b': nrm(30, (NC, D), 0.02),
        'rwkv_w_o': nrm(31, (NC, D, D), D ** -0.5),
        'mlp_w1': nrm(32, (DEPTH, D, D_FF), D ** -0.5),
        'mlp_w2': nrm(33, (DEPTH, D_FF, D), D_FF ** -0.5),
    }


def reference(x, norm_mix, norm_mlp, norm_f, attn_w_qkv, attn_w_o, ssm_w_in, ssm_log_dt,
              ssm_a_re, ssm_a_im, ssm_b_re, ssm_b_im, ssm_c_re, ssm_c_im, ssm_d, ssm_w_out,
              rwkv_mu, rwkv_w_rkv, rwkv_w0, rwkv_w1, rwkv_w2, rwkv_a0, rwkv_a1, rwkv_a2,
              rwkv_g1, rwkv_g2, rwkv_k_k, rwkv_k_a, rwkv_r_k, rwkv_ln_w, rwkv_ln_b, rwkv_w_o,
              mlp_w1, mlp_w2):
    ia = ib = ic = 0
    for layer in range(DEPTH):
        h = rms_norm(x, norm_mix[layer])
        kind = layer % N_MIXERS
        if kind == 0:
            mix = dilated_attention(h, attn_w_qkv[ia], attn_w_o[ia])
            ia += 1
        elif kind == 1:
            mix = s5_mixer(h, ssm_w_in[ib], ssm_log_dt[ib], ssm_a_re[ib], ssm_a_im[ib],
                           ssm_b_re[ib], ssm_b_im[ib], ssm_c_re[ib], ssm_c_im[ib],
                           ssm_d[ib], ssm_w_out[ib])
            ib += 1
        else:
            mix = rwkv7_mixer(h, rwkv_mu[ic], rwkv_w_rkv[ic], rwkv_w0[ic], rwkv_w1[ic], rwkv_w2[ic],
                              rwkv_a0[ic], rwkv_a1[ic], rwkv_a2[ic], rwkv_g1[ic], rwkv_g2[ic],
                              rwkv_k_k[ic], rwkv_k_a[ic], rwkv_r_k[ic], rwkv_ln_w[ic],
                              rwkv_ln_b[ic], rwkv_w_o[ic])
            ic += 1
        x = x + mix
        x = x + squared_relu_mlp(rms_norm(x, norm_mlp[layer]), mlp_w1[layer], mlp_w2[layer])
    return rms_norm(x, norm_f)
```

```python
import numpy as np
from contextlib import ExitStack
import concourse.bass as bass
import concourse.mybir as mybir
from concourse.bass_utils import run_bass_kernel_spmd

F32 = mybir.dt.float32
BF16 = mybir.dt.bfloat16
ALU = mybir.AluOpType
AF = mybir.ActivationFunctionType

D = 2048
KC = 16
NCORES = 8
TOK = 2048
EPS = 1e-5

COMPUTE = ("pe", "dve", "act", "pool")
NDMA = 12


class Sched:
    def __init__(self, nc):
        self.nc = nc
        self.ins = {e: [] for e in ("pe", "dve", "act", "pool", "sp")}
        self.last_w = {}
        self.readers = {}

    def op(self, eng, fn, reads=(), writes=(), dma=False):
        lst = self.ins[eng]
        idx = len(lst)
        me = (eng, idx, dma)
        deps = set()
        for r in reads:
            w = self.last_w.get(r)
            if w is not None:
                deps.add(w)
        for r in writes:
            w = self.last_w.get(r)
            if w is not None:
                deps.add(w)
            for rd in self.readers.get(r, {}).values():
                for t in rd:
                    if t[0] == eng and not dma and not t[2]:
                        continue
                    deps.add(t)
        deps.discard(me)
        if eng == "pe":
            deps = {d for d in deps if d[0] != "pe"}
        lst.append(dict(fn=fn, deps=deps, dma=dma, marked=False))
        for d in deps:
            self.ins[d[0]][d[1]]["marked"] = True
        for r in writes:
            self.last_w[r] = me
            self.readers[r] = {}
        for r in reads:
            rd = self.readers.setdefault(r, {})
            if dma:
                rd.setdefault((eng, "dma"), []).append(me)
            else:
                rd[(eng, "c")] = [me]
        return me

    def emit(self):
        nc = self.nc
        ins = self.ins
        with ExitStack() as es:
            csem = {e: es.enter_context(nc.semaphore("cs_" + e)) for e in COMPUTE}
            dsem = {q: [es.enter_context(nc.semaphore("ds_%s%d" % (q, i))) for i in range(NDMA)]
                    for q in ("sp", "pool")}
            for e, lst in ins.items():
                c = 0
                nd = 0
                for it in lst:
                    if it["dma"]:
                        it["dsem"] = dsem[e][nd % NDMA]
                        it["dkey"] = (e, nd % NDMA)
                        it["dval"] = 16 * (nd // NDMA + 1)
                        it["dprev"] = 16 * (nd // NDMA)
                        nd += 1
                    else:
                        if it["marked"]:
                            c += 1
                        it["cval"] = c
            block = es.enter_context(nc.Block())

            def run(e, engobj):
                seen_c = {x: 0 for x in COMPUTE}
                seen_d = {}
                for it in ins[e]:
                    for d in sorted(it["deps"]):
                        tgt = ins[d[0]][d[1]]
                        if tgt["dma"]:
                            if seen_d.get(tgt["dkey"], 0) >= tgt["dval"]:
                                continue
                            engobj.wait_ge(tgt["dsem"], tgt["dval"])
                            seen_d[tgt["dkey"]] = tgt["dval"]
                        else:
                            v = tgt["cval"]
                            if seen_c[d[0]] >= v:
                                continue
                            engobj.wait_ge(csem[d[0]], v)
                            seen_c[d[0]] = v
                    if it["dma"]:
                        if it["dprev"] > 0 and seen_d.get(it["dkey"], 0) < it["dprev"]:
                            engobj.wait_ge(it["dsem"], it["dprev"])
                            seen_d[it["dkey"]] = it["dprev"]
                        it["fn"](engobj).then_inc(it["dsem"], 16)
                    else:
                        r = it["fn"](engobj)
                        if it["marked"]:
                            r.then_inc(csem[e], 1)
                if e in ("sp", "pool"):
                    last = {}
                    for it in ins[e]:
                        if it["dma"]:
                            last[it["dkey"]] = (it["dsem"], it["dval"])
                    for s, v in last.values():
                        engobj.wait_ge(s, v)

            @block.tensor
            def _(eng):
                run("pe", eng)

            @block.vector
            def _(eng):
                run("dve", eng)

            @block.scalar
            def _(eng):
                run("act", eng)

            @block.gpsimd
            def _(eng):
                run("pool", eng)

            @block.sync
            def _(eng):
                run("sp", eng)


class Prog:
    def __init__(self):
        self.nc = bass.Bass("TRN2", target_bir_lowering=False)
        self.S = Sched(self.nc)
        self.es = ExitStack()
        self.uid = 0

    def sb(self, name, shape, dt=F32):
        return self.es.enter_context(self.nc.sbuf_tensor("s_" + name, shape, dt))

    def ps(self, name, shape, dt=F32):
        return self.es.enter_context(self.nc.psum_tensor("p_" + name, shape, dt))

    def din(self, name, shape, dt=F32):
        return self.nc.dram_tensor(name, list(shape), dt, kind="ExternalInput").ap()

    def dout(self, name, shape, dt=F32):
        return self.nc.dram_tensor(name, list(shape), dt, kind="ExternalOutput").ap()

    def op(self, *a, **k):
        return self.S.op(*a, **k)

    def finish(self):
        self.S.emit()
        self.es.close()
        return self.nc


class Common:
    def __init__(self, P):
        self.P = P
        self.ones = P.sb("ones_f", [128, 128], F32)
        P.op("dve", lambda e: e.memset(self.ones[:], 1.0), writes=["ones"])
        self.sq = [P.sb("sq%d" % i, [128, 512], F32) for i in range(2)]
        self.std = P.sb("std", [128, 512], F32)
        self.rstd = P.sb("rstd", [128, 512], F32)
        self.stat = P.ps("stat_ps", [128, 512])
        self.nsq = 0


def rmsnorm(P, C, x_sb, g_sb, gcol0, out_sb, TT, xkey, okey, out_dt_is_f32=False):
    for h in range(TT // 512):
        sl = slice(h * 512, (h + 1) * 512)
        for kc in range(KC):
            j = C.nsq % 2
            C.nsq += 1
            P.op("act", lambda e, j=j, kc=kc, sl=sl: e.activation(C.sq[j][:], x_sb[:, kc, sl], AF.Square),
                 reads=[xkey + "_%d" % kc], writes=["sq%d" % j])
            P.op("pe", lambda e, j=j, kc=kc: e.matmul(C.stat[:], C.ones[:], C.sq[j][:], start=(kc == 0), stop=(kc == KC - 1)),
                 reads=["ones", "sq%d" % j], writes=["stat"])
        P.op("act", lambda e: e.activation(C.std[:], C.stat[:], AF.Sqrt, bias=EPS, scale=1.0 / D),
             reads=["stat"], writes=["std"])
        P.op("dve", lambda e: e.reciprocal(C.rstd[:], C.std[:]), reads=["std"], writes=["rstd"])
        for kc in range(KC):
            P.op("dve", lambda e, kc=kc, sl=sl: e.scalar_tensor_tensor(
                out_sb[:, kc, sl], x_sb[:, kc, sl], g_sb[:, gcol0 + kc:gcol0 + kc + 1], C.rstd[:], ALU.mult, ALU.mult),
                reads=[xkey + "_%d" % kc, "rstd", "gvec"], writes=[okey + "_%d" % kc])


class MlpBufs:
    FB = 512

    def __init__(self, P):
        FB = self.FB
        self.w1 = [P.sb("w1b%d" % i, [128, KC, FB], BF16) for i in range(2)]
        self.w2 = [P.sb("w2b%d" % i, [128, FB // 128, D], BF16) for i in range(2)]
        self.g1 = [P.ps("g1_%d" % i, [128, 512]) for i in range(2)]
        self.acc = [P.ps("acc_%d" % i, [128, 512]) for i in range(4)]
        self.relu = [P.sb("relu%d" % i, [128, 512], F32) for i in range(2)]
        self.cnt_w = 0
        self.cnt_g1 = 0
        self.cnt_acc = 0
        self.cnt_relu = 0


def mlp_block(P, C, M, x_sb, xn_sb, hT, TT, w1_d, w2_d, xkey, xnkey):
    FB = M.FB
    FBC = FB // 128
    NB = w1_d.shape[1] // FB
    NH = TT // 512
    w1v = w1_d.rearrange("(kc p) f -> p kc f", p=128)
    w2v = w2_d.rearrange("(fc p) n -> p fc n", p=128)

    def gemm2(b):
        par = b % 2
        for h in range(NH):
            sl = slice(h * 512, (h + 1) * 512)
            for n in range(KC):
                a = M.cnt_acc % 4
                M.cnt_acc += 1
                for fc in range(FBC):
                    P.op("pe", lambda e, a=a, fc=fc, n=n, sl=sl, par=par, wp=M.wpar[b]: e.matmul(
                        M.acc[a][:], M.w2[wp][:, fc, n * 128:(n + 1) * 128], hT[par][:, fc, sl],
                        start=(fc == 0), stop=(fc == FBC - 1)),
                        reads=["w2b%d" % M.wpar[b], "hT%d_%d_%d" % (par, fc, h)], writes=["acc%d" % a])
                P.op("dve", lambda e, a=a, n=n, sl=sl: e.tensor_tensor(x_sb[:, n, sl], x_sb[:, n, sl], M.acc[a][:], ALU.add),
                     reads=["acc%d" % a, xkey + "_%d" % n], writes=[xkey + "_%d" % n])

    M.wpar = {}
    pend = []
    for b in range(NB):
        wp = M.cnt_w % 2
        M.cnt_w += 1
        M.wpar[b] = wp
        P.op("pool", lambda e, b=b, wp=wp: e.dma_start(out=M.w1[wp][:], in_=w1v[:, :, b * FB:(b + 1) * FB]),
             writes=["w1b%d" % wp], dma=True)
        P.op("pool", lambda e, b=b, wp=wp: e.dma_start(out=M.w2[wp][:], in_=w2v[:, b * FBC:(b + 1) * FBC, :]),
             writes=["w2b%d" % wp], dma=True)
        par = b % 2
        for h in range(NH):
            sl = slice(h * 512, (h + 1) * 512)
            for fc in range(FBC):
                g = M.cnt_g1 % 2
                M.cnt_g1 += 1
                for kc in range(KC):
                    P.op("pe", lambda e, g=g, kc=kc, fc=fc, sl=sl, wp=wp: e.matmul(
                        M.g1[g][:], M.w1[wp][:, kc, fc * 128:(fc + 1) * 128], xn_sb[:, kc, sl],
                        start=(kc == 0), stop=(kc == KC - 1)),
                        reads=["w1b%d" % wp, xnkey + "_%d" % kc], writes=["g1_%d" % g])
                r = M.cnt_relu % 2
                M.cnt_relu += 1
                P.op("act", lambda e, g=g, r=r: e.activation(M.relu[r][:], M.g1[g][:], AF.Relu),
                     reads=["g1_%d" % g], writes=["relu%d" % r])
                if pend:
                    pend.pop()()
                pend.append(lambda r=r, fc=fc, sl=sl, par=par, h=h: P.op(
                    "act", lambda e: e.activation(hT[par][:, fc, sl], M.relu[r][:], AF.Square),
                    reads=["relu%d" % r], writes=["hT%d_%d_%d" % (par, fc, h)]))
        if b >= 1:
            gemm2(b - 1)
    pend.pop()()
    gemm2(NB - 1)


def proj_block(P, C, M, x_sb, a_sb, TT, wo_d, glu, xkey, akey):
    NH = TT // 512
    wv = wo_d.rearrange("(kc p) f -> p kc f", p=128)
    for nt in range(4):
        wps = []
        for part in range(2 if glu else 1):
            wp = M.cnt_w % 2
            M.cnt_w += 1
            wps.append(wp)
            c0 = part * D + nt * 512
            P.op("pool", lambda e, wp=wp, c0=c0: e.dma_start(out=M.w1[wp][:], in_=wv[:, :, c0:c0 + 512]),
                 writes=["w1b%d" % wp], dma=True)
        for nn in range(4):
            n = nt * 4 + nn
            for h in range(NH):
                sl = slice(h * 512, (h + 1) * 512)
                accs = []
                for part in range(len(wps)):
                    a = M.cnt_acc % 4
                    M.cnt_acc += 1
                    accs.append(a)
                    wp = wps[part]
                    for kc in range(KC):
                        P.op("pe", lambda e, a=a, wp=wp, kc=kc, nn=nn, sl=sl: e.matmul(
                            M.acc[a][:], M.w1[wp][:, kc, nn * 128:(nn + 1) * 128], a_sb[:, kc, sl],
                            start=(kc == 0), stop=(kc == KC - 1)),
                            reads=["w1b%d" % wp, akey + "_%d" % kc], writes=["acc%d" % a])
                if not glu:
                    a = accs[0]
                    P.op("dve", lambda e, a=a, n=n, sl=sl: e.tensor_tensor(x_sb[:, n, sl], x_sb[:, n, sl], M.acc[a][:], ALU.add),
                         reads=["acc%d" % a, xkey + "_%d" % n], writes=[xkey + "_%d" % n])
                else:
                    a1, a2 = accs
                    P.op("act", lambda e, a2=a2: e.activation(M.relu[0][:], M.acc[a2][:], AF.Sigmoid),
                         reads=["acc%d" % a2], writes=["relu0"])
                    P.op("dve", lambda e, a1=a1: e.tensor_tensor(M.relu[1][:], M.acc[a1][:], M.relu[0][:], ALU.mult),
                         reads=["acc%d" % a1, "relu0"], writes=["relu1"])
                    P.op("pool", lambda e, n=n, sl=sl: e.tensor_tensor(x_sb[:, n, sl], x_sb[:, n, sl], M.relu[1][:], ALU.add),
                         reads=["relu1", xkey + "_%d" % n], writes=[xkey + "_%d" % n])


def build_mlp_stage(final_norm, proj=None):
    P = Prog()
    TT = 1024
    xT_d = P.din("xT", [D, TOK])
    w1_d = P.din("w1", [D, 4 * D])
    w2_d = P.din("w2", [4 * D, D])
    g_d = P.din("gv", [128, 2 * KC])
    if proj:
        a_d = P.din("aT", [D, TOK], BF16)
        wo_d = P.din("wo", [D, 2 * D if proj == "glu" else D])
    out_d = P.dout("yT", [D, TOK])
    C = Common(P)
    M = MlpBufs(P)
    x_sb = P.sb("x_sb", [128, KC, TT], F32)
    xn_sb = P.sb("xn_sb", [128, KC, TT], BF16)
    hT = [P.sb("hT%d" % i, [128, M.FB // 128, TT], BF16) for i in range(2)]
    g_sb = P.sb("g_sb", [128, 2 * KC], F32)
    P.op("sp", lambda e: e.dma_start(out=g_sb[:], in_=g_d), writes=["gvec"], dma=True)
    xv = xT_d.rearrange("(kc p) t -> p kc t", p=128)
    ov = out_d.rearrange("(kc p) t -> p kc t", p=128)
    for ps_ in range(TOK // TT):
        tsl = slice(ps_ * TT, (ps_ + 1) * TT)
        for q in range(4):
            P.op("sp", lambda e, q=q, tsl=tsl: e.dma_start(out=x_sb[:, 4 * q:4 * q + 4, :], in_=xv[:, 4 * q:4 * q + 4, tsl]),
                 writes=["x_%d" % k for k in range(4 * q, 4 * q + 4)], dma=True)
        if proj:
            av = a_d.rearrange("(kc p) t -> p kc t", p=128)
            for q in range(2):
                P.op("sp", lambda e, q=q, tsl=tsl: e.dma_start(out=xn_sb[:, 8 * q:8 * q + 8, :], in_=av[:, 8 * q:8 * q + 8, tsl]),
                     writes=["xn_%d" % k for k in range(8 * q, 8 * q + 8)], dma=True)
            proj_block(P, C, M, x_sb, xn_sb, TT, wo_d, proj == "glu", "x", "xn")
        rmsnorm(P, C, x_sb, g_sb, 0, xn_sb, TT, "x", "xn")
        mlp_block(P, C, M, x_sb, xn_sb, hT, TT, w1_d, w2_d, "x", "xn")
        if final_norm:
            rmsnorm(P, C, x_sb, g_sb, KC, x_sb, TT, "x", "x")
        for q in range(4):
            P.op("sp", lambda e, q=q, tsl=tsl: e.dma_start(out=ov[:, 4 * q:4 * q + 4, tsl], in_=x_sb[:, 4 * q:4 * q + 4, :]),
                 reads=["x_%d" % k for k in range(4 * q, 4 * q + 4)], dma=True)
    return P.finish()


def gvec_layout(*vecs):
    return np.ascontiguousarray(np.concatenate([v.reshape(KC, 128).T for v in vecs], axis=1)).astype(np.float32)


_cache = {}


def run_mlp_stage(xT_cores, w1, w2, g_mlp, g_final=None, aT_cores=None, wo=None):
    proj = None if wo is None else ("glu" if wo.shape[1] == 2 * D else "lin")
    key = ("mlp", g_final is not None, proj)
    if key not in _cache:
        _cache[key] = build_mlp_stage(g_final is not None, proj)
    nc = _cache[key]
    gv = gvec_layout(g_mlp, g_final if g_final is not None else g_mlp)
    in_maps = [{"xT": xT_cores[c], "w1": w1, "w2": w2, "gv": gv} for c in range(NCORES)]
    if proj:
        for c in range(NCORES):
            in_maps[c]["aT"] = aT_cores[c]
            in_maps[c]["wo"] = wo
    res = run_bass_kernel_spmd(nc, in_maps, core_ids=list(range(NCORES)))
    return [r["yT"] for r in res.results]


def load_norm_full(P, C, xT_d, g_sb, gcol0, xn_sb, xq, QW=256):
    xv = xT_d.rearrange("(kc p) t -> p kc t", p=128)
    for q in range(TOK // QW):
        b = q % 2
        tsl = slice(q * QW, (q + 1) * QW)
        keys = ["xq%d_%d" % (b, k) for k in range(KC)]
        for s in range(2):
            P.op("sp", lambda e, b=b, s=s, tsl=tsl: e.dma_start(out=xq[b][:, 8 * s:8 * s + 8, :], in_=xv[:, 8 * s:8 * s + 8, tsl]),
                 writes=keys[8 * s:8 * s + 8], dma=True)
        for kc in range(KC):
            j = C.nsq % 2
            C.nsq += 1
            P.op("act", lambda e, j=j, kc=kc, b=b: e.activation(C.sq[j][:, :QW], xq[b][:, kc, :], AF.Square),
                 reads=[keys[kc]], writes=["sq%d" % j])
            P.op("pe", lambda e, j=j, kc=kc: e.matmul(C.stat[:, :QW], C.ones[:], C.sq[j][:, :QW], start=(kc == 0), stop=(kc == KC - 1)),
                 reads=["ones", "sq%d" % j], writes=["stat"])
        P.op("act", lambda e: e.activation(C.std[:, :QW], C.stat[:, :QW], AF.Sqrt, bias=EPS, scale=1.0 / D),
             reads=["stat"], writes=["std"])
        P.op("dve", lambda e: e.reciprocal(C.rstd[:, :QW], C.std[:, :QW]), reads=["std"], writes=["rstd"])
        for kc in range(KC):
            P.op("dve", lambda e, kc=kc, b=b, tsl=tsl: e.scalar_tensor_tensor(
                xn_sb[:, kc, tsl], xq[b][:, kc, :], g_sb[:, gcol0 + kc:gcol0 + kc + 1], C.rstd[:, :QW], ALU.mult, ALU.mult),
                reads=[keys[kc], "rstd", "gvec"], writes=["xn_%d" % kc])


DIL = (1, 4, 16)


def blk_tok_slice(d, blk):
    J = TOK // (128 * d)
    r, j = divmod(blk, J)
    start = j * 128 * d + r
    return slice(start, start + 127 * d + 1, d) if d > 1 else slice(start, start + 128)


def build_qkv_stage():
    P = Prog()
    xT_d = P.din("xT", [D, TOK])
    w_d = P.din("wqkv", [D, 9 * D])
    g_d = P.din("gv", [128, KC])
    q_d = P.dout("qT", [3, 16, 128, TOK], BF16)
    k_d = P.dout("kT", [3, 16, 128, TOK], BF16)
    v_d = P.dout("v", [3, 16, 128, 16, 128], BF16)
    C = Common(P)
    g_sb = P.sb("g_sb", [128, KC], F32)
    P.op("sp", lambda e: e.dma_start(out=g_sb[:], in_=g_d), writes=["gvec"], dma=True)
    xq = [P.sb("xq%d" % i, [128, KC, 256], F32) for i in range(2)]
    xn = P.sb("xn", [128, KC, TOK], BF16)
    load_norm_full(P, C, xT_d, g_sb, 0, xn, xq, 256)
    xnkeys = ["xn_%d" % k for k in range(KC)]
    NW = 3
    wb = [P.sb("wb%d" % i, [128, KC, 512], BF16) for i in range(NW)]
    gp = [P.ps("gp%d" % i, [128, 512]) for i in range(4)]
    qst = [P.sb("qst%d" % i, [128, TOK], BF16) for i in range(3)]
    vst = [P.sb("vst%d" % i, [128, 16, 512], BF16) for i in range(2)]
    wv = w_d.rearrange("(kc p) f -> p kc f", p=128)
    cnt = dict(w=0, gp=0, q=0, v=0, ev=0)
    scale = 128.0 ** -0.5
    for g in range(3):
        d = DIL[g]
        for t in range(3):
            for nt in range(4):
                col0 = g * 3 * D + t * D + nt * 512
                wi = cnt["w"] % NW
                cnt["w"] += 1
                P.op("pool", lambda e, wi=wi, col0=col0: e.dma_start(out=wb[wi][:], in_=wv[:, :, col0:col0 + 512]),
                     writes=["wb%d" % wi], dma=True)
                if t < 2:
                    for hh in range(4):
                        h = nt * 4 + hh
                        qi = cnt["q"] % 3
                        cnt["q"] += 1
                        for tq in range(4):
                            tsl = slice(tq * 512, (tq + 1) * 512)
                            pi = cnt["gp"] % 4
                            cnt["gp"] += 1
                            for kc in range(KC):
                                P.op("pe", lambda e, pi=pi, wi=wi, kc=kc, hh=hh, tsl=tsl: e.matmul(
                                    gp[pi][:], wb[wi][:, kc, hh * 128:(hh + 1) * 128], xn[:, kc, tsl],
                                    start=(kc == 0), stop=(kc == KC - 1)),
                                    reads=["wb%d" % wi, xnkeys[kc]], writes=["gp%d" % pi])
                            sc = scale if t == 0 else 1.0
                            if cnt["ev"] % 2 == 0:
                                P.op("act", lambda e, pi=pi, qi=qi, tsl=tsl, sc=sc: e.activation(qst[qi][:, tsl], gp[pi][:], AF.Copy, scale=sc),
                                     reads=["gp%d" % pi], writes=["qst%d_%d" % (qi, tq)])
                            else:
                                P.op("dve", lambda e, pi=pi, qi=qi, tsl=tsl, sc=sc: e.tensor_scalar(qst[qi][:, tsl], gp[pi][:], sc, None, ALU.mult),
                                     reads=["gp%d" % pi], writes=["qst%d_%d" % (qi, tq)])
                            cnt["ev"] += 1
                        dst = q_d if t == 0 else k_d
                        P.op("sp", lambda e, qi=qi, dst=dst, g=g, h=h: e.dma_start(out=dst[g, h], in_=qst[qi][:]),
                             reads=["qst%d_%d" % (qi, x) for x in range(4)], dma=True)
                else:
                    vi = cnt["v"] % 2
                    cnt["v"] += 1
                    for blk in range(16):
                        bsl = blk_tok_slice(d, blk)
                        pi = cnt["gp"] % 4
                        cnt["gp"] += 1
                        for kc in range(KC):
                            P.op("pe", lambda e, pi=pi, wi=wi, kc=kc, bsl=bsl: e.matmul(
                                gp[pi][:], xn[:, kc, bsl], wb[wi][:, kc, :],
                                start=(kc == 0), stop=(kc == KC - 1)),
                                reads=["wb%d" % wi, xnkeys[kc]], writes=["gp%d" % pi])
                        if cnt["ev"] % 2 == 0:
                            P.op("act", lambda e, pi=pi, vi=vi, blk=blk: e.copy(vst[vi][:, blk, :], gp[pi][:]),
                                 reads=["gp%d" % pi], writes=["vst%d_%d" % (vi, blk)])
                        else:
                            P.op("dve", lambda e, pi=pi, vi=vi, blk=blk: e.tensor_copy(vst[vi][:, blk, :], gp[pi][:]),
                                 reads=["gp%d" % pi], writes=["vst%d_%d" % (vi, blk)])
                        cnt["ev"] += 1
                    for hh in range(4):
                        h = nt * 4 + hh
                        P.op("sp", lambda e, vi=vi, g=g, h=h, hh=hh: e.dma_start(out=v_d[g, h], in_=vst[vi][:, :, hh * 128:(hh + 1) * 128]),
                             reads=["vst%d_%d" % (vi, x) for x in range(16)], dma=True)
    return P.finish()


def run_qkv_stage(xT_cores, wqkv, g_mix):
    if "qkv" not in _cache:
        _cache["qkv"] = build_qkv_stage()
    nc = _cache["qkv"]
    gv = gvec_layout(g_mix)
    in_maps = [{"xT": xT_cores[c], "wqkv": wqkv, "gv": gv} for c in range(NCORES)]
    res = run_bass_kernel_spmd(nc, in_maps, core_ids=list(range(NCORES)))
    return [(r["qT"], r["kT"], r["v"]) for r in res.results]


NEG = -30000.0


def attn_bias_tables(first):
    out = np.zeros((128, 48, 3, 128), np.float32)
    k = np.arange(128)[:, None].astype(np.float64)
    q = np.arange(128)[None, :].astype(np.float64)
    for g in range(3):
        for h in range(16):
            slope = 2.0 ** (-8.0 * (g * 16 + h + 1) / 48.0)
            c = slope * DIL[g]
            cur = np.where(k <= q, -c * (q - k), NEG)
            prev = np.where(k >= q, -c * (128 + q - k), NEG)
            out[:, g * 16 + h, 0, :] = prev
            out[:, g * 16 + h, 1, :] = cur
            out[:, g * 16 + h, 2, :] = NEG if first else prev
    return out


def build_attn_stage():
    P = Prog()
    q_d = P.din("qT", [3, 16, 128, TOK], BF16)
    k_d = P.din("kT", [3, 16, 128, 2 * TOK], BF16)
    v_d = P.din("v", [3, 16, 128, 32, 128], BF16)
    b_d = P.din("bias", [128, 48, 3, 128])
    o_d = P.dout("oT", [D, TOK], BF16)
    bias = P.sb("bias_sb", [128, 48, 3, 128], F32)
    for i in range(4):
        P.op("sp", lambda e, i=i: e.dma_start(out=bias[:, 12 * i:12 * i + 12], in_=b_d[:, 12 * i:12 * i + 12]),
             writes=["bias%d" % i], dma=True)
    ones = P.sb("ones_b", [128, 128], BF16)
    P.op("dve", lambda e: e.memset(ones[:], 1.0), writes=["ones"])
    qh = [P.sb("qh%d" % i, [128, TOK], BF16) for i in range(2)]
    kh = [P.sb("kh%d" % i, [128, 2 * TOK], BF16) for i in range(2)]
    vh = [P.sb("vh%d" % i, [128, 32, 128], BF16) for i in range(2)]
    sps = [P.ps("sps%d" % i, [128, 4, 2, 128]) for i in range(2)]
    ups = [P.ps("ups%d" % i, [128, 512]) for i in range(2)]
    dps = [P.ps("dps%d" % i, [128, 512]) for i in range(2)]
    sbb = [P.sb("sbb%d" % i, [128, 4, 2, 128], F32) for i in range(2)]
    pT = [P.sb("pT%d" % i, [128, 4, 2, 128], BF16) for i in range(2)]
    accU = [P.sb("accU%d" % i, [128, TOK], F32) for i in range(2)]
    accD = [P.sb("accD%d" % i, [128, TOK], F32) for i in range(2)]
    rec = P.sb("rec", [128, TOK], F32)
    osb = [P.sb("osb%d" % i, [128, TOK], BF16) for i in range(2)]
    cnt = dict(ld=0, b=0)
    for h in range(16):
        ai = h % 2
        for g in range(3):
            d = DIL[g]
            J = TOK // (128 * d)
            halo = 128 * d
            li = cnt["ld"] % 2
            cnt["ld"] += 1
            P.op("sp", lambda e, li=li, g=g, h=h: e.dma_start(out=qh[li][:], in_=q_d[g, h]), writes=["qh%d" % li], dma=True)
            P.op("sp", lambda e, li=li, g=g, h=h, halo=halo: e.dma_start(out=kh[li][:, TOK - halo:], in_=k_d[g, h, :, TOK - halo:]),
                 writes=["kh%d" % li], dma=True)
            nb = 16 + d
            P.op("sp", lambda e, li=li, g=g, h=h, nb=nb: e.dma_start(out=vh[li][:, :nb, :], in_=v_d[g, h, :, :nb, :]),
                 writes=["vh%d" % li], dma=True)
            gh = g * 16 + h
            for bt in range(4):
                bi = cnt["b"] % 2
                cnt["b"] += 1
                blks = [divmod(4 * bt + i, J) for i in range(4)]
                for i, (r, j) in enumerate(blks):
                    qs = blk_tok_slice(d, 4 * bt + i)
                    for kb in range(2):
                        st = TOK + (j - 1 + kb) * 128 * d + r
                        ks = slice(st, st + 127 * d + 1, d) if d > 1 else slice(st, st + 128)
                        P.op("pe", lambda e, bi=bi, i=i, kb=kb, li=li, ks=ks, qs=qs: e.matmul(
                            sps[bi][:, i, kb, :], kh[li][:, ks], qh[li][:, qs], start=True, stop=True),
                            reads=["kh%d" % li, "qh%d" % li], writes=["sps%d" % bi])
                for i, (r, j) in enumerate(blks):
                    if j > 0:
                        P.op("dve", lambda e, bi=bi, i=i, gh=gh: e.tensor_tensor(
                            sbb[bi][:, i], sps[bi][:, i], bias[:, gh, 0:2], ALU.add),
                            reads=["sps%d" % bi, "bias%d" % (gh // 12)], writes=["sbb%d_%d" % (bi, i)])
                    else:
                        for kb in range(2):
                            P.op("dve", lambda e, bi=bi, i=i, gh=gh, kb=kb: e.tensor_tensor(
                                sbb[bi][:, i, kb], sps[bi][:, i, kb], bias[:, gh, 2 - kb], ALU.add),
                                reads=["sps%d" % bi, "bias%d" % (gh // 12)], writes=["sbb%d_%d" % (bi, i)])
                P.op("act", lambda e, bi=bi: e.activation(pT[bi][:], sbb[bi][:], AF.Exp),
                     reads=["sbb%d_%d" % (bi, i) for i in range(4)], writes=["pT%d" % bi])
                for i, (r, j) in enumerate(blks):
                    for kb in range(2):
                        vb = r * (J + 1) + (j + kb)
                        P.op("pe", lambda e, bi=bi, i=i, kb=kb, li=li, vb=vb: e.matmul(
                            ups[bi][:, i * 128:(i + 1) * 128], vh[li][:, vb, :], pT[bi][:, i, kb, :],
                            start=(kb == 0), stop=(kb == 1)),
                            reads=["vh%d" % li, "pT%d" % bi], writes=["ups%d" % bi])
                for i in range(4):
                    for kb in range(2):
                        P.op("pe", lambda e, bi=bi, i=i, kb=kb: e.matmul(
                            dps[bi][:, i * 128:(i + 1) * 128], ones[:], pT[bi][:, i, kb, :],
                            start=(kb == 0), stop=(kb == 1)),
                            reads=["ones", "pT%d" % bi], writes=["dps%d" % bi])
                def views(acc, psum):
                    if d == 1:
                        return acc[:, 512 * bt:512 * bt + 512], psum[:]
                    av = acc[:].rearrange("p (l r) -> p r l", r=d)
                    if d == 4:
                        return av[:, bt, :], psum[:]
                    return av[:, 4 * bt:4 * bt + 4, :], psum[:].rearrange("p (i l) -> p i l", i=4)
                for acc, psum, nm in ((accU[ai], ups[bi], "U"), (accD[ai], dps[bi], "D")):
                    av, pv = views(acc, psum)
                    pkey = ("ups%d" if nm == "U" else "dps%d") % bi
                    akey = "acc%s%d" % (nm, ai)
                    if g == 0:
                        P.op("dve", lambda e, av=av, pv=pv: e.tensor_copy(av, pv), reads=[pkey], writes=[akey])
                    else:
                        P.op("dve", lambda e, av=av, pv=pv: e.tensor_tensor(av, av, pv, ALU.add), reads=[pkey, akey], writes=[akey])
        P.op("dve", lambda e, ai=ai: e.reciprocal(rec[:], accD[ai][:]), reads=["accD%d" % ai], writes=["rec"])
        P.op("pool", lambda e, ai=ai: e.tensor_tensor(osb[ai][:], accU[ai][:], rec[:], ALU.mult),
             reads=["accU%d" % ai, "rec"], writes=["osb%d" % ai])
        P.op("sp", lambda e, ai=ai, h=h: e.dma_start(out=o_d[h * 128:(h + 1) * 128, :], in_=osb[ai][:]),
             reads=["osb%d" % ai], dma=True)
    return P.finish()


def attn_host_layout(qkv):
    maps = []
    for c in range(NCORES):
        qT, kT, v = qkv[c]
        first = (c % 4 == 0)
        kext = np.zeros((3, 16, 128, 2 * TOK), dtype=kT.dtype)
        kext[..., TOK:] = kT
        vext = np.zeros((3, 16, 128, 32, 128), dtype=v.dtype)
        for g in range(3):
            d = DIL[g]
            J = TOK // (128 * d)
            vg = v[g].reshape(16, 128, d, J, 128)
            ve = vext[g, :, :, :d * (J + 1), :].reshape(16, 128, d, J + 1, 128)
            ve[:, :, :, 1:, :] = vg
            if not first:
                pk, pv = qkv[c - 1][1], qkv[c - 1][2]
                kext[g, :, :, TOK - 128 * d:TOK] = pk[g, :, :, TOK - 128 * d:]
                ve[:, :, :, 0, :] = pv[g].reshape(16, 128, d, J, 128)[:, :, :, J - 1, :]
            vext[g, :, :, :d * (J + 1), :] = ve.reshape(16, 128, d * (J + 1), 128)
        maps.append({"qT": qT, "kT": kext, "v": vext, "bias": attn_bias_tables(first)})
    return maps


def run_attn_stage(qkv):
    import time
    t0 = time.time()
    if "attn" not in _cache:
        _cache["attn"] = build_attn_stage()
    t1 = time.time()
    maps = attn_host_layout(qkv)
    t2 = time.time()
    res = run_bass_kernel_spmd(_cache["attn"], maps, core_ids=list(range(NCORES)))
    print("attn stage: build %.1f layout %.1f run %.1f" % (t1 - t0, t2 - t1, time.time() - t2), flush=True)
    return [r["oT"] for r in res.results]


def build_normproj_stage(n_out):
    P = Prog()
    xT_d = P.din("xT", [D, TOK])
    w_d = P.din("w", [D, n_out])
    g_d = P.din("gv", [128, KC])
    o_d = P.dout("uT", [n_out, TOK])
    C = Common(P)
    g_sb = P.sb("g_sb", [128, KC], F32)
    P.op("sp", lambda e: e.dma_start(out=g_sb[:], in_=g_d), writes=["gvec"], dma=True)
    xq = [P.sb("xq%d" % i, [128, KC, 256], F32) for i in range(2)]
    xn = P.sb("xn", [128, KC, TOK], BF16)
    load_norm_full(P, C, xT_d, g_sb, 0, xn, xq, 256)
    wb = [P.sb("wb%d" % i, [128, KC, 512], BF16) for i in range(2)]
    gp = [P.ps("gp%d" % i, [128, 512]) for i in range(4)]
    ost = [P.sb("ost%d" % i, [128, TOK], F32) for i in range(2)]
    wv = w_d.rearrange("(kc p) f -> p kc f", p=128)
    cnt = dict(gp=0, ev=0, o=0)
    for nt in range(n_out // 512):
        wi = nt % 2
        P.op("pool", lambda e, wi=wi, nt=nt: e.dma_start(out=wb[wi][:], in_=wv[:, :, nt * 512:(nt + 1) * 512]),
             writes=["wb%d" % wi], dma=True)
        for hh in range(4):
            n = nt * 4 + hh
            oi = cnt["o"] % 2
            cnt["o"] += 1
            for tq in range(4):
                tsl = slice(tq * 512, (tq + 1) * 512)
                pi = cnt["gp"] % 4
                cnt["gp"] += 1
                for kc in range(KC):
                    P.op("pe", lambda e, pi=pi, wi=wi, kc=kc, hh=hh, tsl=tsl: e.matmul(
                        gp[pi][:], wb[wi][:, kc, hh * 128:(hh + 1) * 128], xn[:, kc, tsl],
                        start=(kc == 0), stop=(kc == KC - 1)),
                        reads=["wb%d" % wi, "xn_%d" % kc], writes=["gp%d" % pi])
                if cnt["ev"] % 2 == 0:
                    P.op("act", lambda e, pi=pi, oi=oi, tsl=tsl: e.copy(ost[oi][:, tsl], gp[pi][:]),
                         reads=["gp%d" % pi], writes=["ost%d_%d" % (oi, tq)])
                else:
                    P.op("dve", lambda e, pi=pi, oi=oi, tsl=tsl: e.tensor_copy(ost[oi][:, tsl], gp[pi][:]),
                         reads=["gp%d" % pi], writes=["ost%d_%d" % (oi, tq)])
                cnt["ev"] += 1
            P.op("sp", lambda e, oi=oi, n=n: e.dma_start(out=o_d[n * 128:(n + 1) * 128, :], in_=ost[oi][:]),
                 reads=["ost%d_%d" % (oi, x) for x in range(4)], dma=True)
    return P.finish()


def run_normproj_stage(xT_cores, w, g_mix):
    key = ("normproj", w.shape[1])
    if key not in _cache:
        _cache[key] = build_normproj_stage(w.shape[1])
    gv = gvec_layout(g_mix)
    in_maps = [{"xT": xT_cores[c], "w": w, "gv": gv} for c in range(NCORES)]
    res = run_bass_kernel_spmd(_cache[key], in_maps, core_ids=list(range(NCORES)))
    return [r["uT"] for r in res.results]


SEQ = 8192
S5_NT = 1024
MAGIC = 12582912.0
TWO_PI = 6.283185307179586
C1 = 6.28125
C2 = TWO_PI - C1


def build_s5_stage():
    P = Prog()
    NT = S5_NT
    NTILE = SEQ // NT
    u_d = P.din("u", [512, SEQ])
    are_d = P.din("are", [128, 16])
    aim_d = P.din("aim", [128, 16])
    ldt_d = P.din("ldt", [128, 16])
    bre_d = P.din("bre", [128, 16, 128])
    bim_d = P.din("bim", [128, 16, 128])
    cre_d = P.din("cre", [128, 16, 128])
    cim_d = P.din("cim", [128, 16, 128])
    dv_d = P.din("dv", [128, 4])
    al_d = P.din("aloc", [128, NT])
    bl_d = P.din("bloc", [128, NT])
    y_d = P.dout("yT", [512, SEQ], BF16)

    def small(name, shape=(128, 16)):
        return P.sb(name, list(shape), F32)

    are, aim, ldt = small("are"), small("aim"), small("ldt")
    bre = P.sb("bre", [128, 16, 128], F32)
    bim = P.sb("bim", [128, 16, 128], F32)
    cre = P.sb("cre", [128, 16, 128], F32)
    cim = P.sb("cim", [128, 16, 128], F32)
    dv = small("dv", (128, 4))
    aloc = P.sb("aloc", [128, NT], F32)
    bloc = P.sb("bloc", [128, NT], F32)
    for t, dd, k in ((are, are_d, "are"), (aim, aim_d, "aim"), (ldt, ldt_d, "ldt"), (bre, bre_d, "bre"), (bim, bim_d, "bim"),
                     (cre, cre_d, "cre"), (cim, cim_d, "cim"), (dv, dv_d, "dv"), (aloc, al_d, "aloc"), (bloc, bl_d, "bloc")):
        P.op("sp", lambda e, t=t, dd=dd: e.dma_start(out=t[:], in_=dd), writes=[k], dma=True)

    names = ["dt", "th", "lr", "m", "tmp", "n", "r1", "thr", "s", "sh", "sq", "cs", "abr", "abi", "den", "rden", "zr",
             "t1", "t2", "cr", "ci", "ncr", "phi", "phr", "nci"]
    T = {n: small("p_" + n) for n in names}

    def dve(fn, reads, writes):
        P.op("dve", fn, reads=reads, writes=writes)

    def act(fn, reads, writes):
        P.op("act", fn, reads=reads, writes=writes)

    act(lambda e: e.activation(T["dt"][:], ldt[:], AF.Exp), ["ldt"], ["dt"])
    dve(lambda e: e.tensor_tensor(T["th"][:], T["dt"][:], aim[:], ALU.mult), ["dt", "aim"], ["th"])
    dve(lambda e: e.tensor_tensor(T["lr"][:], T["dt"][:], are[:], ALU.mult), ["dt", "are"], ["lr"])
    act(lambda e: e.activation(T["m"][:], T["lr"][:], AF.Exp), ["lr"], ["m"])

    def reduce_angle(src, dst, ksrc, kdst):
        dve(lambda e: e.tensor_scalar(T["tmp"][:], src[:], 1.0 / TWO_PI, MAGIC, ALU.mult, ALU.add), [ksrc], ["tmp"])
        dve(lambda e: e.tensor_scalar(T["n"][:], T["tmp"][:], MAGIC, None, ALU.subtract), ["tmp"], ["n"])
        dve(lambda e: e.scalar_tensor_tensor(T["r1"][:], T["n"][:], -C1, src[:], ALU.mult, ALU.add), ["n", ksrc], ["r1"])
        dve(lambda e: e.scalar_tensor_tensor(dst[:], T["n"][:], -C2, T["r1"][:], ALU.mult, ALU.add), ["n", "r1"], [kdst])

    reduce_angle(T["th"], T["thr"], "th", "thr")
    act(lambda e: e.activation(T["s"][:], T["thr"][:], AF.Sin), ["thr"], ["s"])
    act(lambda e: e.activation(T["sh"][:], T["thr"][:], AF.Sin, scale=0.5), ["thr"], ["sh"])
    act(lambda e: e.activation(T["sq"][:], T["sh"][:], AF.Square), ["sh"], ["sq"])
    act(lambda e: e.activation(T["cs"][:], T["sq"][:], AF.Identity, bias=1.0, scale=-2.0), ["sq"], ["cs"])
    dve(lambda e: e.tensor_tensor(T["abr"][:], T["m"][:], T["cs"][:], ALU.mult), ["m", "cs"], ["abr"])
    dve(lambda e: e.tensor_tensor(T["abi"][:], T["m"][:], T["s"][:], ALU.mult), ["m", "s"], ["abi"])
    dve(lambda e: e.tensor_tensor(T["t1"][:], are[:], are[:], ALU.mult), ["are"], ["t1"])
    dve(lambda e: e.tensor_tensor(T["t2"][:], aim[:], aim[:], ALU.mult), ["aim"], ["t2"])
    dve(lambda e: e.tensor_tensor(T["den"][:], T["t1"][:], T["t2"][:], ALU.add), ["t1", "t2"], ["den"])
    dve(lambda e: e.reciprocal(T["rden"][:], T["den"][:]), ["den"], ["rden"])
    dve(lambda e: e.tensor_scalar(T["zr"][:], T["abr"][:], -1.0, None, ALU.add), ["abr"], ["zr"])
    dve(lambda e: e.tensor_tensor(T["t1"][:], T["zr"][:], are[:], ALU.mult), ["zr", "are"], ["t1"])
    dve(lambda e: e.tensor_tensor(T["t2"][:], T["abi"][:], aim[:], ALU.mult), ["abi", "aim"], ["t2"])
    dve(lambda e: e.tensor_tensor(T["cr"][:], T["t1"][:], T["t2"][:], ALU.add), ["t1", "t2"], ["cr0"])
    dve(lambda e: e.tensor_tensor(T["cr"][:], T["cr"][:], T["rden"][:], ALU.mult), ["cr0", "rden"], ["cr"])
    dve(lambda e: e.tensor_tensor(T["t1"][:], T["abi"][:], are[:], ALU.mult), ["abi", "are", "cr0"], ["t1"])
    dve(lambda e: e.tensor_tensor(T["t2"][:], T["zr"][:], aim[:], ALU.mult), ["zr", "aim", "cr0"], ["t2"])
    dve(lambda e: e.tensor_tensor(T["ci"][:], T["t1"][:], T["t2"][:], ALU.subtract), ["t1", "t2"], ["ci0"])
    dve(lambda e: e.tensor_tensor(T["ci"][:], T["ci"][:], T["rden"][:], ALU.mult), ["ci0", "rden"], ["ci"])
    dve(lambda e: e.tensor_scalar(T["ncr"][:], T["cr"][:], -1.0, None, ALU.mult), ["cr"], ["ncr"])
    dve(lambda e: e.tensor_scalar(T["nci"][:], T["ci"][:], -1.0, None, ALU.mult), ["ci"], ["nci"])
    dve(lambda e: e.tensor_scalar(T["phi"][:], T["thr"][:], 64.0, None, ALU.mult), ["thr"], ["phi"])
    reduce_angle(T["phi"], T["phr"], "phi", "phr")
    off = P.sb("off", [128, NTILE, 16], F32)
    for tt in range(NTILE):
        dve(lambda e, tt=tt: e.tensor_scalar(off[:, tt, :], T["phr"][:], float(tt * (NT // 64)), None, ALU.mult), ["phr"], ["off"])
    ctr = P.sb("ctr", [128, 16, 128], F32)
    cti = P.sb("cti", [128, 16, 128], F32)
    ctmp = P.sb("ctmp", [128, 128], F32)
    for pr in range(16):
        dve(lambda e, pr=pr: e.tensor_scalar(ctmp[:], cim[:, pr, :], T["nci"][:, pr:pr + 1], None, ALU.mult), ["cim", "nci"], ["ctmp"])
        dve(lambda e, pr=pr: e.scalar_tensor_tensor(ctr[:, pr, :], cre[:, pr, :], T["cr"][:, pr:pr + 1], ctmp[:], ALU.mult, ALU.add),
            ["cre", "cr", "ctmp"], ["ctr"])
        dve(lambda e, pr=pr: e.tensor_scalar(ctmp[:], cim[:, pr, :], T["ncr"][:, pr:pr + 1], None, ALU.mult), ["cim", "ncr", "ctr"], ["ctmp"])
        dve(lambda e, pr=pr: e.scalar_tensor_tensor(cti[:, pr, :], cre[:, pr, :], T["nci"][:, pr:pr + 1], ctmp[:], ALU.mult, ALU.add),
            ["cre", "nci", "ctmp"], ["cti"])

    carry = P.sb("carry", [128, 16, 2], F32)
    dve(lambda e: e.memset(carry[:], 0.0), [], ["carry%d" % i for i in range(16)])

    def big(name):
        return P.sb(name, [128, NT], F32)

    uch = P.sb("uch", [128, SEQ], F32)
    base = big("base")
    t1b = big("t1b")
    ang, tmpb, nb_, r1b, red = big("ang"), big("tmpb"), big("nb"), big("r1b"), big("red")
    sn = [big("sn%d" % i) for i in range(2)]
    cs = [big("cs%d" % i) for i in range(2)]
    shb, sqb = big("shb"), big("sqb")
    zr = [big("zr0")]
    zi = [big("zi0")]
    a1, a2, b1, b2 = big("a1"), big("a2"), big("b1"), big("b2")
    zhr, zhi = big("zhr"), big("zhi")
    wr, wi = big("wr"), big("wi")
    xr = [big("xr0")]
    xi = [big("xi0")]
    zps = [P.ps("zps%d" % i, [128, 512]) for i in range(4)]
    yps = [P.ps("yps%d" % i, [128, 512]) for i in range(4)]
    gl = {n: P.sb("gl_" + n, [128, 512], F32) for n in ("ypre", "sq", "t", "inner", "sg")}
    ysb = [P.sb("ysb%d" % i, [128, NT], BF16) for i in range(2)]
    NS = NT // 512
    cnt = dict(z=0, it=0, y=0, o=0)
    GC = 2.0 * 0.7978845608028654
    for ch in range(4):
        for q in range(4):
            P.op("sp", lambda e, ch=ch, q=q: e.dma_start(out=uch[:, q * 2048:(q + 1) * 2048],
                                                         in_=u_d[ch * 128:(ch + 1) * 128, q * 2048:(q + 1) * 2048]),
                 writes=["uch%d" % q], dma=True)
        for tt in range(NTILE):
            tsl0 = tt * NT
            yb = [(cnt["y"] + i) % 4 for i in range(NS)]
            cnt["y"] += NS
            for pq in range(4):
                pr = ch * 4 + pq
                it = cnt["it"] % 2
                cnt["it"] += 1
                P.op("pool", lambda e, pr=pr: e.tensor_scalar(t1b[:], bloc[:], T["thr"][:, pr:pr + 1], None, ALU.mult),
                     reads=["bloc", "thr"], writes=["t1b"])
                dve(lambda e, pr=pr: e.scalar_tensor_tensor(base[:], aloc[:], T["phr"][:, pr:pr + 1], t1b[:], ALU.mult, ALU.add),
                    ["aloc", "phr", "t1b"], ["base"])
                dve(lambda e, pr=pr, tt=tt: e.tensor_scalar(ang[:], base[:], off[:, tt, pr:pr + 1], None, ALU.add), ["base", "off"], ["ang"])
                dve(lambda e: e.tensor_scalar(tmpb[:], ang[:], 1.0 / TWO_PI, MAGIC, ALU.mult, ALU.add), ["ang"], ["tmpb"])
                dve(lambda e: e.tensor_scalar(nb_[:], tmpb[:], MAGIC, None, ALU.subtract), ["tmpb"], ["nb"])
                dve(lambda e: e.scalar_tensor_tensor(r1b[:], nb_[:], -C1, ang[:], ALU.mult, ALU.add), ["nb", "ang"], ["r1b"])
                dve(lambda e: e.scalar_tensor_tensor(red[:], nb_[:], -C2, r1b[:], ALU.mult, ALU.add), ["nb", "r1b"], ["red"])
                act(lambda e, it=it: e.activation(sn[it][:], red[:], AF.Sin), ["red"], ["sn%d" % it])
                act(lambda e: e.activation(shb[:], red[:], AF.Sin, scale=0.5), ["red"], ["shb"])
                act(lambda e: e.activation(sqb[:], shb[:], AF.Square), ["shb"], ["sqb"])
                act(lambda e, it=it: e.activation(cs[it][:], sqb[:], AF.Identity, bias=1.0, scale=-2.0), ["sqb"], ["cs%d" % it])
                for s_ in range(NS):
                    usl = slice(tsl0 + s_ * 512, tsl0 + (s_ + 1) * 512)
                    ssl = slice(s_ * 512, (s_ + 1) * 512)
                    for (bt, zt, nm) in ((bre, zr, "zr"), (bim, zi, "zi")):
                        zp = cnt["z"] % 4
                        cnt["z"] += 1
                        P.op("pe", lambda e, zp=zp, bt=bt, pr=pr, usl=usl: e.matmul(zps[zp][:], bt[:, pr, :], uch[:, usl], start=True, stop=True),
                             reads=["bre", "bim", "uch%d" % (usl.start // 2048)], writes=["zps%d" % zp])
                        act(lambda e, zp=zp, zt=zt, it=it, ssl=ssl: e.copy(zt[0][:, ssl], zps[zp][:]), ["zps%d" % zp], ["%s%d_%d" % (nm, 0, s_)])
                zrk = ["zr%d_%d" % (0, s_) for s_ in range(NS)]
                zik = ["zi%d_%d" % (0, s_) for s_ in range(NS)]
                P.op("pool", lambda e, it=it: e.tensor_tensor(a1[:], zr[0][:], cs[it][:], ALU.mult), reads=zrk + ["cs%d" % it], writes=["a1"])
                P.op("pool", lambda e, it=it: e.tensor_tensor(a2[:], zi[0][:], sn[it][:], ALU.mult), reads=zik + ["sn%d" % it], writes=["a2"])
                P.op("pool", lambda e: e.tensor_tensor(zhr[:], a1[:], a2[:], ALU.add), reads=["a1", "a2"], writes=["zhr"])
                dve(lambda e, it=it: e.tensor_tensor(b1[:], zi[0][:], cs[it][:], ALU.mult), zik + ["cs%d" % it], ["b1"])
                dve(lambda e, it=it: e.tensor_tensor(b2[:], zr[0][:], sn[it][:], ALU.mult), zrk + ["sn%d" % it], ["b2"])
                dve(lambda e: e.tensor_tensor(zhi[:], b1[:], b2[:], ALU.subtract), ["b1", "b2"], ["zhi"])
                mcol = T["m"][:, pr:pr + 1].to_broadcast([128, NT])
                dve(lambda e, pr=pr, mcol=mcol: e.tensor_tensor_scan(wr[:], mcol, zhr[:], carry[:, pr, 0:1], ALU.mult, ALU.add),
                    ["m", "zhr", "carry%d" % pr], ["wr"])
                dve(lambda e, pr=pr, mcol=mcol: e.tensor_tensor_scan(wi[:], mcol, zhi[:], carry[:, pr, 1:2], ALU.mult, ALU.add),
                    ["m", "zhi", "carry%d" % pr], ["wi"])
                act(lambda e, pr=pr: e.copy(carry[:, pr, 0:1], wr[:, NT - 1:NT]), ["wr"], ["carry%d" % pr])
                act(lambda e, pr=pr: e.copy(carry[:, pr, 1:2], wi[:, NT - 1:NT]), ["wi"], ["carry%d" % pr])
                P.op("pool", lambda e, it=it: e.tensor_tensor(a1[:], wr[:], cs[it][:], ALU.mult), reads=["wr", "cs%d" % it], writes=["a1"])
                P.op("pool", lambda e, it=it: e.tensor_tensor(a2[:], wi[:], sn[it][:], ALU.mult), reads=["wi", "sn%d" % it], writes=["a2"])
                P.op("pool", lambda e, it=it: e.tensor_tensor(xr[0][:], a1[:], a2[:], ALU.subtract), reads=["a1", "a2"], writes=["xr0"])
                dve(lambda e, it=it: e.tensor_tensor(b1[:], wi[:], cs[it][:], ALU.mult), ["wi", "cs%d" % it], ["b1"])
                dve(lambda e, it=it: e.tensor_tensor(b2[:], wr[:], sn[it][:], ALU.mult), ["wr", "sn%d" % it], ["b2"])
                dve(lambda e, it=it: e.tensor_tensor(xi[0][:], b1[:], b2[:], ALU.add), ["b1", "b2"], ["xi0"])
                for s_ in range(NS):
                    ssl = slice(s_ * 512, (s_ + 1) * 512)
                    P.op("pe", lambda e, s_=s_, pq=pq, pr=pr, it=it, ssl=ssl: e.matmul(
                        yps[yb[s_]][:], ctr[:, pr, :], xr[0][:, ssl], start=(pq == 0), stop=False),
                        reads=["ctr", "xr0"], writes=["yps%d" % yb[s_]])
                    P.op("pe", lambda e, s_=s_, pq=pq, pr=pr, it=it, ssl=ssl: e.matmul(
                        yps[yb[s_]][:], cti[:, pr, :], xi[0][:, ssl], start=False, stop=(pq == 3)),
                        reads=["cti", "xi0"], writes=["yps%d" % yb[s_]])
            oi = cnt["o"] % 2
            cnt["o"] += 1
            for s_ in range(NS):
                usl = slice(tsl0 + s_ * 512, tsl0 + (s_ + 1) * 512)
                ssl = slice(s_ * 512, (s_ + 1) * 512)
                dve(lambda e, s_=s_, usl=usl, ch=ch: e.scalar_tensor_tensor(gl["ypre"][:], uch[:, usl], dv[:, ch:ch + 1], yps[yb[s_]][:], ALU.mult, ALU.add),
                    ["uch%d" % (usl.start // 2048), "dv", "yps%d" % yb[s_]], ["g_ypre"])
                act(lambda e: e.activation(gl["sq"][:], gl["ypre"][:], AF.Square), ["g_ypre"], ["g_sq"])
                act(lambda e: e.activation(gl["t"][:], gl["sq"][:], AF.Identity, bias=1.0, scale=0.044715), ["g_sq"], ["g_t"])
                P.op("pool", lambda e: e.tensor_tensor(gl["inner"][:], gl["t"][:], gl["ypre"][:], ALU.mult), reads=["g_t", "g_ypre"], writes=["g_inner"])
                act(lambda e: e.activation(gl["sg"][:], gl["inner"][:], AF.Sigmoid, scale=GC), ["g_inner"], ["g_sg"])
                P.op("pool", lambda e, oi=oi, ssl=ssl: e.tensor_tensor(ysb[oi][:, ssl], gl["ypre"][:], gl["sg"][:], ALU.mult),
                     reads=["g_ypre", "g_sg"], writes=["ysb%d" % oi])
            P.op("sp", lambda e, oi=oi, ch=ch, tsl0=tsl0: e.dma_start(out=y_d[ch * 128:(ch + 1) * 128, tsl0:tsl0 + NT], in_=ysb[oi][:]),
                 reads=["ysb%d" % oi], dma=True)
    return P.finish()


def s5_host_layout(uT_cores, log_dt, a_re, a_im, b_re, b_im, c_re, c_im, dskip):
    NT = S5_NT
    loc = np.arange(NT)
    aloc = np.broadcast_to((loc // 64).astype(np.float32), (128, NT)).copy()
    bloc = np.broadcast_to((loc % 64).astype(np.float32), (128, NT)).copy()
    maps = []
    for c in range(NCORES):
        seq, gq = divmod(c, 4)
        u = np.concatenate([uT_cores[4 * seq + i][512 * gq:512 * gq + 512, :] for i in range(4)], axis=1)
        gs = np.arange(32 * gq, 32 * gq + 32).reshape(16, 2)
        are = a_re[gs].transpose(1, 2, 0).reshape(128, 16)
        aim = a_im[gs].transpose(1, 2, 0).reshape(128, 16)
        ldt = np.repeat(log_dt[gs].transpose(1, 0)[:, None, :], 64, axis=1).reshape(128, 16)
        bre = np.zeros((128, 16, 128), np.float32)
        bim = np.zeros((128, 16, 128), np.float32)
        cre = np.zeros((128, 16, 128), np.float32)
        cim = np.zeros((128, 16, 128), np.float32)
        for pr in range(16):
            for g2 in range(2):
                g = gs[pr, g2]
                r0 = (pr % 4) * 32 + g2 * 16
                bre[r0:r0 + 16, pr, g2 * 64:(g2 + 1) * 64] = b_re[g].T
                bim[r0:r0 + 16, pr, g2 * 64:(g2 + 1) * 64] = b_im[g].T
                cre[g2 * 64:(g2 + 1) * 64, pr, r0:r0 + 16] = c_re[g].T
                cim[g2 * 64:(g2 + 1) * 64, pr, r0:r0 + 16] = c_im[g].T
        dv = dskip[32 * gq:32 * gq + 32].reshape(4, 128).T
        maps.append(dict(u=np.ascontiguousarray(u), are=np.ascontiguousarray(are), aim=np.ascontiguousarray(aim),
                         ldt=np.ascontiguousarray(ldt), bre=bre, bim=bim, cre=cre, cim=cim,
                         dv=np.ascontiguousarray(dv), aloc=aloc, bloc=bloc))
    return maps


def run_s5_stage(uT_cores, log_dt, a_re, a_im, b_re, b_im, c_re, c_im, dskip):
    if "s5" not in _cache:
        _cache["s5"] = build_s5_stage()
    maps = s5_host_layout(uT_cores, log_dt, a_re, a_im, b_re, b_im, c_re, c_im, dskip)
    res = run_bass_kernel_spmd(_cache["s5"], maps, core_ids=list(range(NCORES)))
    ys = [r["yT"] for r in res.results]
    out = []
    for c in range(NCORES):
        seq, i = divmod(c, 4)
        out.append(np.ascontiguousarray(np.concatenate([ys[4 * seq + gq][:, i * TOK:(i + 1) * TOK] for gq in range(4)], axis=0)))
    return out


def build_rwkv_proj_stage():
    P = Prog()
    TT = 1024
    PW = 128
    xT_d = P.din("xT", [D, TOK])
    xp_d = P.din("xprev", [D, 1])
    g_d = P.din("gv", [128, KC])
    mu_d = P.din("mu", [128, 6 * KC])
    wrkv_d = P.din("wrkv", [3, D, D])
    w1_d = P.din("w1", [D, 96])
    w2_d = P.din("w2", [96, D])
    a1_d = P.din("a1", [D, 96])
    a2_d = P.din("a2", [96, D])
    g1_d = P.din("g1", [D, 256])
    g2_d = P.din("g2", [256, D])
    w0_d = P.din("w0", [128, KC])
    a0_d = P.din("a0", [128, KC])
    outs = {n: P.dout(n, [D, TOK]) for n in ("r", "k", "v", "ld", "a", "g")}
    C = Common(P)
    g_sb = P.sb("g_sb", [128, KC], F32)
    mu_sb = P.sb("mu_sb", [128, 6 * KC], F32)
    w0_sb = P.sb("w0_sb", [128, KC], F32)
    a0_sb = P.sb("a0_sb", [128, KC], F32)
    P.op("sp", lambda e: e.dma_start(out=g_sb[:], in_=g_d), writes=["gvec"], dma=True)
    P.op("sp", lambda e: e.dma_start(out=mu_sb[:], in_=mu_d), writes=["mu"], dma=True)
    P.op("sp", lambda e: e.dma_start(out=w0_sb[:], in_=w0_d), writes=["w0"], dma=True)
    P.op("sp", lambda e: e.dma_start(out=a0_sb[:], in_=a0_d), writes=["a0"], dma=True)
    w1_sb = P.sb("w1_sb", [128, KC, 96], BF16)
    a1_sb = P.sb("a1_sb", [128, KC, 96], BF16)
    g1_sb = P.sb("g1_sb", [128, KC, 256], BF16)
    w2_sb = P.sb("w2_sb", [96, D], BF16)
    a2_sb = P.sb("a2_sb", [96, D], BF16)
    g2_sb = P.sb("g2_sb", [128, 2, D], BF16)
    P.op("pool", lambda e: e.dma_start(out=w1_sb[:], in_=w1_d.rearrange("(kc p) f -> p kc f", p=128)), writes=["w1"], dma=True)
    P.op("pool", lambda e: e.dma_start(out=a1_sb[:], in_=a1_d.rearrange("(kc p) f -> p kc f", p=128)), writes=["a1"], dma=True)
    P.op("pool", lambda e: e.dma_start(out=g1_sb[:], in_=g1_d.rearrange("(kc p) f -> p kc f", p=128)), writes=["g1"], dma=True)
    P.op("pool", lambda e: e.dma_start(out=w2_sb[:], in_=w2_d), writes=["w2"], dma=True)
    P.op("pool", lambda e: e.dma_start(out=a2_sb[:], in_=a2_d), writes=["a2"], dma=True)
    P.op("pool", lambda e: e.dma_start(out=g2_sb[:], in_=g2_d.rearrange("(c p) f -> p c f", p=128)), writes=["g2"], dma=True)

    xq = [P.sb("xq%d" % i, [128, KC, PW + 1], F32) for i in range(2)]
    hf = P.sb("hf", [128, KC, PW + 1], F32)
    h_sb = P.sb("h_sb", [128, KC, TT], BF16)
    xx_sb = P.sb("xx_sb", [128, KC, TT], BF16)
    xj = [P.sb("xj0", [128, KC, TT], BF16)]
    wb = [P.sb("wb%d" % i, [128, KC, 512], BF16) for i in range(2)]
    gp = [P.ps("gp%d" % i, [128, 512]) for i in range(4)]
    lp = [P.ps("lp%d" % i, [128, 512]) for i in range(2)]
    ost = [P.sb("ost%d" % i, [128, TT], F32) for i in range(2)]
    lo = [P.sb("lo0", [128, 2, TT], BF16)]
    xv = xT_d.rearrange("(kc p) t -> p kc t", p=128)
    xpv = xp_d.rearrange("(kc p) t -> p kc t", p=128)
    cnt = dict(x=0, gp=0, lp=0, ev=0, o=0, w=0, j=0, lo=0)
    NPC = TT // PW
    for ps_ in range(TOK // TT):
        t0 = ps_ * TT
        for pc in range(NPC):
            tp = t0 + pc * PW
            b = cnt["x"] % 2
            cnt["x"] += 1
            keys = ["xq%d_%d" % (b, k) for k in range(KC)]
            if tp == 0:
                P.op("sp", lambda e, b=b: e.dma_start(out=xq[b][:, :, 0:1], in_=xpv, allow_slow_non_contiguous=True), writes=keys, dma=True)
                P.op("sp", lambda e, b=b: e.dma_start(out=xq[b][:, :, 1:], in_=xv[:, :, 0:PW]), writes=keys, dma=True)
            else:
                P.op("sp", lambda e, b=b, tp=tp: e.dma_start(out=xq[b][:], in_=xv[:, :, tp - 1:tp + PW]), writes=keys, dma=True)
            N1 = PW + 1
            for kc in range(KC):
                j = C.nsq % 2
                C.nsq += 1
                P.op("act", lambda e, j=j, kc=kc, b=b: e.activation(C.sq[j][:, :N1], xq[b][:, kc, :], AF.Square),
                     reads=[keys[kc]], writes=["sq%d" % j])
                P.op("pe", lambda e, j=j, kc=kc: e.matmul(C.stat[:, :N1], C.ones[:], C.sq[j][:, :N1], start=(kc == 0), stop=(kc == KC - 1)),
                     reads=["ones", "sq%d" % j], writes=["stat"])
            P.op("act", lambda e: e.activation(C.std[:, :N1], C.stat[:, :N1], AF.Sqrt, bias=EPS, scale=1.0 / D), reads=["stat"], writes=["std"])
            P.op("dve", lambda e: e.reciprocal(C.rstd[:, :N1], C.std[:, :N1]), reads=["std"], writes=["rstd"])
            psl = slice(pc * PW, (pc + 1) * PW)
            for kc in range(KC):
                P.op("dve", lambda e, kc=kc, b=b: e.scalar_tensor_tensor(
                    hf[:, kc, :], xq[b][:, kc, :], g_sb[:, kc:kc + 1], C.rstd[:, :N1], ALU.mult, ALU.mult),
                    reads=[keys[kc], "rstd", "gvec"], writes=["hf_%d" % kc])
                P.op("act", lambda e, kc=kc, psl=psl: e.copy(h_sb[:, kc, psl], hf[:, kc, 1:]), reads=["hf_%d" % kc], writes=["h_%d" % kc])
                P.op("pool", lambda e, kc=kc, psl=psl: e.tensor_tensor(xx_sb[:, kc, psl], hf[:, kc, 0:PW], hf[:, kc, 1:], ALU.subtract),
                     reads=["hf_%d" % kc], writes=["xx_%d" % kc])

        def make_xj(j):
            ji = 0
            cnt["j"] += 1
            for kc in range(KC):
                P.op("dve", lambda e, kc=kc, ji=ji, j=j: e.scalar_tensor_tensor(
                    xj[ji][:, kc, :], xx_sb[:, kc, :], mu_sb[:, j * KC + kc:j * KC + kc + 1], h_sb[:, kc, :], ALU.mult, ALU.add),
                    reads=["xx_%d" % kc, "h_%d" % kc, "mu"], writes=["xj%d_%d" % (ji, kc)])
            return ji

        def big_proj(ji, w_dram, name, post):
            wv = w_dram.rearrange("(kc p) f -> p kc f", p=128)
            for nt in range(4):
                wi = cnt["w"] % 2
                cnt["w"] += 1
                P.op("pool", lambda e, wi=wi, nt=nt: e.dma_start(out=wb[wi][:], in_=wv[:, :, nt * 512:(nt + 1) * 512]),
                     writes=["wb%d" % wi], dma=True)
                for hh in range(4):
                    n = nt * 4 + hh
                    oi = cnt["o"] % 2
                    cnt["o"] += 1
                    for th in range(TT // 512):
                        tsl = slice(th * 512, (th + 1) * 512)
                        pi = cnt["gp"] % 4
                        cnt["gp"] += 1
                        for kc in range(KC):
                            P.op("pe", lambda e, pi=pi, wi=wi, kc=kc, hh=hh, tsl=tsl, ji=ji: e.matmul(
                                gp[pi][:], wb[wi][:, kc, hh * 128:(hh + 1) * 128], xj[ji][:, kc, tsl],
                                start=(kc == 0), stop=(kc == KC - 1)),
                                reads=["wb%d" % wi, "xj%d_%d" % (ji, kc)], writes=["gp%d" % pi])
                        post(gp[pi], "gp%d" % pi, ost[oi][:, tsl], "ost%d_%d" % (oi, th), n)
                    P.op("sp", lambda e, oi=oi, n=n, name=name, t0=t0: e.dma_start(out=outs[name][n * 128:(n + 1) * 128, t0:t0 + TT], in_=ost[oi][:]),
                         reads=["ost%d_%d" % (oi, x) for x in range(TT // 512)], dma=True)

        def post_copy(ps, pkey, dst, dkey, n):
            if cnt["ev"] % 2 == 0:
                P.op("act", lambda e: e.copy(dst, ps[:]), reads=[pkey], writes=[dkey])
            else:
                P.op("dve", lambda e: e.tensor_copy(dst, ps[:]), reads=[pkey], writes=[dkey])
            cnt["ev"] += 1

        for j, name in ((0, "r"), (1, "k"), (2, "v")):
            ji = make_xj(j)
            big_proj(ji, wrkv_d[j], name, post_copy)

        def lora(ji, l1_sb, l1key, nl, func, l2, l2key, name, post):
            li = 0
            cnt["lo"] += 1
            nlc = (nl + 127) // 128
            for c in range(nlc):
                m = min(128, nl - c * 128)
                for th in range(TT // 512):
                    tsl = slice(th * 512, (th + 1) * 512)
                    pi = cnt["lp"] % 2
                    cnt["lp"] += 1
                    for kc in range(KC):
                        P.op("pe", lambda e, pi=pi, kc=kc, c=c, m=m, tsl=tsl, ji=ji: e.matmul(
                            lp[pi][:m, :], l1_sb[:, kc, c * 128:c * 128 + m], xj[ji][:, kc, tsl],
                            start=(kc == 0), stop=(kc == KC - 1)),
                            reads=[l1key, "xj%d_%d" % (ji, kc)], writes=["lp%d" % pi])
                    P.op("act", lambda e, pi=pi, li=li, c=c, m=m, tsl=tsl: e.activation(lo[li][:m, c, tsl], lp[pi][:m, :], func),
                         reads=["lp%d" % pi], writes=["lo%d" % li])
            for n in range(KC):
                oi = cnt["o"] % 2
                cnt["o"] += 1
                for th in range(TT // 512):
                    tsl = slice(th * 512, (th + 1) * 512)
                    pi = cnt["gp"] % 4
                    cnt["gp"] += 1
                    for c in range(nlc):
                        m = min(128, nl - c * 128)
                        lhs = l2[:m, c, n * 128:(n + 1) * 128] if nlc > 1 else l2[:m, n * 128:(n + 1) * 128]
                        P.op("pe", lambda e, pi=pi, lhs=lhs, li=li, c=c, m=m, tsl=tsl: e.matmul(
                            gp[pi][:], lhs, lo[li][:m, c, tsl], start=(c == 0), stop=(c == nlc - 1)),
                            reads=[l2key, "lo%d" % li], writes=["gp%d" % pi])
                    post(gp[pi], "gp%d" % pi, ost[oi][:, tsl], "ost%d_%d" % (oi, th), n)
                P.op("sp", lambda e, oi=oi, n=n, name=name, t0=t0: e.dma_start(out=outs[name][n * 128:(n + 1) * 128, t0:t0 + TT], in_=ost[oi][:]),
                     reads=["ost%d_%d" % (oi, x) for x in range(TT // 512)], dma=True)

        def post_ld(ps, pkey, dst, dkey, n):
            P.op("act", lambda e: e.activation(dst, ps[:], AF.Sigmoid, bias=w0_sb[:, n:n + 1], scale=1.0), reads=[pkey, "w0"], writes=[dkey])
            P.op("dve", lambda e: e.tensor_scalar(dst, dst, -0.6065306597126334, None, ALU.mult), reads=[dkey], writes=[dkey])

        def post_a(ps, pkey, dst, dkey, n):
            P.op("act", lambda e: e.activation(dst, ps[:], AF.Sigmoid, bias=a0_sb[:, n:n + 1], scale=1.0), reads=[pkey, "a0"], writes=[dkey])

        ji = make_xj(3)
        lora(ji, w1_sb, "w1", 96, AF.Tanh, w2_sb, "w2", "ld", post_ld)
        ji = make_xj(4)
        lora(ji, a1_sb, "a1", 96, AF.Copy, a2_sb, "a2", "a", post_a)
        ji = make_xj(5)
        lora(ji, g1_sb, "g1", 256, AF.Sigmoid, g2_sb, "g2", "g", post_copy)
    return P.finish()


def run_rwkv_proj_stage(xT_cores, inp, L, ic):
    if "rwkvp" not in _cache:
        _cache["rwkvp"] = build_rwkv_proj_stage()
    gv = gvec_layout(inp["norm_mix"][L])
    mu = gvec_layout(*[inp["rwkv_mu"][ic][j] for j in range(6)])
    maps = []
    for c in range(NCORES):
        if c % 4 == 0:
            xprev = np.zeros((D, 1), np.float32)
        else:
            xprev = np.ascontiguousarray(xT_cores[c - 1][:, TOK - 1:TOK])
        maps.append(dict(xT=xT_cores[c], xprev=xprev, gv=gv, mu=mu, wrkv=inp["rwkv_w_rkv"][ic], w1=inp["rwkv_w1"][ic],
                         w2=inp["rwkv_w2"][ic], a1=inp["rwkv_a1"][ic], a2=inp["rwkv_a2"][ic], g1=inp["rwkv_g1"][ic],
                         g2=inp["rwkv_g2"][ic], w0=gvec_layout(inp["rwkv_w0"][ic]), a0=gvec_layout(inp["rwkv_a0"][ic])))
    res = run_bass_kernel_spmd(_cache["rwkvp"], maps, core_ids=list(range(NCORES)))
    return [{n: r[n] for n in ("r", "k", "v", "ld", "a", "g")} for r in res.results]


RW_ST = 256
GN_EPS = 64 * 1e-5


def build_rwkv_core_stage():
    import os
    P = Prog()
    ST = RW_ST
    NST = SEQ // ST
    NCH = ST // 64
    ins_d = {n: P.din(n, [512, SEQ]) for n in ("r", "k", "v", "ld", "a", "g")}
    par_d = P.din("par", [128, 4, 5])
    id_d = P.din("ident", [128, 128])
    ob_d = P.din("onesbd", [128, 128])
    mk_d = P.din("mask320", [128, 320])
    is_d = P.din("identst", [128, 64])
    rm_d = P.din("resetm", [128, ST])
    o_d = P.dout("oT", [512, SEQ], BF16)
    par = P.sb("par", [128, 4, 5])
    ident = P.sb("ident", [128, 128])
    onesbd = P.sb("onesbd", [128, 128])
    mask = P.sb("mask320", [128, 320])
    identst = P.sb("identst", [128, 64])
    resetm = P.sb("resetm", [128, ST])
    for t, dd, k in ((par, par_d, "par"), (ident, id_d, "ident"), (onesbd, ob_d, "onesbd"), (mask, mk_d, "mask"),
                     (identst, is_d, "identst"), (resetm, rm_d, "resetm")):
        P.op("sp", lambda e, t=t, dd=dd: e.dma_start(out=t[:], in_=dd), writes=[k], dma=True)
    banks = [P.ps("bk%d" % i, [128, 512]) for i in range(8)]
    bkA = [banks[2 * hp] for hp in range(4)]
    bkB = [banks[2 * hp + 1] for hp in range(4)]
    Gps = [bkA[hp][:, 0:320] for hp in range(4)]
    Tps = [bkA[hp][:, 320:512] for hp in range(4)]
    sqp = [bkB[hp][:, 0:128] for hp in range(4)]
    Pps = [bkB[hp][:, 128:192] for hp in range(4)]
    Wps = [bkB[hp][:, 192:256] for hp in range(4)]
    Ups = [bkB[hp][:, 256:320] for hp in range(4)]
    Yps = [bkB[hp][:, 320:384] for hp in range(4)]
    Sps = [bkB[hp][:, 384:448] for hp in range(4)]
    Ytr = [bkB[hp][:, 448:512] for hp in range(4)]
    KA = ["bkA%d" % hp for hp in range(4)]
    KB = ["bkB%d" % hp for hp in range(4)]

    def T_(name, w, dt=F32):
        return P.sb(name, [128, w], dt)

    inb = {n: [[T_("in_%s_%d_%d" % (n, hp, b), ST) for b in range(2)] for hp in range(4)] for n in ("r", "k", "v", "ld", "a", "g")}
    prep = {n: [[T_("pp_%s_%d_%d" % (n, hp, b), ST) for b in range(2)] for hp in range(4)] for n in ("At", "Rt", "Bt", "Kt", "gam", "bonus")}
    S = [[T_("S_%d_%d" % (hp, b), 64) for b in range(2)] for hp in range(4)]
    for hp in range(4):
        P.op("dve", lambda e, hp=hp: e.memset(S[hp][0][:], 0.0), writes=["S%d_0" % hp])
    sc = {n: T_("sc_" + n, ST) for n in ("cs", "csm", "gm1", "ig", "kk", "kksq", "nrm", "rn", "kkn", "t", "kp", "bv", "rk",
                                          "y", "ysq", "mean", "msq", "var", "std", "rstd", "yc", "yn", "z", "z2")}
    osb = [T_("osb%d" % i, ST, BF16) for i in range(2)]
    Gm = [T_("Gm%d" % hp, 320) for hp in range(4)]
    TTs = [T_("TTs%d" % hp, 192) for hp in range(4)]
    Nn = [[T_("Nn%d_%d" % (hp, b), 128) for b in range(2)] for hp in range(4)]
    Pm = [[T_("Pm%d_%d" % (hp, b), 64) for b in range(2)] for hp in range(4)]
    Wsb = [T_("Wsb%d" % hp, 64) for hp in range(4)]
    Usb = [T_("Usb%d" % hp, 64) for hp in range(4)]
    Ysb = [T_("Ysb%d" % hp, 64) for hp in range(4)]
    ycm = [T_("ycm%d" % hp, ST) for hp in range(4)]
    cnt = dict(ev=0, o=0, g=0, sq=0, pp=0, y=0, s=0)
    HS = [slice(0, 64), slice(64, 128)]

    def evac(dst, src, reads, writes, eng=None):
        if eng is None:
            eng = "act" if cnt["ev"] % 2 == 0 else "dve"
            cnt["ev"] += 1
        if eng == "act":
            P.op("act", lambda e: e.copy(dst, src), reads=reads, writes=writes)
        else:
            P.op("dve", lambda e: e.tensor_copy(dst, src), reads=reads, writes=writes)

    def do_prep(hp, st):
        b = st % 2
        t0 = st * ST
        I = {n: inb[n][hp][b] for n in inb}
        Pp = {n: prep[n][hp][b] for n in prep}
        ik = lambda n: "in_%s_%d_%d" % (n, hp, b)
        pk = lambda n: "pp_%s_%d_%d" % (n, hp, b)
        for n in ("r", "k", "v", "ld", "a", "g"):
            P.op("sp", lambda e, n=n, t=I[n]: e.dma_start(out=t[:], in_=ins_d[n][hp * 128:(hp + 1) * 128, t0:t0 + ST]),
                 writes=[ik(n)], dma=True)
        col = lambda j: par[:, hp, j:j + 1]
        dve = lambda fn, r, w: P.op("dve", fn, reads=r, writes=w)
        act = lambda fn, r, w: P.op("act", fn, reads=r, writes=w)
        pool = lambda fn, r, w: P.op("pool", fn, reads=r, writes=w)
        dve(lambda e: e.tensor_tensor_scan(sc["cs"][:], resetm[:], I["ld"][:], 0.0, ALU.mult, ALU.add), [ik("ld"), "resetm"], ["cs"])
        act(lambda e: e.activation(Pp["gam"][:], sc["cs"][:], AF.Exp), ["cs"], [pk("gam")])
        pool(lambda e: e.tensor_tensor(sc["csm"][:], sc["cs"][:], I["ld"][:], ALU.subtract), ["cs", ik("ld")], ["csm"])
        act(lambda e: e.activation(sc["gm1"][:], sc["csm"][:], AF.Exp), ["csm"], ["gm1"])
        act(lambda e: e.activation(sc["ig"][:], sc["cs"][:], AF.Exp, scale=-1.0), ["cs"], ["ig"])
        pool(lambda e: e.tensor_scalar(sc["kk"][:], I["k"][:], col(0), None, ALU.mult), [ik("k"), "par"], ["kk"])
        act(lambda e: e.activation(sc["kksq"][:], sc["kk"][:], AF.Square), ["kk"], ["kksq"])
        P.op("pe", lambda e: e.matmul(bkA[hp][:, 0:256], onesbd[:], sc["kksq"][:], start=True, stop=True), reads=["onesbd", "kksq"], writes=[KA[hp]])
        act(lambda e: e.activation(sc["nrm"][:], bkA[hp][:, 0:256], AF.Sqrt), [KA[hp]], ["nrm", KA[hp]])
        dve(lambda e: e.tensor_scalar(sc["nrm"][:], sc["nrm"][:], 1e-12, None, ALU.max), ["nrm"], ["nrm"])
        dve(lambda e: e.reciprocal(sc["rn"][:], sc["nrm"][:]), ["nrm"], ["rn"])
        pool(lambda e: e.tensor_tensor(sc["kkn"][:], sc["kk"][:], sc["rn"][:], ALU.mult), ["kk", "rn"], ["kkn"])
        dve(lambda e: e.tensor_scalar(sc["t"][:], I["a"][:], col(1), col(1), ALU.mult, ALU.subtract), [ik("a"), "par"], ["t"])
        dve(lambda e: e.scalar_tensor_tensor(sc["kp"][:], sc["t"][:], 1.0, I["k"][:], ALU.add, ALU.mult), ["t", ik("k")], ["kp"])
        dve(lambda e: e.scalar_tensor_tensor(Pp["At"][:], sc["gm1"][:], -1.0, sc["kkn"][:], ALU.mult, ALU.mult), ["gm1", "kkn"], [pk("At")])
        pool(lambda e: e.tensor_tensor(Pp["Rt"][:], Pp["gam"][:], I["r"][:], ALU.mult), [pk("gam"), ik("r")], [pk("Rt")])
        pool(lambda e: e.tensor_tensor(sc["bv"][:], sc["kkn"][:], I["a"][:], ALU.mult), ["kkn", ik("a")], ["bv"])
        pool(lambda e: e.tensor_tensor(Pp["Bt"][:], sc["bv"][:], sc["ig"][:], ALU.mult), ["bv", "ig"], [pk("Bt")])
        dve(lambda e: e.tensor_tensor(Pp["Kt"][:], sc["kp"][:], sc["ig"][:], ALU.mult), ["kp", "ig"], [pk("Kt")])
        dve(lambda e: e.scalar_tensor_tensor(sc["rk"][:], I["r"][:], col(2), sc["kp"][:], ALU.mult, ALU.mult), [ik("r"), "par", "kp"], ["rk"])
        P.op("pe", lambda e: e.matmul(bkA[hp][:, 256:512], onesbd[:], sc["rk"][:], start=True, stop=True), reads=["onesbd", "rk"], writes=[KA[hp]])
        dve(lambda e: e.tensor_tensor(Pp["bonus"][:], bkA[hp][:, 256:512], I["v"][:], ALU.mult), [KA[hp], ik("v")], [pk("bonus"), KA[hp]])

    def chunk_stages(hp, st, ch):
        b = st % 2
        sl = slice(ch * 64, (ch + 1) * 64)
        Pp = {n: prep[n][hp][b] for n in prep}
        pk = lambda n: "pp_%s_%d_%d" % (n, hp, b)
        vt = inb["v"][hp][b]
        vk = "in_v_%d_%d" % (hp, b)
        cidx = st * NCH + ch
        sp_, sn_ = S[hp][cidx % 2], S[hp][(cidx + 1) % 2]
        skp, skn = "S%d_%d" % (hp, cidx % 2), "S%d_%d" % (hp, (cidx + 1) % 2)
        gi = hp
        gk, tk = KA[hp], KA[hp]
        stages = []

        def mm(out, lhsT, rhs, start, stop, reads, writes):
            P.op("pe", lambda e: e.matmul(out, lhsT, rhs, start=start, stop=stop), reads=reads, writes=writes)

        def st_gram():
            for h2 in range(2):
                hs = HS[h2]
                pairs = (("Bt", "At"), ("At", "Bt"), ("Kt", "At"), ("Bt", "Rt"), ("Kt", "Rt"))
                for i, (l, r) in enumerate(pairs):
                    mm(Gps[gi][hs, 64 * i:64 * i + 64], Pp[l][hs, sl], Pp[r][hs, sl], True, True, [pk(l), pk(r)], [gk])
                for i, (t, k) in enumerate(((Pp["Bt"], pk("Bt")), (Pp["Kt"], pk("Kt")), (vt, vk))):
                    P.op("pe", lambda e, i=i, t=t, hs=hs, h2=h2: e.matmul(Tps[gi][hs, 64 * i:64 * i + 64], t[hs, sl], ident[hs, 64 * h2:64 * h2 + 64], start=True, stop=True),
                         reads=[k, "ident"], writes=[tk])
        stages.append(st_gram)

        def st_gevac():
            P.op("dve", lambda e: e.tensor_tensor(Gm[hp][:], Gps[gi], mask[:], ALU.mult), reads=[gk, "mask"], writes=["Gm%d" % hp, gk])
            P.op("act", lambda e: e.copy(TTs[hp][:], Tps[gi]), reads=[tk], writes=["TTs%d" % hp, tk])
        stages.append(st_gevac)

        def st_p0():
            P.op("pool", lambda e: e.tensor_tensor(Pm[hp][0][:], Gm[hp][:, 0:64], identst[:], ALU.add), reads=["Gm%d" % hp, "identst"], writes=["Pm%d_0" % hp])
        stages.append(st_p0)
        state = dict(N=(Gm[hp][:, 0:64], "Gm%d" % hp), NT=(Gm[hp][:, 64:128], "Gm%d" % hp))
        for lvl in range(5):
            def st_sq(lvl=lvl):
                si = hp
                state["si"] = si
                (N, nk), (NT, ntk) = state["N"], state["NT"]
                for h2 in range(2):
                    hs = HS[h2]
                    mm(sqp[si][hs, 0:64], NT[hs], N[hs], True, True, [nk, ntk], [KB[hp]])
                    mm(sqp[si][hs, 64:128], N[hs], NT[hs], True, True, [nk, ntk], [KB[hp]])
            stages.append(st_sq)

            def st_sqev(lvl=lvl):
                si = state["si"]
                nb = Nn[hp][lvl % 2]
                nbk = "Nn%d_%d" % (hp, lvl % 2)
                evac(nb[:], sqp[si], [KB[hp]], [nbk, KB[hp]])
                state["N"] = (nb[:, 0:64], nbk)
                state["NT"] = (nb[:, 64:128], nbk)
            stages.append(st_sqev)

            def st_pu(lvl=lvl):
                pi = hp
                state["pi"] = pi
                (NT, ntk) = state["NT"]
                pc = Pm[hp][lvl % 2]
                pck = "Pm%d_%d" % (hp, lvl % 2)
                for h2 in range(2):
                    hs = HS[h2]
                    mm(Pps[pi][hs], NT[hs], pc[hs], True, False, [ntk, pck], [KB[hp]])
                    mm(Pps[pi][hs], ident[hs, 64 * h2:64 * h2 + 64], pc[hs], False, True, ["ident", pck], [KB[hp]])
            stages.append(st_pu)

            def st_puev(lvl=lvl):
                pi = state["pi"]
                pn = Pm[hp][(lvl + 1) % 2]
                evac(pn[:], Pps[pi], [KB[hp]], ["Pm%d_%d" % (hp, (lvl + 1) % 2), KB[hp]])
            stages.append(st_puev)
        Tm, Tk = Pm[hp][1], "Pm%d_1" % hp

        def st_w():
            for h2 in range(2):
                hs = HS[h2]
                mm(Wps[hp][hs], Pp["At"][hs, sl], sp_[hs], True, False, [pk("At"), skp], [KB[hp]])
                mm(Wps[hp][hs], Gm[hp][hs, 128:192], TTs[hp][hs, 128:192], False, True, ["Gm%d" % hp, "TTs%d" % hp], [KB[hp]])
        stages.append(st_w)
        stages.append(lambda: evac(Wsb[hp][:], Wps[hp], [KB[hp]], ["Wsb%d" % hp, KB[hp]], "act"))

        def st_u():
            for h2 in range(2):
                hs = HS[h2]
                mm(Ups[hp][hs], Tm[hs], Wsb[hp][hs], True, True, [Tk, "Wsb%d" % hp], [KB[hp]])
        stages.append(st_u)
        stages.append(lambda: evac(Usb[hp][:], Ups[hp], [KB[hp]], ["Usb%d" % hp, KB[hp]], "dve"))

        def st_ys():
            yi = hp
            state["yi"] = yi
            for h2 in range(2):
                hs = HS[h2]
                mm(Sps[yi][hs], TTs[hp][hs, 0:64], Usb[hp][hs], True, False, ["TTs%d" % hp, "Usb%d" % hp], [KB[hp]])
                mm(Sps[yi][hs], TTs[hp][hs, 64:128], TTs[hp][hs, 128:192], False, False, ["TTs%d" % hp], [KB[hp]])
                mm(Sps[yi][hs], ident[hs, 64 * h2:64 * h2 + 64], sp_[hs], False, True, ["ident", skp], [KB[hp]])
            for h2 in range(2):
                hs = HS[h2]
                mm(Yps[yi][hs], Pp["Rt"][hs, sl], sp_[hs], True, False, [pk("Rt"), skp], [KB[hp]])
                mm(Yps[yi][hs], Gm[hp][hs, 192:256], Usb[hp][hs], False, False, ["Gm%d" % hp, "Usb%d" % hp], [KB[hp]])
                mm(Yps[yi][hs], Gm[hp][hs, 256:320], TTs[hp][hs, 128:192], False, True, ["Gm%d" % hp, "TTs%d" % hp], [KB[hp]])
        stages.append(st_ys)

        def st_ysev():
            yi = state["yi"]
            ce = ch * 64 + 63
            P.op("act", lambda e: e.activation(sn_[:], Sps[yi], AF.Copy, scale=Pp["gam"][:, ce:ce + 1]),
                 reads=[KB[hp], pk("gam")], writes=[skn, KB[hp]])
            P.op("dve", lambda e: e.tensor_copy(Ysb[hp][:], Yps[yi]), reads=[KB[hp]], writes=["Ysb%d" % hp, KB[hp]])
        stages.append(st_ysev)

        def st_ytr():
            for h2 in range(2):
                hs = HS[h2]
                P.op("pe", lambda e, hs=hs, h2=h2: e.matmul(Ytr[hp][hs], Ysb[hp][hs], ident[hs, 64 * h2:64 * h2 + 64], start=True, stop=True),
                     reads=["Ysb%d" % hp, "ident"], writes=[KB[hp]])
        stages.append(st_ytr)
        stages.append(lambda: evac(ycm[hp][:, sl], Ytr[hp], [KB[hp]], ["ycm%d" % hp, KB[hp]], "act"))
        return stages

    def do_epilogue(hp, st):
        b = st % 2
        t0 = st * ST
        gt = inb["g"][hp][b]
        gk_ = "in_g_%d_%d" % (hp, b)
        bon = prep["bonus"][hp][b]
        bk = "pp_bonus_%d_%d" % (hp, b)
        col = lambda j: par[:, hp, j:j + 1]
        dve = lambda fn, r, w: P.op("dve", fn, reads=r, writes=w)
        act = lambda fn, r, w: P.op("act", fn, reads=r, writes=w)
        pool = lambda fn, r, w: P.op("pool", fn, reads=r, writes=w)
        yk = "ycm%d" % hp
        sum_ps = bkA[hp][:, 0:256]
        ssq_ps = bkA[hp][:, 256:512]
        act(lambda e: e.activation(sc["ysq"][:], ycm[hp][:], AF.Square), [yk], ["ysq"])
        P.op("pe", lambda e: e.matmul(sum_ps, onesbd[:], ycm[hp][:], start=True, stop=True), reads=["onesbd", yk], writes=[KA[hp]])
        P.op("pe", lambda e: e.matmul(ssq_ps, onesbd[:], sc["ysq"][:], start=True, stop=True), reads=["onesbd", "ysq"], writes=[KA[hp]])
        dve(lambda e: e.tensor_scalar(sc["mean"][:], sum_ps, 1.0 / 64, None, ALU.mult), [KA[hp]], ["mean", KA[hp]])
        pool(lambda e: e.tensor_tensor(sc["msq"][:], sc["mean"][:], sc["mean"][:], ALU.mult), ["mean"], ["msq"])
        dve(lambda e: e.scalar_tensor_tensor(sc["var"][:], ssq_ps, 1.0 / 64, sc["msq"][:], ALU.mult, ALU.subtract), [KA[hp], "msq"], ["var", KA[hp]])
        act(lambda e: e.activation(sc["std"][:], sc["var"][:], AF.Sqrt, bias=GN_EPS, scale=1.0), ["var"], ["std"])
        dve(lambda e: e.reciprocal(sc["rstd"][:], sc["std"][:]), ["std"], ["rstd"])
        pool(lambda e: e.tensor_tensor(sc["yc"][:], ycm[hp][:], sc["mean"][:], ALU.subtract), [yk, "mean"], ["yc"])
        pool(lambda e: e.tensor_tensor(sc["yn"][:], sc["yc"][:], sc["rstd"][:], ALU.mult), ["yc", "rstd"], ["yn"])
        dve(lambda e: e.tensor_scalar(sc["z"][:], sc["yn"][:], col(3), col(4), ALU.mult, ALU.add), ["yn", "par"], ["z"])
        pool(lambda e: e.tensor_tensor(sc["z2"][:], sc["z"][:], bon[:], ALU.add), ["z", bk], ["z2"])
        oi = cnt["o"] % 2
        cnt["o"] += 1
        pool(lambda e: e.tensor_tensor(osb[oi][:], sc["z2"][:], gt[:], ALU.mult), ["z2", gk_], ["osb%d" % oi])
        P.op("sp", lambda e: e.dma_start(out=o_d[hp * 128:(hp + 1) * 128, t0:t0 + ST], in_=osb[oi][:]), reads=["osb%d" % oi], dma=True)

    import os
    for st in range(int(os.environ.get('RW_NST', NST))):
        for hp in range(4):
            do_prep(hp, st)
        for ch in range(int(os.environ.get('RW_NCH', NCH))):
            sts = [chunk_stages(hp, st, ch) for hp in range(4)]
            for i in range(min(len(sts[0]), int(os.environ.get('RW_NSTAGE', 1000)))):
                for hp in range(4):
                    sts[hp][i]()
        if int(os.environ.get('RW_EPI', 1)):
            for hp in range(4):
                do_epilogue(hp, st)
    return P.finish()


def rwkv_consts():
    ident = np.eye(128, dtype=np.float32)
    onesbd = np.zeros((128, 128), np.float32)
    onesbd[:64, :64] = 1.0
    onesbd[64:, 64:] = 1.0
    i = (np.arange(128) % 64)[:, None]
    t = np.arange(64)[None, :]
    su = (i < t).astype(np.float32)
    sl_ = (t < i).astype(np.float32)
    ui = (i <= t).astype(np.float32)
    mask = np.concatenate([su, sl_, su, ui, ui], axis=1)
    identst = (i == t).astype(np.float32)
    resetm = np.ones((128, RW_ST), np.float32)
    resetm[:, ::64] = 0.0
    return dict(ident=ident, onesbd=onesbd, mask320=np.ascontiguousarray(mask), identst=identst, resetm=resetm)


def run_rwkv_core_stage(proj, inp, ic):
    if "rwkvc" not in _cache:
        _cache["rwkvc"] = build_rwkv_core_stage()
    consts = rwkv_consts()
    pv = np.stack([inp["rwkv_k_k"][ic], inp["rwkv_k_a"][ic], inp["rwkv_r_k"][ic].reshape(-1), inp["rwkv_ln_w"][ic], inp["rwkv_ln_b"][ic]], axis=1)
    maps = []
    for c in range(NCORES):
        seq, hq = divmod(c, 4)
        m = dict(consts)
        for n in ("r", "k", "v", "ld", "a", "g"):
            m[n] = np.ascontiguousarray(np.concatenate([proj[4 * seq + i][n][512 * hq:512 * hq + 512, :] for i in range(4)], axis=1))
        m["par"] = np.ascontiguousarray(pv[512 * hq:512 * hq + 512].reshape(4, 128, 5).transpose(1, 0, 2)).astype(np.float32)
        maps.append(m)
    res = run_bass_kernel_spmd(_cache["rwkvc"], maps, core_ids=list(range(NCORES)))
    ys = [r["oT"] for r in res.results]
    out = []
    for c in range(NCORES):
        seq, i = divmod(c, 4)
        out.append(np.ascontiguousarray(np.concatenate([ys[4 * seq + hq][:, i * TOK:(i + 1) * TOK] for hq in range(4)], axis=0)))
    return out


def kernel(**inputs):
    inp = {k: np.asarray(v) for k, v in inputs.items()}
    x = inp["x"].reshape(NCORES * TOK, D)
    xT = [np.ascontiguousarray(x[c * TOK:(c + 1) * TOK].T) for c in range(NCORES)]
    ia = ib = ic = 0
    depth = inp["norm_mix"].shape[0]
    for layer in range(depth):
        kind = layer % 3
        gfin = inp["norm_f"] if layer == depth - 1 else None
        if kind == 0:
            qkv = run_qkv_stage(xT, inp["attn_w_qkv"][ia], inp["norm_mix"][layer])
            aT = run_attn_stage(qkv)
            wo = inp["attn_w_o"][ia]
            ia += 1
        elif kind == 1:
            uT = run_normproj_stage(xT, inp["ssm_w_in"][ib], inp["norm_mix"][layer])
            aT = run_s5_stage(uT, inp["ssm_log_dt"][ib], inp["ssm_a_re"][ib], inp["ssm_a_im"][ib], inp["ssm_b_re"][ib],
                              inp["ssm_b_im"][ib], inp["ssm_c_re"][ib], inp["ssm_c_im"][ib], inp["ssm_d"][ib])
            wo = inp["ssm_w_out"][ib]
            ib += 1
        else:
            proj = run_rwkv_proj_stage(xT, inp, layer, ic)
            aT = run_rwkv_core_stage(proj, inp, ic)
            wo = inp["rwkv_w_o"][ic]
            ic += 1
        xT = run_mlp_stage(xT, inp["mlp_w1"][layer], inp["mlp_w2"][layer], inp["norm_mlp"][layer], g_final=gfin,
                           aT_cores=aT, wo=wo)
    out = np.concatenate([np.asarray(t).T for t in xT], axis=0).reshape(inp["x"].shape)
    return np.ascontiguousarray(out.astype(np.float32))
```

```python
import numpy as np
from contextlib import ExitStack
import concourse.bass as bass
import concourse.mybir as mybir
from concourse.bass_utils import run_bass_kernel_spmd

F32 = mybir.dt.float32
BF16 = mybir.dt.bfloat16
ALU = mybir.AluOpType
AF = mybir.ActivationFunctionType

D = 2048
KC = 16
NCORES = 8
TOK = 2048
EPS = 1e-5

COMPUTE = ("pe", "dve", "act", "pool")
NDMA = 12


class Sched:
    def __init__(self, nc):
        self.nc = nc
        self.ins = {e: [] for e in ("pe", "dve", "act", "pool", "sp")}
        self.last_w = {}
        self.readers = {}

    def op(self, eng, fn, reads=(), writes=(), dma=False):
        lst = self.ins[eng]
        idx = len(lst)
        me = (eng, idx, dma)
        deps = set()
        for r in reads:
            w = self.last_w.get(r)
            if w is not None:
                deps.add(w)
        for r in writes:
            w = self.last_w.get(r)
            if w is not None:
                deps.add(w)
            for rd in self.readers.get(r, {}).values():
                for t in rd:
                    if t[0] == eng and not dma and not t[2]:
                        continue
                    deps.add(t)
        deps.discard(me)
        if eng == "pe":
            deps = {d for d in deps if d[0] != "pe"}
        lst.append(dict(fn=fn, deps=deps, dma=dma, marked=False))
        for d in deps:
            self.ins[d[0]][d[1]]["marked"] = True
        for r in writes:
            self.last_w[r] = me
            self.readers[r] = {}
        for r in reads:
            rd = self.readers.setdefault(r, {})
            if dma:
                rd.setdefault((eng, "dma"), []).append(me)
            else:
                rd[(eng, "c")] = [me]
        return me

    def emit(self):
        nc = self.nc
        ins = self.ins
        with ExitStack() as es:
            csem = {e: es.enter_context(nc.semaphore("cs_" + e)) for e in COMPUTE}
            dsem = {q: [es.enter_context(nc.semaphore("ds_%s%d" % (q, i))) for i in range(NDMA)]
                    for q in ("sp", "pool")}
            for e, lst in ins.items():
                c = 0
                nd = 0
                for it in lst:
                    if it["dma"]:
                        it["dsem"] = dsem[e][nd % NDMA]
                        it["dkey"] = (e, nd % NDMA)
                        it["dval"] = 16 * (nd // NDMA + 1)
                        it["dprev"] = 16 * (nd // NDMA)
                        nd += 1
                    else:
                        if it["marked"]:
                            c += 1
                        it["cval"] = c
            block = es.enter_context(nc.Block())

            def run(e, engobj):
                seen_c = {x: 0 for x in COMPUTE}
                seen_d = {}
                for it in ins[e]:
                    for d in sorted(it["deps"]):
                        tgt = ins[d[0]][d[1]]
                        if tgt["dma"]:
                            if seen_d.get(tgt["dkey"], 0) >= tgt["dval"]:
                                continue
                            engobj.wait_ge(tgt["dsem"], tgt["dval"])
                            seen_d[tgt["dkey"]] = tgt["dval"]
                        else:
                            v = tgt["cval"]
                            if seen_c[d[0]] >= v:
                                continue
                            engobj.wait_ge(csem[d[0]], v)
                            seen_c[d[0]] = v
                    if it["dma"]:
                        if it["dprev"] > 0 and seen_d.get(it["dkey"], 0) < it["dprev"]:
                            engobj.wait_ge(it["dsem"], it["dprev"])
                            seen_d[it["dkey"]] = it["dprev"]
                        it["fn"](engobj).then_inc(it["dsem"], 16)
                    else:
                        r = it["fn"](engobj)
                        if it["marked"]:
                            r.then_inc(csem[e], 1)
                if e in ("sp", "pool"):
                    last = {}
                    for it in ins[e]:
                        if it["dma"]:
                            last[it["dkey"]] = (it["dsem"], it["dval"])
                    for s, v in last.values():
                        engobj.wait_ge(s, v)

            @block.tensor
            def _(eng):
                run("pe", eng)

            @block.vector
            def _(eng):
                run("dve", eng)

            @block.scalar
            def _(eng):
                run("act", eng)

            @block.gpsimd
            def _(eng):
                run("pool", eng)

            @block.sync
            def _(eng):
                run("sp", eng)


class Prog:
    def __init__(self):
        self.nc = bass.Bass("TRN2", target_bir_lowering=False)
        self.S = Sched(self.nc)
        self.es = ExitStack()
        self.uid = 0

    def sb(self, name, shape, dt=F32):
        return self.es.enter_context(self.nc.sbuf_tensor("s_" + name, shape, dt))

    def ps(self, name, shape, dt=F32):
        return self.es.enter_context(self.nc.psum_tensor("p_" + name, shape, dt))

    def din(self, name, shape, dt=F32):
        return self.nc.dram_tensor(name, list(shape), dt, kind="ExternalInput").ap()

    def dout(self, name, shape, dt=F32):
        return self.nc.dram_tensor(name, list(shape), dt, kind="ExternalOutput").ap()

    def op(self, *a, **k):
        return self.S.op(*a, **k)

    def finish(self):
        self.S.emit()
        self.es.close()
        return self.nc


class Common:
    def __init__(self, P):
        self.P = P
        self.ones = P.sb("ones_f", [128, 128], F32)
        P.op("dve", lambda e: e.memset(self.ones[:], 1.0), writes=["ones"])
        self.sq = [P.sb("sq%d" % i, [128, 512], F32) for i in range(2)]
        self.std = P.sb("std", [128, 512], F32)
        self.rstd = P.sb("rstd", [128, 512], F32)
        self.stat = P.ps("stat_ps", [128, 512])
        self.nsq = 0


def rmsnorm(P, C, x_sb, g_sb, gcol0, out_sb, TT, xkey, okey, out_dt_is_f32=False):
    for h in range(TT // 512):
        sl = slice(h * 512, (h + 1) * 512)
        for kc in range(KC):
            j = C.nsq % 2
            C.nsq += 1
            P.op("act", lambda e, j=j, kc=kc, sl=sl: e.activation(C.sq[j][:], x_sb[:, kc, sl], AF.Square),
                 reads=[xkey + "_%d" % kc], writes=["sq%d" % j])
            P.op("pe", lambda e, j=j, kc=kc: e.matmul(C.stat[:], C.ones[:], C.sq[j][:], start=(kc == 0), stop=(kc == KC - 1)),
                 reads=["ones", "sq%d" % j], writes=["stat"])
        P.op("act", lambda e: e.activation(C.std[:], C.stat[:], AF.Sqrt, bias=EPS, scale=1.0 / D),
             reads=["stat"], writes=["std"])
        P.op("dve", lambda e: e.reciprocal(C.rstd[:], C.std[:]), reads=["std"], writes=["rstd"])
        for kc in range(KC):
            P.op("dve", lambda e, kc=kc, sl=sl: e.scalar_tensor_tensor(
                out_sb[:, kc, sl], x_sb[:, kc, sl], g_sb[:, gcol0 + kc:gcol0 + kc + 1], C.rstd[:], ALU.mult, ALU.mult),
                reads=[xkey + "_%d" % kc, "rstd", "gvec"], writes=[okey + "_%d" % kc])


class MlpBufs:
    FB = 512

    def __init__(self, P):
        FB = self.FB
        self.w1 = [P.sb("w1b%d" % i, [128, KC, FB], BF16) for i in range(2)]
        self.w2 = [P.sb("w2b%d" % i, [128, FB // 128, D], BF16) for i in range(2)]
        self.g1 = [P.ps("g1_%d" % i, [128, 512]) for i in range(2)]
        self.acc = [P.ps("acc_%d" % i, [128, 512]) for i in range(4)]
        self.relu = [P.sb("relu%d" % i, [128, 512], F32) for i in range(2)]
        self.cnt_w = 0
        self.cnt_g1 = 0
        self.cnt_acc = 0
        self.cnt_relu = 0


def mlp_block(P, C, M, x_sb, xn_sb, hT, TT, w1_d, w2_d, xkey, xnkey):
    FB = M.FB
    FBC = FB // 128
    NB = w1_d.shape[1] // FB
    NH = TT // 512
    w1v = w1_d.rearrange("(kc p) f -> p kc f", p=128)
    w2v = w2_d.rearrange("(fc p) n -> p fc n", p=128)

    def gemm2(b):
        par = b % 2
        for h in range(NH):
            sl = slice(h * 512, (h + 1) * 512)
            for n in range(KC):
                a = M.cnt_acc % 4
                M.cnt_acc += 1
                for fc in range(FBC):
                    P.op("pe", lambda e, a=a, fc=fc, n=n, sl=sl, par=par, wp=M.wpar[b]: e.matmul(
                        M.acc[a][:], M.w2[wp][:, fc, n * 128:(n + 1) * 128], hT[par][:, fc, sl],
                        start=(fc == 0), stop=(fc == FBC - 1)),
                        reads=["w2b%d" % M.wpar[b], "hT%d_%d_%d" % (par, fc, h)], writes=["acc%d" % a])
                P.op("dve", lambda e, a=a, n=n, sl=sl: e.tensor_tensor(x_sb[:, n, sl], x_sb[:, n, sl], M.acc[a][:], ALU.add),
                     reads=["acc%d" % a, xkey + "_%d" % n], writes=[xkey + "_%d" % n])

    M.wpar = {}
    pend = []
    for b in range(NB):
        wp = M.cnt_w % 2
        M.cnt_w += 1
        M.wpar[b] = wp
        P.op("pool", lambda e, b=b, wp=wp: e.dma_start(out=M.w1[wp][:], in_=w1v[:, :, b * FB:(b + 1) * FB]),
             writes=["w1b%d" % wp], dma=True)
        P.op("pool", lambda e, b=b, wp=wp: e.dma_start(out=M.w2[wp][:], in_=w2v[:, b * FBC:(b + 1) * FBC, :]),
             writes=["w2b%d" % wp], dma=True)
        par = b % 2
        for h in range(NH):
            sl = slice(h * 512, (h + 1) * 512)
            for fc in range(FBC):
                g = M.cnt_g1 % 2
                M.cnt_g1 += 1
                for kc in range(KC):
                    P.op("pe", lambda e, g=g, kc=kc, fc=fc, sl=sl, wp=wp: e.matmul(
                        M.g1[g][:], M.w1[wp][:, kc, fc * 128:(fc + 1) * 128], xn_sb[:, kc, sl],
                        start=(kc == 0), stop=(kc == KC - 1)),
                        reads=["w1b%d" % wp, xnkey + "_%d" % kc], writes=["g1_%d" % g])
                r = M.cnt_relu % 2
                M.cnt_relu += 1
                P.op("act", lambda e, g=g, r=r: e.activation(M.relu[r][:], M.g1[g][:], AF.Relu),
                     reads=["g1_%d" % g], writes=["relu%d" % r])
                if pend:
                    pend.pop()()
                pend.append(lambda r=r, fc=fc, sl=sl, par=par, h=h: P.op(
                    "act", lambda e: e.activation(hT[par][:, fc, sl], M.relu[r][:], AF.Square),
                    reads=["relu%d" % r], writes=["hT%d_%d_%d" % (par, fc, h)]))
        if b >= 1:
            gemm2(b - 1)
    pend.pop()()
    gemm2(NB - 1)


def proj_block(P, C, M, x_sb, a_sb, TT, wo_d, glu, xkey, akey):
    NH = TT // 512
    wv = wo_d.rearrange("(kc p) f -> p kc f", p=128)
    for nt in range(4):
        wps = []
        for part in range(2 if glu else 1):
            wp = M.cnt_w % 2
            M.cnt_w += 1
            wps.append(wp)
            c0 = part * D + nt * 512
            P.op("pool", lambda e, wp=wp, c0=c0: e.dma_start(out=M.w1[wp][:], in_=wv[:, :, c0:c0 + 512]),
                 writes=["w1b%d" % wp], dma=True)
        for nn in range(4):
            n = nt * 4 + nn
            for h in range(NH):
                sl = slice(h * 512, (h + 1) * 512)
                accs = []
                for part in range(len(wps)):
                    a = M.cnt_acc % 4
                    M.cnt_acc += 1
                    accs.append(a)
                    wp = wps[part]
                    for kc in range(KC):
                        P.op("pe", lambda e, a=a, wp=wp, kc=kc, nn=nn, sl=sl: e.matmul(
                            M.acc[a][:], M.w1[wp][:, kc, nn * 128:(nn + 1) * 128], a_sb[:, kc, sl],
                            start=(kc == 0), stop=(kc == KC - 1)),
                            reads=["w1b%d" % wp, akey + "_%d" % kc], writes=["acc%d" % a])
                if not glu:
                    a = accs[0]
                    P.op("dve", lambda e, a=a, n=n, sl=sl: e.tensor_tensor(x_sb[:, n, sl], x_sb[:, n, sl], M.acc[a][:], ALU.add),
                         reads=["acc%d" % a, xkey + "_%d" % n], writes=[xkey + "_%d" % n])
                else:
                    a1, a2 = accs
                    P.op("act", lambda e, a2=a2: e.activation(M.relu[0][:], M.acc[a2][:], AF.Sigmoid),
                         reads=["acc%d" % a2], writes=["relu0"])
                    P.op("dve", lambda e, a1=a1: e.tensor_tensor(M.relu[1][:], M.acc[a1][:], M.relu[0][:], ALU.mult),
                         reads=["acc%d" % a1, "relu0"], writes=["relu1"])
                    P.op("dve", lambda e, n=n, sl=sl: e.tensor_tensor(x_sb[:, n, sl], x_sb[:, n, sl], M.relu[1][:], ALU.add),
                         reads=["relu1", xkey + "_%d" % n], writes=[xkey + "_%d" % n])


def build_mlp_stage(final_norm, proj=None):
    P = Prog()
    TT = 1024
    xT_d = P.din("xT", [D, TOK])
    w1_d = P.din("w1", [D, 4 * D])
    w2_d = P.din("w2", [4 * D, D])
    g_d = P.din("gv", [128, 2 * KC])
    if proj:
        a_d = P.din("aT", [D, TOK], BF16)
        wo_d = P.din("wo", [D, 2 * D if proj == "glu" else D])
    out_d = P.dout("yT", [D, TOK])
    C = Common(P)
    M = MlpBufs(P)
    x_sb = P.sb("x_sb", [128, KC, TT], F32)
    xn_sb = P.sb("xn_sb", [128, KC, TT], BF16)
    hT = [P.sb("hT%d" % i, [128, M.FB // 128, TT], BF16) for i in range(2)]
    g_sb = P.sb("g_sb", [128, 2 * KC], F32)
    P.op("sp", lambda e: e.dma_start(out=g_sb[:], in_=g_d), writes=["gvec"], dma=True)
    xv = xT_d.rearrange("(kc p) t -> p kc t", p=128)
    ov = out_d.rearrange("(kc p) t -> p kc t", p=128)
    for ps_ in range(TOK // TT):
        tsl = slice(ps_ * TT, (ps_ + 1) * TT)
        for q in range(4):
            P.op("sp", lambda e, q=q, tsl=tsl: e.dma_start(out=x_sb[:, 4 * q:4 * q + 4, :], in_=xv[:, 4 * q:4 * q + 4, tsl]),
                 writes=["x_%d" % k for k in range(4 * q, 4 * q + 4)], dma=True)
        if proj:
            av = a_d.rearrange("(kc p) t -> p kc t", p=128)
            for q in range(2):
                P.op("sp", lambda e, q=q, tsl=tsl: e.dma_start(out=xn_sb[:, 8 * q:8 * q + 8, :], in_=av[:, 8 * q:8 * q + 8, tsl]),
                     writes=["xn_%d" % k for k in range(8 * q, 8 * q + 8)], dma=True)
            proj_block(P, C, M, x_sb, xn_sb, TT, wo_d, proj == "glu", "x", "xn")
        rmsnorm(P, C, x_sb, g_sb, 0, xn_sb, TT, "x", "xn")
        mlp_block(P, C, M, x_sb, xn_sb, hT, TT, w1_d, w2_d, "x", "xn")
        if final_norm:
            rmsnorm(P, C, x_sb, g_sb, KC, x_sb, TT, "x", "x")
        for q in range(4):
            P.op("sp", lambda e, q=q, tsl=tsl: e.dma_start(out=ov[:, 4 * q:4 * q + 4, tsl], in_=x_sb[:, 4 * q:4 * q + 4, :]),
                 reads=["x_%d" % k for k in range(4 * q, 4 * q + 4)], dma=True)
    return P.finish()


def gvec_layout(*vecs):
    return np.ascontiguousarray(np.concatenate([v.reshape(KC, 128).T for v in vecs], axis=1)).astype(np.float32)


_cache = {}


_last = {}


def _run(nc, maps):
    import os
    if os.environ.get("KTRACE"):
        res = run_bass_kernel_spmd(nc, maps, core_ids=list(range(NCORES)), trace=True)
        print("KTRACE exec_time_ns", res.exec_time_ns, flush=True)
        _last["res"] = res
        return res
    return run_bass_kernel_spmd(nc, maps, core_ids=list(range(NCORES)))


def run_mlp_stage(xT_cores, w1, w2, g_mlp, g_final=None, aT_cores=None, wo=None):
    proj = None if wo is None else ("glu" if wo.shape[1] == 2 * D else "lin")
    key = ("mlp", g_final is not None, proj)
    if key not in _cache:
        _cache[key] = build_mlp_stage(g_final is not None, proj)
    nc = _cache[key]
    gv = gvec_layout(g_mlp, g_final if g_final is not None else g_mlp)
    in_maps = [{"xT": xT_cores[c], "w1": w1, "w2": w2, "gv": gv} for c in range(NCORES)]
    if proj:
        for c in range(NCORES):
            in_maps[c]["aT"] = aT_cores[c]
            in_maps[c]["wo"] = wo
    res = _run(nc, in_maps)
    return [r["yT"] for r in res.results]


def load_norm_full(P, C, xT_d, g_sb, gcol0, xn_sb, xq, QW=256):
    xv = xT_d.rearrange("(kc p) t -> p kc t", p=128)
    for q in range(TOK // QW):
        b = q % 2
        tsl = slice(q * QW, (q + 1) * QW)
        keys = ["xq%d_%d" % (b, k) for k in range(KC)]
        for s in range(2):
            P.op("sp", lambda e, b=b, s=s, tsl=tsl: e.dma_start(out=xq[b][:, 8 * s:8 * s + 8, :], in_=xv[:, 8 * s:8 * s + 8, tsl]),
                 writes=keys[8 * s:8 * s + 8], dma=True)
        for kc in range(KC):
            j = C.nsq % 2
            C.nsq += 1
            P.op("act", lambda e, j=j, kc=kc, b=b: e.activation(C.sq[j][:, :QW], xq[b][:, kc, :], AF.Square),
                 reads=[keys[kc]], writes=["sq%d" % j])
            P.op("pe", lambda e, j=j, kc=kc: e.matmul(C.stat[:, :QW], C.ones[:], C.sq[j][:, :QW], start=(kc == 0), stop=(kc == KC - 1)),
                 reads=["ones", "sq%d" % j], writes=["stat"])
        P.op("act", lambda e: e.activation(C.std[:, :QW], C.stat[:, :QW], AF.Sqrt, bias=EPS, scale=1.0 / D),
             reads=["stat"], writes=["std"])
        P.op("dve", lambda e: e.reciprocal(C.rstd[:, :QW], C.std[:, :QW]), reads=["std"], writes=["rstd"])
        for kc in range(KC):
            P.op("dve", lambda e, kc=kc, b=b, tsl=tsl: e.scalar_tensor_tensor(
                xn_sb[:, kc, tsl], xq[b][:, kc, :], g_sb[:, gcol0 + kc:gcol0 + kc + 1], C.rstd[:, :QW], ALU.mult, ALU.mult),
                reads=[keys[kc], "rstd", "gvec"], writes=["xn_%d" % kc])


DIL = (1, 4, 16)


def blk_tok_slice(d, blk):
    J = TOK // (128 * d)
    r, j = divmod(blk, J)
    start = j * 128 * d + r
    return slice(start, start + 127 * d + 1, d) if d > 1 else slice(start, start + 128)


def build_qkv_stage():
    P = Prog()
    xT_d = P.din("xT", [D, TOK])
    w_d = P.din("wqkv", [D, 9 * D])
    g_d = P.din("gv", [128, KC])
    q_d = P.dout("qT", [3, 16, 128, TOK], BF16)
    k_d = P.dout("kT", [3, 16, 128, TOK], BF16)
    v_d = P.dout("v", [3, 16, 128, 16, 128], BF16)
    C = Common(P)
    g_sb = P.sb("g_sb", [128, KC], F32)
    P.op("sp", lambda e: e.dma_start(out=g_sb[:], in_=g_d), writes=["gvec"], dma=True)
    xq = [P.sb("xq%d" % i, [128, KC, 256], F32) for i in range(2)]
    xn = P.sb("xn", [128, KC, TOK], BF16)
    load_norm_full(P, C, xT_d, g_sb, 0, xn, xq, 256)
    xnkeys = ["xn_%d" % k for k in range(KC)]
    NW = 3
    wb = [P.sb("wb%d" % i, [128, KC, 512], BF16) for i in range(NW)]
    gp = [P.ps("gp%d" % i, [128, 512]) for i in range(4)]
    qst = [P.sb("qst%d" % i, [128, TOK], BF16) for i in range(3)]
    vst = [P.sb("vst%d" % i, [128, 16, 512], BF16) for i in range(2)]
    wv = w_d.rearrange("(kc p) f -> p kc f", p=128)
    cnt = dict(w=0, gp=0, q=0, v=0, ev=0)
    scale = 128.0 ** -0.5
    for g in range(3):
        d = DIL[g]
        for t in range(3):
            for nt in range(4):
                col0 = g * 3 * D + t * D + nt * 512
                wi = cnt["w"] % NW
                cnt["w"] += 1
                P.op("pool", lambda e, wi=wi, col0=col0: e.dma_start(out=wb[wi][:], in_=wv[:, :, col0:col0 + 512]),
                     writes=["wb%d" % wi], dma=True)
                if t < 2:
                    for hh in range(4):
                        h = nt * 4 + hh
                        qi = cnt["q"] % 3
                        cnt["q"] += 1
                        for tq in range(4):
                            tsl = slice(tq * 512, (tq + 1) * 512)
                            pi = cnt["gp"] % 4
                            cnt["gp"] += 1
                            for kc in range(KC):
                                P.op("pe", lambda e, pi=pi, wi=wi, kc=kc, hh=hh, tsl=tsl: e.matmul(
                                    gp[pi][:], wb[wi][:, kc, hh * 128:(hh + 1) * 128], xn[:, kc, tsl],
                                    start=(kc == 0), stop=(kc == KC - 1)),
                                    reads=["wb%d" % wi, xnkeys[kc]], writes=["gp%d" % pi])
                            sc = scale if t == 0 else 1.0
                            if cnt["ev"] % 2 == 0:
                                P.op("act", lambda e, pi=pi, qi=qi, tsl=tsl, sc=sc: e.activation(qst[qi][:, tsl], gp[pi][:], AF.Copy, scale=sc),
                                     reads=["gp%d" % pi], writes=["qst%d_%d" % (qi, tq)])
                            else:
                                P.op("dve", lambda e, pi=pi, qi=qi, tsl=tsl, sc=sc: e.tensor_scalar(qst[qi][:, tsl], gp[pi][:], sc, None, ALU.mult),
                                     reads=["gp%d" % pi], writes=["qst%d_%d" % (qi, tq)])
                            cnt["ev"] += 1
                        dst = q_d if t == 0 else k_d
                        P.op("sp", lambda e, qi=qi, dst=dst, g=g, h=h: e.dma_start(out=dst[g, h], in_=qst[qi][:]),
                             reads=["qst%d_%d" % (qi, x) for x in range(4)], dma=True)
                else:
                    vi = cnt["v"] % 2
                    cnt["v"] += 1
                    for blk in range(16):
                        bsl = blk_tok_slice(d, blk)
                        pi = cnt["gp"] % 4
                        cnt["gp"] += 1
                        for kc in range(KC):
                            P.op("pe", lambda e, pi=pi, wi=wi, kc=kc, bsl=bsl: e.matmul(
                                gp[pi][:], xn[:, kc, bsl], wb[wi][:, kc, :],
                                start=(kc == 0), stop=(kc == KC - 1)),
                                reads=["wb%d" % wi, xnkeys[kc]], writes=["gp%d" % pi])
                        if cnt["ev"] % 2 == 0:
                            P.op("act", lambda e, pi=pi, vi=vi, blk=blk: e.copy(vst[vi][:, blk, :], gp[pi][:]),
                                 reads=["gp%d" % pi], writes=["vst%d_%d" % (vi, blk)])
                        else:
                            P.op("dve", lambda e, pi=pi, vi=vi, blk=blk: e.tensor_copy(vst[vi][:, blk, :], gp[pi][:]),
                                 reads=["gp%d" % pi], writes=["vst%d_%d" % (vi, blk)])
                        cnt["ev"] += 1
                    for hh in range(4):
                        h = nt * 4 + hh
                        P.op("sp", lambda e, vi=vi, g=g, h=h, hh=hh: e.dma_start(out=v_d[g, h], in_=vst[vi][:, :, hh * 128:(hh + 1) * 128]),
                             reads=["vst%d_%d" % (vi, x) for x in range(16)], dma=True)
    return P.finish()


def run_qkv_stage(xT_cores, wqkv, g_mix):
    if "qkv" not in _cache:
        _cache["qkv"] = build_qkv_stage()
    nc = _cache["qkv"]
    gv = gvec_layout(g_mix)
    in_maps = [{"xT": xT_cores[c], "wqkv": wqkv, "gv": gv} for c in range(NCORES)]
    res = _run(nc, in_maps)
    return [(r["qT"], r["kT"], r["v"]) for r in res.results]


NEG = -30000.0


def attn_bias_tables(first):
    out = np.zeros((128, 48, 3, 128), np.float32)
    k = np.arange(128)[:, None].astype(np.float64)
    q = np.arange(128)[None, :].astype(np.float64)
    for g in range(3):
        for h in range(16):
            slope = 2.0 ** (-8.0 * (g * 16 + h + 1) / 48.0)
            c = slope * DIL[g]
            cur = np.where(k <= q, -c * (q - k), NEG)
            prev = np.where(k >= q, -c * (128 + q - k), NEG)
            out[:, g * 16 + h, 0, :] = prev
            out[:, g * 16 + h, 1, :] = cur
            out[:, g * 16 + h, 2, :] = NEG if first else prev
    return out


def build_attn_stage():
    P = Prog()
    q_d = P.din("qT", [3, 16, 128, TOK], BF16)
    k_d = P.din("kT", [3, 16, 128, 2 * TOK], BF16)
    v_d = P.din("v", [3, 16, 128, 32, 128], BF16)
    b_d = P.din("bias", [128, 48, 3, 128])
    o_d = P.dout("oT", [D, TOK], BF16)
    bias = P.sb("bias_sb", [128, 48, 3, 128], F32)
    for i in range(4):
        P.op("sp", lambda e, i=i: e.dma_start(out=bias[:, 12 * i:12 * i + 12], in_=b_d[:, 12 * i:12 * i + 12]),
             writes=["bias%d" % i], dma=True)
    ones = P.sb("ones_b", [128, 128], BF16)
    P.op("dve", lambda e: e.memset(ones[:], 1.0), writes=["ones"])
    qh = [P.sb("qh%d" % i, [128, TOK], BF16) for i in range(2)]
    kh = [P.sb("kh%d" % i, [128, 2 * TOK], BF16) for i in range(2)]
    vh = [P.sb("vh%d" % i, [128, 32, 128], BF16) for i in range(2)]
    sps = [P.ps("sps%d" % i, [128, 4, 2, 128]) for i in range(2)]
    ups = [P.ps("ups%d" % i, [128, 512]) for i in range(2)]
    dps = [P.ps("dps%d" % i, [128, 512]) for i in range(2)]
    sbb = [P.sb("sbb%d" % i, [128, 4, 2, 128], F32) for i in range(2)]
    pT = [P.sb("pT%d" % i, [128, 4, 2, 128], BF16) for i in range(2)]
    accU = [P.sb("accU%d" % i, [128, TOK], F32) for i in range(2)]
    accD = [P.sb("accD%d" % i, [128, TOK], F32) for i in range(2)]
    rec = P.sb("rec", [128, TOK], F32)
    osb = [P.sb("osb%d" % i, [128, TOK], BF16) for i in range(2)]
    cnt = dict(ld=0, b=0)
    for h in range(16):
        ai = h % 2
        for g in range(3):
            d = DIL[g]
            J = TOK // (128 * d)
            halo = 128 * d
            li = cnt["ld"] % 2
            cnt["ld"] += 1
            P.op("sp", lambda e, li=li, g=g, h=h: e.dma_start(out=qh[li][:], in_=q_d[g, h]), writes=["qh%d" % li], dma=True)
            P.op("sp", lambda e, li=li, g=g, h=h, halo=halo: e.dma_start(out=kh[li][:, TOK - halo:], in_=k_d[g, h, :, TOK - halo:]),
                 writes=["kh%d" % li], dma=True)
            nb = 16 + d
            P.op("sp", lambda e, li=li, g=g, h=h, nb=nb: e.dma_start(out=vh[li][:, :nb, :], in_=v_d[g, h, :, :nb, :]),
                 writes=["vh%d" % li], dma=True)
            gh = g * 16 + h
            for bt in range(4):
                bi = cnt["b"] % 2
                cnt["b"] += 1
                blks = [divmod(4 * bt + i, J) for i in range(4)]
                for i, (r, j) in enumerate(blks):
                    qs = blk_tok_slice(d, 4 * bt + i)
                    for kb in range(2):
                        st = TOK + (j - 1 + kb) * 128 * d + r
                        ks = slice(st, st + 127 * d + 1, d) if d > 1 else slice(st, st + 128)
                        P.op("pe", lambda e, bi=bi, i=i, kb=kb, li=li, ks=ks, qs=qs: e.matmul(
                            sps[bi][:, i, kb, :], kh[li][:, ks], qh[li][:, qs], start=True, stop=True),
                            reads=["kh%d" % li, "qh%d" % li], writes=["sps%d" % bi])
                for i, (r, j) in enumerate(blks):
                    if j > 0:
                        P.op("dve", lambda e, bi=bi, i=i, gh=gh: e.tensor_tensor(
                            sbb[bi][:, i], sps[bi][:, i], bias[:, gh, 0:2], ALU.add),
                            reads=["sps%d" % bi, "bias%d" % (gh // 12)], writes=["sbb%d_%d" % (bi, i)])
                    else:
                        for kb in range(2):
                            P.op("dve", lambda e, bi=bi, i=i, gh=gh, kb=kb: e.tensor_tensor(
                                sbb[bi][:, i, kb], sps[bi][:, i, kb], bias[:, gh, 2 - kb], ALU.add),
                                reads=["sps%d" % bi, "bias%d" % (gh // 12)], writes=["sbb%d_%d" % (bi, i)])
                P.op("act", lambda e, bi=bi: e.activation(pT[bi][:], sbb[bi][:], AF.Exp),
                     reads=["sbb%d_%d" % (bi, i) for i in range(4)], writes=["pT%d" % bi])
                for i, (r, j) in enumerate(blks):
                    for kb in range(2):
                        vb = r * (J + 1) + (j + kb)
                        P.op("pe", lambda e, bi=bi, i=i, kb=kb, li=li, vb=vb: e.matmul(
                            ups[bi][:, i * 128:(i + 1) * 128], vh[li][:, vb, :], pT[bi][:, i, kb, :],
                            start=(kb == 0), stop=(kb == 1)),
                            reads=["vh%d" % li, "pT%d" % bi], writes=["ups%d" % bi])
                for i in range(4):
                    for kb in range(2):
                        P.op("pe", lambda e, bi=bi, i=i, kb=kb: e.matmul(
                            dps[bi][:, i * 128:(i + 1) * 128], ones[:], pT[bi][:, i, kb, :],
                            start=(kb == 0), stop=(kb == 1)),
                            reads=["ones", "pT%d" % bi], writes=["dps%d" % bi])
                def views(acc, psum):
                    if d == 1:
                        return acc[:, 512 * bt:512 * bt + 512], psum[:]
                    av = acc[:].rearrange("p (l r) -> p r l", r=d)
                    if d == 4:
                        return av[:, bt, :], psum[:]
                    return av[:, 4 * bt:4 * bt + 4, :], psum[:].rearrange("p (i l) -> p i l", i=4)
                for acc, psum, nm in ((accU[ai], ups[bi], "U"), (accD[ai], dps[bi], "D")):
                    av, pv = views(acc, psum)
                    pkey = ("ups%d" if nm == "U" else "dps%d") % bi
                    akey = "acc%s%d" % (nm, ai)
                    if g == 0:
                        P.op("dve", lambda e, av=av, pv=pv: e.tensor_copy(av, pv), reads=[pkey], writes=[akey])
                    else:
                        P.op("dve", lambda e, av=av, pv=pv: e.tensor_tensor(av, av, pv, ALU.add), reads=[pkey, akey], writes=[akey])
        P.op("dve", lambda e, ai=ai: e.reciprocal(rec[:], accD[ai][:]), reads=["accD%d" % ai], writes=["rec"])
        P.op("pool", lambda e, ai=ai: e.tensor_tensor(osb[ai][:], accU[ai][:], rec[:], ALU.mult),
             reads=["accU%d" % ai, "rec"], writes=["osb%d" % ai])
        P.op("sp", lambda e, ai=ai, h=h: e.dma_start(out=o_d[h * 128:(h + 1) * 128, :], in_=osb[ai][:]),
             reads=["osb%d" % ai], dma=True)
    return P.finish()


def attn_host_layout(qkv):
    maps = []
    for c in range(NCORES):
        qT, kT, v = qkv[c]
        first = (c % 4 == 0)
        kext = np.zeros((3, 16, 128, 2 * TOK), dtype=kT.dtype)
        kext[..., TOK:] = kT
        vext = np.zeros((3, 16, 128, 32, 128), dtype=v.dtype)
        for g in range(3):
            d = DIL[g]
            J = TOK // (128 * d)
            vg = v[g].reshape(16, 128, d, J, 128)
            ve = vext[g, :, :, :d * (J + 1), :].reshape(16, 128, d, J + 1, 128)
            ve[:, :, :, 1:, :] = vg
            if not first:
                pk, pv = qkv[c - 1][1], qkv[c - 1][2]
                kext[g, :, :, TOK - 128 * d:TOK] = pk[g, :, :, TOK - 128 * d:]
                ve[:, :, :, 0, :] = pv[g].reshape(16, 128, d, J, 128)[:, :, :, J - 1, :]
            vext[g, :, :, :d * (J + 1), :] = ve.reshape(16, 128, d * (J + 1), 128)
        maps.append({"qT": qT, "kT": kext, "v": vext, "bias": attn_bias_tables(first)})
    return maps


def run_attn_stage(qkv):
    import time
    t0 = time.time()
    if "attn" not in _cache:
        _cache["attn"] = build_attn_stage()
    t1 = time.time()
    maps = attn_host_layout(qkv)
    t2 = time.time()
    res = _run(_cache["attn"], maps)
    print("attn stage: build %.1f layout %.1f run %.1f" % (t1 - t0, t2 - t1, time.time() - t2), flush=True)
    return [r["oT"] for r in res.results]


def build_normproj_stage(n_out):
    P = Prog()
    xT_d = P.din("xT", [D, TOK])
    w_d = P.din("w", [D, n_out])
    g_d = P.din("gv", [128, KC])
    o_d = P.dout("uT", [n_out, TOK])
    C = Common(P)
    g_sb = P.sb("g_sb", [128, KC], F32)
    P.op("sp", lambda e: e.dma_start(out=g_sb[:], in_=g_d), writes=["gvec"], dma=True)
    xq = [P.sb("xq%d" % i, [128, KC, 256], F32) for i in range(2)]
    xn = P.sb("xn", [128, KC, TOK], BF16)
    load_norm_full(P, C, xT_d, g_sb, 0, xn, xq, 256)
    wb = [P.sb("wb%d" % i, [128, KC, 512], BF16) for i in range(2)]
    gp = [P.ps("gp%d" % i, [128, 512]) for i in range(4)]
    ost = [P.sb("ost%d" % i, [128, TOK], F32) for i in range(2)]
    wv = w_d.rearrange("(kc p) f -> p kc f", p=128)
    cnt = dict(gp=0, ev=0, o=0)
    for nt in range(n_out // 512):
        wi = nt % 2
        P.op("pool", lambda e, wi=wi, nt=nt: e.dma_start(out=wb[wi][:], in_=wv[:, :, nt * 512:(nt + 1) * 512]),
             writes=["wb%d" % wi], dma=True)
        for hh in range(4):
            n = nt * 4 + hh
            oi = cnt["o"] % 2
            cnt["o"] += 1
            for tq in range(4):
                tsl = slice(tq * 512, (tq + 1) * 512)
                pi = cnt["gp"] % 4
                cnt["gp"] += 1
                for kc in range(KC):
                    P.op("pe", lambda e, pi=pi, wi=wi, kc=kc, hh=hh, tsl=tsl: e.matmul(
                        gp[pi][:], wb[wi][:, kc, hh * 128:(hh + 1) * 128], xn[:, kc, tsl],
                        start=(kc == 0), stop=(kc == KC - 1)),
                        reads=["wb%d" % wi, "xn_%d" % kc], writes=["gp%d" % pi])
                if cnt["ev"] % 2 == 0:
                    P.op("act", lambda e, pi=pi, oi=oi, tsl=tsl: e.copy(ost[oi][:, tsl], gp[pi][:]),
                         reads=["gp%d" % pi], writes=["ost%d_%d" % (oi, tq)])
                else:
                    P.op("dve", lambda e, pi=pi, oi=oi, tsl=tsl: e.tensor_copy(ost[oi][:, tsl], gp[pi][:]),
                         reads=["gp%d" % pi], writes=["ost%d_%d" % (oi, tq)])
                cnt["ev"] += 1
            P.op("sp", lambda e, oi=oi, n=n: e.dma_start(out=o_d[n * 128:(n + 1) * 128, :], in_=ost[oi][:]),
                 reads=["ost%d_%d" % (oi, x) for x in range(4)], dma=True)
    return P.finish()


def run_normproj_stage(xT_cores, w, g_mix):
    key = ("normproj", w.shape[1])
    if key not in _cache:
        _cache[key] = build_normproj_stage(w.shape[1])
    gv = gvec_layout(g_mix)
    in_maps = [{"xT": xT_cores[c], "w": w, "gv": gv} for c in range(NCORES)]
    res = _run(_cache[key], in_maps)
    return [r["uT"] for r in res.results]


SEQ = 8192
S5_NT = 512
MAGIC = 12582912.0
TWO_PI = 6.283185307179586
C1 = 6.28125
C2 = TWO_PI - C1


def build_s5_stage():
    P = Prog()
    NT = S5_NT
    NTILE = SEQ // NT
    u_d = P.din("u", [512, SEQ])
    are_d = P.din("are", [128, 16])
    aim_d = P.din("aim", [128, 16])
    ldt_d = P.din("ldt", [128, 16])
    bre_d = P.din("bre", [128, 16, 128])
    bim_d = P.din("bim", [128, 16, 128])
    cre_d = P.din("cre", [128, 16, 128])
    cim_d = P.din("cim", [128, 16, 128])
    dv_d = P.din("dv", [128, 4])
    al_d = P.din("aloc", [128, NT])
    bl_d = P.din("bloc", [128, NT])
    y_d = P.dout("yT", [512, SEQ], BF16)

    def small(name, shape=(128, 16)):
        return P.sb(name, list(shape), F32)

    are, aim, ldt = small("are"), small("aim"), small("ldt")
    bre = P.sb("bre", [128, 16, 128], F32)
    bim = P.sb("bim", [128, 16, 128], F32)
    cre = P.sb("cre", [128, 16, 128], F32)
    cim = P.sb("cim", [128, 16, 128], F32)
    dv = small("dv", (128, 4))
    aloc = P.sb("aloc", [128, NT], F32)
    bloc = P.sb("bloc", [128, NT], F32)
    for t, dd, k in ((are, are_d, "are"), (aim, aim_d, "aim"), (ldt, ldt_d, "ldt"), (bre, bre_d, "bre"), (bim, bim_d, "bim"),
                     (cre, cre_d, "cre"), (cim, cim_d, "cim"), (dv, dv_d, "dv"), (aloc, al_d, "aloc"), (bloc, bl_d, "bloc")):
        P.op("sp", lambda e, t=t, dd=dd: e.dma_start(out=t[:], in_=dd), writes=[k], dma=True)

    names = ["dt", "th", "lr", "m", "tmp", "n", "r1", "thr", "s", "sh", "sq", "cs", "abr", "abi", "den", "rden", "zr",
             "t1", "t2", "cr", "ci", "ncr", "phi", "phr", "nci"]
    T = {n: small("p_" + n) for n in names}

    def dve(fn, reads, writes):
        P.op("dve", fn, reads=reads, writes=writes)

    def act(fn, reads, writes):
        P.op("act", fn, reads=reads, writes=writes)

    act(lambda e: e.activation(T["dt"][:], ldt[:], AF.Exp), ["ldt"], ["dt"])
    dve(lambda e: e.tensor_tensor(T["th"][:], T["dt"][:], aim[:], ALU.mult), ["dt", "aim"], ["th"])
    dve(lambda e: e.tensor_tensor(T["lr"][:], T["dt"][:], are[:], ALU.mult), ["dt", "are"], ["lr"])
    act(lambda e: e.activation(T["m"][:], T["lr"][:], AF.Exp), ["lr"], ["m"])

    def reduce_angle(src, dst, ksrc, kdst):
        dve(lambda e: e.tensor_scalar(T["tmp"][:], src[:], 1.0 / TWO_PI, MAGIC, ALU.mult, ALU.add), [ksrc], ["tmp"])
        dve(lambda e: e.tensor_scalar(T["n"][:], T["tmp"][:], MAGIC, None, ALU.subtract), ["tmp"], ["n"])
        dve(lambda e: e.scalar_tensor_tensor(T["r1"][:], T["n"][:], -C1, src[:], ALU.mult, ALU.add), ["n", ksrc], ["r1"])
        dve(lambda e: e.scalar_tensor_tensor(dst[:], T["n"][:], -C2, T["r1"][:], ALU.mult, ALU.add), ["n", "r1"], [kdst])

    reduce_angle(T["th"], T["thr"], "th", "thr")
    act(lambda e: e.activation(T["s"][:], T["thr"][:], AF.Sin), ["thr"], ["s"])
    act(lambda e: e.activation(T["sh"][:], T["thr"][:], AF.Sin, scale=0.5), ["thr"], ["sh"])
    act(lambda e: e.activation(T["sq"][:], T["sh"][:], AF.Square), ["sh"], ["sq"])
    act(lambda e: e.activation(T["cs"][:], T["sq"][:], AF.Identity, bias=1.0, scale=-2.0), ["sq"], ["cs"])
    dve(lambda e: e.tensor_tensor(T["abr"][:], T["m"][:], T["cs"][:], ALU.mult), ["m", "cs"], ["abr"])
    dve(lambda e: e.tensor_tensor(T["abi"][:], T["m"][:], T["s"][:], ALU.mult), ["m", "s"], ["abi"])
    dve(lambda e: e.tensor_tensor(T["t1"][:], are[:], are[:], ALU.mult), ["are"], ["t1"])
    dve(lambda e: e.tensor_tensor(T["t2"][:], aim[:], aim[:], ALU.mult), ["aim"], ["t2"])
    dve(lambda e: e.tensor_tensor(T["den"][:], T["t1"][:], T["t2"][:], ALU.add), ["t1", "t2"], ["den"])
    dve(lambda e: e.reciprocal(T["rden"][:], T["den"][:]), ["den"], ["rden"])
    dve(lambda e: e.tensor_scalar(T["zr"][:], T["abr"][:], -1.0, None, ALU.add), ["abr"], ["zr"])
    dve(lambda e: e.tensor_tensor(T["t1"][:], T["zr"][:], are[:], ALU.mult), ["zr", "are"], ["t1"])
    dve(lambda e: e.tensor_tensor(T["t2"][:], T["abi"][:], aim[:], ALU.mult), ["abi", "aim"], ["t2"])
    dve(lambda e: e.tensor_tensor(T["cr"][:], T["t1"][:], T["t2"][:], ALU.add), ["t1", "t2"], ["cr0"])
    dve(lambda e: e.tensor_tensor(T["cr"][:], T["cr"][:], T["rden"][:], ALU.mult), ["cr0", "rden"], ["cr"])
    dve(lambda e: e.tensor_tensor(T["t1"][:], T["abi"][:], are[:], ALU.mult), ["abi", "are", "cr0"], ["t1"])
    dve(lambda e: e.tensor_tensor(T["t2"][:], T["zr"][:], aim[:], ALU.mult), ["zr", "aim", "cr0"], ["t2"])
    dve(lambda e: e.tensor_tensor(T["ci"][:], T["t1"][:], T["t2"][:], ALU.subtract), ["t1", "t2"], ["ci0"])
    dve(lambda e: e.tensor_tensor(T["ci"][:], T["ci"][:], T["rden"][:], ALU.mult), ["ci0", "rden"], ["ci"])
    dve(lambda e: e.tensor_scalar(T["ncr"][:], T["cr"][:], -1.0, None, ALU.mult), ["cr"], ["ncr"])
    dve(lambda e: e.tensor_scalar(T["nci"][:], T["ci"][:], -1.0, None, ALU.mult), ["ci"], ["nci"])
    dve(lambda e: e.tensor_scalar(T["phi"][:], T["thr"][:], 64.0, None, ALU.mult), ["thr"], ["phi"])
    reduce_angle(T["phi"], T["phr"], "phi", "phr")
    off = P.sb("off", [128, NTILE, 16], F32)
    for tt in range(NTILE):
        dve(lambda e, tt=tt: e.tensor_scalar(off[:, tt, :], T["phr"][:], float(tt * (NT // 64)), None, ALU.mult), ["phr"], ["off"])
    ctr, cti = cre, cim
    nctr = P.sb("nctr", [128, 16, 128], F32)
    ctmp = P.sb("ctmp", [128, 128], F32)
    ctA = P.sb("ctA", [128, 128], F32)
    ctB = P.sb("ctB", [128, 128], F32)
    for pr in range(16):
        dve(lambda e, pr=pr: e.tensor_scalar(ctmp[:], cim[:, pr, :], T["nci"][:, pr:pr + 1], None, ALU.mult), ["cim", "nci"], ["ctmp"])
        dve(lambda e, pr=pr: e.scalar_tensor_tensor(ctA[:], cre[:, pr, :], T["cr"][:, pr:pr + 1], ctmp[:], ALU.mult, ALU.add),
            ["cre", "cr", "ctmp"], ["ctA"])
        dve(lambda e, pr=pr: e.tensor_scalar(ctmp[:], cim[:, pr, :], T["ncr"][:, pr:pr + 1], None, ALU.mult), ["cim", "ncr", "ctA"], ["ctmp"])
        dve(lambda e, pr=pr: e.scalar_tensor_tensor(ctB[:], cre[:, pr, :], T["nci"][:, pr:pr + 1], ctmp[:], ALU.mult, ALU.add),
            ["cre", "nci", "ctmp"], ["ctB"])
        dve(lambda e, pr=pr: e.tensor_copy(cre[:, pr, :], ctA[:]), ["ctA", "ctB"], ["cre"])
        dve(lambda e, pr=pr: e.tensor_copy(cim[:, pr, :], ctB[:]), ["ctB"], ["cim"])
        dve(lambda e, pr=pr: e.tensor_scalar(nctr[:, pr, :], ctA[:], -1.0, None, ALU.mult), ["ctA"], ["nctr"])
    for n_ in ("dl", "dlr", "sD", "shD", "sqD", "cD"):
        T[n_] = small("p_" + n_)
    dve(lambda e: e.tensor_scalar(T["dl"][:], T["phr"][:], float(NT // 64), None, ALU.mult), ["phr"], ["dl"])
    reduce_angle(T["dl"], T["dlr"], "dl", "dlr")
    act(lambda e: e.activation(T["sD"][:], T["dlr"][:], AF.Sin), ["dlr"], ["sD"])
    act(lambda e: e.activation(T["shD"][:], T["dlr"][:], AF.Sin, scale=0.5), ["dlr"], ["shD"])
    act(lambda e: e.activation(T["sqD"][:], T["shD"][:], AF.Square), ["shD"], ["sqD"])
    act(lambda e: e.activation(T["cD"][:], T["sqD"][:], AF.Identity, bias=1.0, scale=-2.0), ["sqD"], ["cD"])

    carry = P.sb("carry", [128, 16, 2], F32)
    dve(lambda e: e.memset(carry[:], 0.0), [], ["carry%d" % i for i in range(16)])

    uch = P.sb("uch", [128, SEQ], F32)

    def big(name):
        return P.sb(name, [128, NT], F32)

    SC = {n: big("c_" + n) for n in ("base", "t1b", "tmpb", "nb", "r1b", "red", "shb", "sqb", "p1", "p2", "p3", "p4", "zhr", "zhi")}
    tabc = [[big("tabc_%d_%d" % (pq, i)) for i in range(2)] for pq in range(4)]
    tabs = [[big("tabs_%d_%d" % (pq, i)) for i in range(2)] for pq in range(4)]
    PB = {n: [big("%s_%d" % (n, i)) for i in range(2)] for n in ("wr", "wi", "q1", "q2", "q3", "q4", "t1", "t2")}
    zps = [P.ps("zps%d" % i, [128, 512]) for i in range(4)]
    yps = [P.ps("yps%d" % i, [128, 512]) for i in range(4)]
    gl = {n: P.sb("gl_" + n, [128, 512], F32) for n in ("ypre", "sq", "t", "inner", "sg")}
    ysb = [P.sb("ysb%d" % i, [128, NT], BF16) for i in range(2)]
    assert NT == 512
    cnt = dict(z=0, it=0, y=0, o=0)
    GC = 2.0 * 0.7978845608028654
    for ch in range(4):
        for q in range(4):
            P.op("sp", lambda e, ch=ch, q=q: e.dma_start(out=uch[:, q * 2048:(q + 1) * 2048],
                                                         in_=u_d[ch * 128:(ch + 1) * 128, q * 2048:(q + 1) * 2048]),
                 writes=["uch%d" % q], dma=True)
        for tt in range(NTILE):
            tsl0 = tt * NT
            usl = slice(tsl0, tsl0 + NT)
            ukey = "uch%d" % (tsl0 // 2048)
            yb = cnt["y"] % 4
            cnt["y"] += 1
            tp = tt % 2
            for pq in range(4):
                pr = ch * 4 + pq
                it = cnt["it"] % 2
                cnt["it"] += 1
                b = {n: PB[n][it] for n in PB}
                k = {n: "%s_%d" % (n, it) for n in PB}
                cs_, sn_ = tabc[pq][tp], tabs[pq][tp]
                kc_, ks_ = "tabc_%d_%d" % (pq, tp), "tabs_%d_%d" % (pq, tp)
                if tt == 0:
                    dve(lambda e, pr=pr: e.tensor_scalar(SC["t1b"][:], bloc[:], T["thr"][:, pr:pr + 1], None, ALU.mult), ["bloc", "thr"], ["c_t1b"])
                    dve(lambda e, pr=pr: e.scalar_tensor_tensor(SC["base"][:], aloc[:], T["phr"][:, pr:pr + 1], SC["t1b"][:], ALU.mult, ALU.add),
                        ["aloc", "phr", "c_t1b"], ["c_base"])
                    dve(lambda e: e.tensor_scalar(SC["tmpb"][:], SC["base"][:], 1.0 / TWO_PI, MAGIC, ALU.mult, ALU.add), ["c_base"], ["c_tmpb"])
                    dve(lambda e: e.tensor_scalar(SC["nb"][:], SC["tmpb"][:], MAGIC, None, ALU.subtract), ["c_tmpb"], ["c_nb"])
                    dve(lambda e: e.scalar_tensor_tensor(SC["r1b"][:], SC["nb"][:], -C1, SC["base"][:], ALU.mult, ALU.add), ["c_nb", "c_base"], ["c_r1b"])
                    dve(lambda e: e.scalar_tensor_tensor(SC["red"][:], SC["nb"][:], -C2, SC["r1b"][:], ALU.mult, ALU.add), ["c_nb", "c_r1b"], ["c_red"])
                    act(lambda e, sn_=sn_: e.activation(sn_[:], SC["red"][:], AF.Sin), ["c_red"], [ks_])
                    act(lambda e: e.activation(SC["shb"][:], SC["red"][:], AF.Sin, scale=0.5), ["c_red"], ["c_shb"])
                    act(lambda e: e.activation(SC["sqb"][:], SC["shb"][:], AF.Square), ["c_shb"], ["c_sqb"])
                    act(lambda e, cs_=cs_: e.activation(cs_[:], SC["sqb"][:], AF.Identity, bias=1.0, scale=-2.0), ["c_sqb"], [kc_])
                else:
                    co, so = tabc[pq][1 - tp], tabs[pq][1 - tp]
                    kco, kso = "tabc_%d_%d" % (pq, 1 - tp), "tabs_%d_%d" % (pq, 1 - tp)
                    act(lambda e, b=b, so=so, pr=pr: e.activation(b["t1"][:], so[:], AF.Copy, scale=T["sD"][:, pr:pr + 1]), [kso, "sD"], [k["t1"]])
                    act(lambda e, b=b, co=co, pr=pr: e.activation(b["t2"][:], co[:], AF.Copy, scale=T["sD"][:, pr:pr + 1]), [kco, "sD"], [k["t2"]])
                    dve(lambda e, b=b, co=co, cs_=cs_, pr=pr: e.scalar_tensor_tensor(cs_[:], co[:], T["cD"][:, pr:pr + 1], b["t1"][:], ALU.mult, ALU.subtract),
                        [kco, "cD", k["t1"]], [kc_])
                    dve(lambda e, b=b, so=so, sn_=sn_, pr=pr: e.scalar_tensor_tensor(sn_[:], so[:], T["cD"][:, pr:pr + 1], b["t2"][:], ALU.mult, ALU.add),
                        [kso, "cD", k["t2"]], [ks_])
                zr_i = cnt["z"] % 4
                zi_i = (cnt["z"] + 1) % 4
                cnt["z"] += 2
                P.op("pe", lambda e, zr_i=zr_i, pr=pr, usl=usl: e.matmul(zps[zr_i][:], bre[:, pr, :], uch[:, usl], start=True, stop=True),
                     reads=["bre", ukey], writes=["zps%d" % zr_i])
                P.op("pe", lambda e, zi_i=zi_i, pr=pr, usl=usl: e.matmul(zps[zi_i][:], bim[:, pr, :], uch[:, usl], start=True, stop=True),
                     reads=["bim", ukey], writes=["zps%d" % zi_i])
                kzr, kzi = "zps%d" % zr_i, "zps%d" % zi_i
                dve(lambda e, zr_i=zr_i, cs_=cs_: e.tensor_tensor(SC["p1"][:], zps[zr_i][:], cs_[:], ALU.mult), [kzr, kc_], ["c_p1", kzr])
                dve(lambda e, zi_i=zi_i, sn_=sn_: e.tensor_tensor(SC["p2"][:], zps[zi_i][:], sn_[:], ALU.mult), [kzi, ks_], ["c_p2", kzi])
                dve(lambda e: e.tensor_tensor(SC["zhr"][:], SC["p1"][:], SC["p2"][:], ALU.add), ["c_p1", "c_p2"], ["c_zhr"])
                dve(lambda e, zi_i=zi_i, cs_=cs_: e.tensor_tensor(SC["p3"][:], zps[zi_i][:], cs_[:], ALU.mult), [kzi, kc_], ["c_p3", kzi])
                dve(lambda e, zr_i=zr_i, sn_=sn_: e.tensor_tensor(SC["p4"][:], zps[zr_i][:], sn_[:], ALU.mult), [kzr, ks_], ["c_p4", kzr])
                dve(lambda e: e.tensor_tensor(SC["zhi"][:], SC["p3"][:], SC["p4"][:], ALU.subtract), ["c_p3", "c_p4"], ["c_zhi"])
                mcol = T["m"][:, pr:pr + 1].to_broadcast([128, NT])
                dve(lambda e, pr=pr, mcol=mcol, b=b: e.tensor_tensor_scan(b["wr"][:], mcol, SC["zhr"][:], carry[:, pr, 0:1], ALU.mult, ALU.add),
                    ["m", "c_zhr", "carry%d" % pr], [k["wr"]])
                dve(lambda e, pr=pr, mcol=mcol, b=b: e.tensor_tensor_scan(b["wi"][:], mcol, SC["zhi"][:], carry[:, pr, 1:2], ALU.mult, ALU.add),
                    ["m", "c_zhi", "carry%d" % pr], [k["wi"]])
                act(lambda e, pr=pr, b=b: e.copy(carry[:, pr, 0:1], b["wr"][:, NT - 1:NT]), [k["wr"]], ["carry%d" % pr])
                act(lambda e, pr=pr, b=b: e.copy(carry[:, pr, 1:2], b["wi"][:, NT - 1:NT]), [k["wi"]], ["carry%d" % pr])
                dve(lambda e, b=b, cs_=cs_: e.tensor_tensor(b["q1"][:], b["wr"][:], cs_[:], ALU.mult), [k["wr"], kc_], [k["q1"]])
                dve(lambda e, b=b, sn_=sn_: e.tensor_tensor(b["q2"][:], b["wi"][:], sn_[:], ALU.mult), [k["wi"], ks_], [k["q2"]])
                dve(lambda e, b=b, cs_=cs_: e.tensor_tensor(b["q3"][:], b["wi"][:], cs_[:], ALU.mult), [k["wi"], kc_], [k["q3"]])
                dve(lambda e, b=b, sn_=sn_: e.tensor_tensor(b["q4"][:], b["wr"][:], sn_[:], ALU.mult), [k["wr"], ks_], [k["q4"]])
                for qi, (ct, qn) in enumerate(((ctr, "q1"), (nctr, "q2"), (cti, "q3"), (cti, "q4"))):
                    P.op("pe", lambda e, ct=ct, qn=qn, pr=pr, b=b, yb=yb, qi=qi, pq=pq: e.matmul(
                        yps[yb][:], ct[:, pr, :], b[qn][:], start=(pq == 0 and qi == 0), stop=(pq == 3 and qi == 3)),
                        reads=["cre", "cim", "nctr", k[qn]], writes=["yps%d" % yb])
            oi = cnt["o"] % 2
            cnt["o"] += 1
            dve(lambda e, usl=usl, ch=ch, yb=yb: e.scalar_tensor_tensor(gl["ypre"][:], uch[:, usl], dv[:, ch:ch + 1], yps[yb][:], ALU.mult, ALU.add),
                [ukey, "dv", "yps%d" % yb], ["g_ypre", "yps%d" % yb])
            act(lambda e: e.activation(gl["sq"][:], gl["ypre"][:], AF.Square), ["g_ypre"], ["g_sq"])
            act(lambda e: e.activation(gl["t"][:], gl["sq"][:], AF.Identity, bias=1.0, scale=0.044715), ["g_sq"], ["g_t"])
            dve(lambda e: e.tensor_tensor(gl["inner"][:], gl["t"][:], gl["ypre"][:], ALU.mult), ["g_t", "g_ypre"], ["g_inner"])
            act(lambda e: e.activation(gl["sg"][:], gl["inner"][:], AF.Sigmoid, scale=GC), ["g_inner"], ["g_sg"])
            dve(lambda e, oi=oi: e.tensor_tensor(ysb[oi][:], gl["ypre"][:], gl["sg"][:], ALU.mult), ["g_ypre", "g_sg"], ["ysb%d" % oi])
            P.op("sp", lambda e, oi=oi, ch=ch, tsl0=tsl0: e.dma_start(out=y_d[ch * 128:(ch + 1) * 128, tsl0:tsl0 + NT], in_=ysb[oi][:]),
                 reads=["ysb%d" % oi], dma=True)
    return P.finish()


def s5_host_layout(uT_cores, log_dt, a_re, a_im, b_re, b_im, c_re, c_im, dskip):
    NT = S5_NT
    loc = np.arange(NT)
    aloc = np.broadcast_to((loc // 64).astype(np.float32), (128, NT)).copy()
    bloc = np.broadcast_to((loc % 64).astype(np.float32), (128, NT)).copy()
    maps = []
    for c in range(NCORES):
        seq, gq = divmod(c, 4)
        u = np.concatenate([uT_cores[4 * seq + i][512 * gq:512 * gq + 512, :] for i in range(4)], axis=1)
        gs = np.arange(32 * gq, 32 * gq + 32).reshape(16, 2)
        are = a_re[gs].transpose(1, 2, 0).reshape(128, 16)
        aim = a_im[gs].transpose(1, 2, 0).reshape(128, 16)
        ldt = np.repeat(log_dt[gs].transpose(1, 0)[:, None, :], 64, axis=1).reshape(128, 16)
        bre = np.zeros((128, 16, 128), np.float32)
        bim = np.zeros((128, 16, 128), np.float32)
        cre = np.zeros((128, 16, 128), np.float32)
        cim = np.zeros((128, 16, 128), np.float32)
        for pr in range(16):
            for g2 in range(2):
                g = gs[pr, g2]
                r0 = (pr % 4) * 32 + g2 * 16
                bre[r0:r0 + 16, pr, g2 * 64:(g2 + 1) * 64] = b_re[g].T
                bim[r0:r0 + 16, pr, g2 * 64:(g2 + 1) * 64] = b_im[g].T
                cre[g2 * 64:(g2 + 1) * 64, pr, r0:r0 + 16] = c_re[g].T
                cim[g2 * 64:(g2 + 1) * 64, pr, r0:r0 + 16] = c_im[g].T
        dv = dskip[32 * gq:32 * gq + 32].reshape(4, 128).T
        maps.append(dict(u=np.ascontiguousarray(u), are=np.ascontiguousarray(are), aim=np.ascontiguousarray(aim),
                         ldt=np.ascontiguousarray(ldt), bre=bre, bim=bim, cre=cre, cim=cim,
                         dv=np.ascontiguousarray(dv), aloc=aloc, bloc=bloc))
    return maps


def run_s5_stage(uT_cores, log_dt, a_re, a_im, b_re, b_im, c_re, c_im, dskip):
    if "s5" not in _cache:
        _cache["s5"] = build_s5_stage()
    maps = s5_host_layout(uT_cores, log_dt, a_re, a_im, b_re, b_im, c_re, c_im, dskip)
    res = _run(_cache["s5"], maps)
    ys = [r["yT"] for r in res.results]
    out = []
    for c in range(NCORES):
        seq, i = divmod(c, 4)
        out.append(np.ascontiguousarray(np.concatenate([ys[4 * seq + gq][:, i * TOK:(i + 1) * TOK] for gq in range(4)], axis=0)))
    return out


def build_rwkv_proj_stage():
    P = Prog()
    TT = 1024
    PW = 128
    xT_d = P.din("xT", [D, TOK])
    xp_d = P.din("xprev", [D, 1])
    g_d = P.din("gv", [128, KC])
    mu_d = P.din("mu", [128, 6 * KC])
    wrkv_d = P.din("wrkv", [3, D, D])
    w1_d = P.din("w1", [D, 96])
    w2_d = P.din("w2", [96, D])
    a1_d = P.din("a1", [D, 96])
    a2_d = P.din("a2", [96, D])
    g1_d = P.din("g1", [D, 256])
    g2_d = P.din("g2", [256, D])
    w0_d = P.din("w0", [128, KC])
    a0_d = P.din("a0", [128, KC])
    outs = {n: P.dout(n, [D, TOK]) for n in ("r", "k", "v", "ld", "a", "g")}
    C = Common(P)
    g_sb = P.sb("g_sb", [128, KC], F32)
    mu_sb = P.sb("mu_sb", [128, 6 * KC], F32)
    w0_sb = P.sb("w0_sb", [128, KC], F32)
    a0_sb = P.sb("a0_sb", [128, KC], F32)
    P.op("sp", lambda e: e.dma_start(out=g_sb[:], in_=g_d), writes=["gvec"], dma=True)
    P.op("sp", lambda e: e.dma_start(out=mu_sb[:], in_=mu_d), writes=["mu"], dma=True)
    P.op("sp", lambda e: e.dma_start(out=w0_sb[:], in_=w0_d), writes=["w0"], dma=True)
    P.op("sp", lambda e: e.dma_start(out=a0_sb[:], in_=a0_d), writes=["a0"], dma=True)
    w1_sb = P.sb("w1_sb", [128, KC, 96], BF16)
    a1_sb = P.sb("a1_sb", [128, KC, 96], BF16)
    g1_sb = P.sb("g1_sb", [128, KC, 256], BF16)
    w2_sb = P.sb("w2_sb", [96, D], BF16)
    a2_sb = P.sb("a2_sb", [96, D], BF16)
    g2_sb = P.sb("g2_sb", [128, 2, D], BF16)
    P.op("pool", lambda e: e.dma_start(out=w1_sb[:], in_=w1_d.rearrange("(kc p) f -> p kc f", p=128)), writes=["w1"], dma=True)
    P.op("pool", lambda e: e.dma_start(out=a1_sb[:], in_=a1_d.rearrange("(kc p) f -> p kc f", p=128)), writes=["a1"], dma=True)
    P.op("pool", lambda e: e.dma_start(out=g1_sb[:], in_=g1_d.rearrange("(kc p) f -> p kc f", p=128)), writes=["g1"], dma=True)
    P.op("pool", lambda e: e.dma_start(out=w2_sb[:], in_=w2_d), writes=["w2"], dma=True)
    P.op("pool", lambda e: e.dma_start(out=a2_sb[:], in_=a2_d), writes=["a2"], dma=True)
    P.op("pool", lambda e: e.dma_start(out=g2_sb[:], in_=g2_d.rearrange("(c p) f -> p c f", p=128)), writes=["g2"], dma=True)

    xq = [P.sb("xq%d" % i, [128, KC, PW + 1], F32) for i in range(2)]
    hf = P.sb("hf", [128, KC, PW + 1], F32)
    h_sb = P.sb("h_sb", [128, KC, TT], BF16)
    xx_sb = P.sb("xx_sb", [128, KC, TT], BF16)
    xj = [P.sb("xj0", [128, KC, TT], BF16)]
    wb = [P.sb("wb%d" % i, [128, KC, 512], BF16) for i in range(2)]
    gp = [P.ps("gp%d" % i, [128, 512]) for i in range(4)]
    lp = [P.ps("lp%d" % i, [128, 512]) for i in range(2)]
    ost = [P.sb("ost%d" % i, [128, TT], F32) for i in range(2)]
    lo = [P.sb("lo0", [128, 2, TT], BF16)]
    xv = xT_d.rearrange("(kc p) t -> p kc t", p=128)
    xpv = xp_d.rearrange("(kc p) t -> p kc t", p=128)
    cnt = dict(x=0, gp=0, lp=0, ev=0, o=0, w=0, j=0, lo=0)
    NPC = TT // PW
    for ps_ in range(TOK // TT):
        t0 = ps_ * TT
        for pc in range(NPC):
            tp = t0 + pc * PW
            b = cnt["x"] % 2
            cnt["x"] += 1
            keys = ["xq%d_%d" % (b, k) for k in range(KC)]
            if tp == 0:
                P.op("sp", lambda e, b=b: e.dma_start(out=xq[b][:, :, 0:1], in_=xpv, allow_slow_non_contiguous=True), writes=keys, dma=True)
                P.op("sp", lambda e, b=b: e.dma_start(out=xq[b][:, :, 1:], in_=xv[:, :, 0:PW]), writes=keys, dma=True)
            else:
                P.op("sp", lambda e, b=b, tp=tp: e.dma_start(out=xq[b][:], in_=xv[:, :, tp - 1:tp + PW]), writes=keys, dma=True)
            N1 = PW + 1
            for kc in range(KC):
                j = C.nsq % 2
                C.nsq += 1
                P.op("act", lambda e, j=j, kc=kc, b=b: e.activation(C.sq[j][:, :N1], xq[b][:, kc, :], AF.Square),
                     reads=[keys[kc]], writes=["sq%d" % j])
                P.op("pe", lambda e, j=j, kc=kc: e.matmul(C.stat[:, :N1], C.ones[:], C.sq[j][:, :N1], start=(kc == 0), stop=(kc == KC - 1)),
                     reads=["ones", "sq%d" % j], writes=["stat"])
            P.op("act", lambda e: e.activation(C.std[:, :N1], C.stat[:, :N1], AF.Sqrt, bias=EPS, scale=1.0 / D), reads=["stat"], writes=["std"])
            P.op("dve", lambda e: e.reciprocal(C.rstd[:, :N1], C.std[:, :N1]), reads=["std"], writes=["rstd"])
            psl = slice(pc * PW, (pc + 1) * PW)
            for kc in range(KC):
                P.op("dve", lambda e, kc=kc, b=b: e.scalar_tensor_tensor(
                    hf[:, kc, :], xq[b][:, kc, :], g_sb[:, kc:kc + 1], C.rstd[:, :N1], ALU.mult, ALU.mult),
                    reads=[keys[kc], "rstd", "gvec"], writes=["hf_%d" % kc])
                P.op("act", lambda e, kc=kc, psl=psl: e.copy(h_sb[:, kc, psl], hf[:, kc, 1:]), reads=["hf_%d" % kc], writes=["h_%d" % kc])
                P.op("dve", lambda e, kc=kc, psl=psl: e.tensor_tensor(xx_sb[:, kc, psl], hf[:, kc, 0:PW], hf[:, kc, 1:], ALU.subtract),
                     reads=["hf_%d" % kc], writes=["xx_%d" % kc])

        def make_xj(j):
            ji = 0
            cnt["j"] += 1
            for kc in range(KC):
                P.op("dve", lambda e, kc=kc, ji=ji, j=j: e.scalar_tensor_tensor(
                    xj[ji][:, kc, :], xx_sb[:, kc, :], mu_sb[:, j * KC + kc:j * KC + kc + 1], h_sb[:, kc, :], ALU.mult, ALU.add),
                    reads=["xx_%d" % kc, "h_%d" % kc, "mu"], writes=["xj%d_%d" % (ji, kc)])
            return ji

        def big_proj(ji, w_dram, name, post):
            wv = w_dram.rearrange("(kc p) f -> p kc f", p=128)
            for nt in range(4):
                wi = cnt["w"] % 2
                cnt["w"] += 1
                P.op("pool", lambda e, wi=wi, nt=nt: e.dma_start(out=wb[wi][:], in_=wv[:, :, nt * 512:(nt + 1) * 512]),
                     writes=["wb%d" % wi], dma=True)
                for hh in range(4):
                    n = nt * 4 + hh
                    oi = cnt["o"] % 2
                    cnt["o"] += 1
                    for th in range(TT // 512):
                        tsl = slice(th * 512, (th + 1) * 512)
                        pi = cnt["gp"] % 4
                        cnt["gp"] += 1
                        for kc in range(KC):
                            P.op("pe", lambda e, pi=pi, wi=wi, kc=kc, hh=hh, tsl=tsl, ji=ji: e.matmul(
                                gp[pi][:], wb[wi][:, kc, hh * 128:(hh + 1) * 128], xj[ji][:, kc, tsl],
                                start=(kc == 0), stop=(kc == KC - 1)),
                                reads=["wb%d" % wi, "xj%d_%d" % (ji, kc)], writes=["gp%d" % pi])
                        post(gp[pi], "gp%d" % pi, ost[oi][:, tsl], "ost%d_%d" % (oi, th), n)
                    P.op("sp", lambda e, oi=oi, n=n, name=name, t0=t0: e.dma_start(out=outs[name][n * 128:(n + 1) * 128, t0:t0 + TT], in_=ost[oi][:]),
                         reads=["ost%d_%d" % (oi, x) for x in range(TT // 512)], dma=True)

        def post_copy(ps, pkey, dst, dkey, n):
            if cnt["ev"] % 2 == 0:
                P.op("act", lambda e: e.copy(dst, ps[:]), reads=[pkey], writes=[dkey])
            else:
                P.op("dve", lambda e: e.tensor_copy(dst, ps[:]), reads=[pkey], writes=[dkey])
            cnt["ev"] += 1

        for j, name in ((0, "r"), (1, "k"), (2, "v")):
            ji = make_xj(j)
            big_proj(ji, wrkv_d[j], name, post_copy)

        def lora(ji, l1_sb, l1key, nl, func, l2, l2key, name, post):
            li = 0
            cnt["lo"] += 1
            nlc = (nl + 127) // 128
            for c in range(nlc):
                m = min(128, nl - c * 128)
                for th in range(TT // 512):
                    tsl = slice(th * 512, (th + 1) * 512)
                    pi = cnt["lp"] % 2
                    cnt["lp"] += 1
                    for kc in range(KC):
                        P.op("pe", lambda e, pi=pi, kc=kc, c=c, m=m, tsl=tsl, ji=ji: e.matmul(
                            lp[pi][:m, :], l1_sb[:, kc, c * 128:c * 128 + m], xj[ji][:, kc, tsl],
                            start=(kc == 0), stop=(kc == KC - 1)),
                            reads=[l1key, "xj%d_%d" % (ji, kc)], writes=["lp%d" % pi])
                    P.op("act", lambda e, pi=pi, li=li, c=c, m=m, tsl=tsl: e.activation(lo[li][:m, c, tsl], lp[pi][:m, :], func),
                         reads=["lp%d" % pi], writes=["lo%d" % li])
            for n in range(KC):
                oi = cnt["o"] % 2
                cnt["o"] += 1
                for th in range(TT // 512):
                    tsl = slice(th * 512, (th + 1) * 512)
                    pi = cnt["gp"] % 4
                    cnt["gp"] += 1
                    for c in range(nlc):
                        m = min(128, nl - c * 128)
                        lhs = l2[:m, c, n * 128:(n + 1) * 128] if nlc > 1 else l2[:m, n * 128:(n + 1) * 128]
                        P.op("pe", lambda e, pi=pi, lhs=lhs, li=li, c=c, m=m, tsl=tsl: e.matmul(
                            gp[pi][:], lhs, lo[li][:m, c, tsl], start=(c == 0), stop=(c == nlc - 1)),
                            reads=[l2key, "lo%d" % li], writes=["gp%d" % pi])
                    post(gp[pi], "gp%d" % pi, ost[oi][:, tsl], "ost%d_%d" % (oi, th), n)
                P.op("sp", lambda e, oi=oi, n=n, name=name, t0=t0: e.dma_start(out=outs[name][n * 128:(n + 1) * 128, t0:t0 + TT], in_=ost[oi][:]),
                     reads=["ost%d_%d" % (oi, x) for x in range(TT // 512)], dma=True)

        def post_ld(ps, pkey, dst, dkey, n):
            P.op("act", lambda e: e.activation(dst, ps[:], AF.Sigmoid, bias=w0_sb[:, n:n + 1], scale=1.0), reads=[pkey, "w0"], writes=[dkey])
            P.op("dve", lambda e: e.tensor_scalar(dst, dst, -0.6065306597126334, None, ALU.mult), reads=[dkey], writes=[dkey])

        def post_a(ps, pkey, dst, dkey, n):
            P.op("act", lambda e: e.activation(dst, ps[:], AF.Sigmoid, bias=a0_sb[:, n:n + 1], scale=1.0), reads=[pkey, "a0"], writes=[dkey])

        ji = make_xj(3)
        lora(ji, w1_sb, "w1", 96, AF.Tanh, w2_sb, "w2", "ld", post_ld)
        ji = make_xj(4)
        lora(ji, a1_sb, "a1", 96, AF.Copy, a2_sb, "a2", "a", post_a)
        ji = make_xj(5)
        lora(ji, g1_sb, "g1", 256, AF.Sigmoid, g2_sb, "g2", "g", post_copy)
    return P.finish()


def run_rwkv_proj_stage(xT_cores, inp, L, ic):
    if "rwkvp" not in _cache:
        _cache["rwkvp"] = build_rwkv_proj_stage()
    gv = gvec_layout(inp["norm_mix"][L])
    mu = gvec_layout(*[inp["rwkv_mu"][ic][j] for j in range(6)])
    maps = []
    for c in range(NCORES):
        if c % 4 == 0:
            xprev = np.zeros((D, 1), np.float32)
        else:
            xprev = np.ascontiguousarray(xT_cores[c - 1][:, TOK - 1:TOK])
        maps.append(dict(xT=xT_cores[c], xprev=xprev, gv=gv, mu=mu, wrkv=inp["rwkv_w_rkv"][ic], w1=inp["rwkv_w1"][ic],
                         w2=inp["rwkv_w2"][ic], a1=inp["rwkv_a1"][ic], a2=inp["rwkv_a2"][ic], g1=inp["rwkv_g1"][ic],
                         g2=inp["rwkv_g2"][ic], w0=gvec_layout(inp["rwkv_w0"][ic]), a0=gvec_layout(inp["rwkv_a0"][ic])))
    res = _run(_cache["rwkvp"], maps)
    return [{n: r[n] for n in ("r", "k", "v", "ld", "a", "g")} for r in res.results]


RW_ST = 256
GN_EPS = 64 * 1e-5


def build_rwkv_core_stage():
    import os
    P = Prog()
    ST = RW_ST
    NST = SEQ // ST
    NCH = ST // 64
    ins_d = {n: P.din(n, [512, SEQ]) for n in ("r", "k", "v", "ld", "a", "g")}
    par_d = P.din("par", [128, 4, 5])
    id_d = P.din("ident", [128, 128])
    ob_d = P.din("onesbd", [128, 128])
    mk_d = P.din("mask320", [128, 320])
    is_d = P.din("identst", [128, 64])
    rm_d = P.din("resetm", [128, ST])
    o_d = P.dout("oT", [512, SEQ], BF16)
    par = P.sb("par", [128, 4, 5])
    ident = P.sb("ident", [128, 128])
    onesbd = P.sb("onesbd", [128, 128])
    mask = P.sb("mask320", [128, 320])
    identst = P.sb("identst", [128, 64])
    resetm = P.sb("resetm", [128, ST])
    for t, dd, k in ((par, par_d, "par"), (ident, id_d, "ident"), (onesbd, ob_d, "onesbd"), (mask, mk_d, "mask"),
                     (identst, is_d, "identst"), (resetm, rm_d, "resetm")):
        P.op("sp", lambda e, t=t, dd=dd: e.dma_start(out=t[:], in_=dd), writes=[k], dma=True)
    banks = [P.ps("bk%d" % i, [128, 512]) for i in range(8)]
    bkA = [banks[2 * hp] for hp in range(4)]
    bkB = [banks[2 * hp + 1] for hp in range(4)]
    Gps = [bkA[hp][:, 0:320] for hp in range(4)]
    Tps = [bkA[hp][:, 320:512] for hp in range(4)]
    sqp = [bkB[hp][:, 0:128] for hp in range(4)]
    Pps = [bkB[hp][:, 128:192] for hp in range(4)]
    Wps = [bkB[hp][:, 192:256] for hp in range(4)]
    Ups = [bkB[hp][:, 256:320] for hp in range(4)]
    Yps = [bkB[hp][:, 320:384] for hp in range(4)]
    Sps = [bkB[hp][:, 384:448] for hp in range(4)]
    Ytr = [bkB[hp][:, 448:512] for hp in range(4)]
    KA = ["bkA%d" % hp for hp in range(4)]
    KB = ["bkB%d" % hp for hp in range(4)]

    def T_(name, w, dt=F32):
        return P.sb(name, [128, w], dt)

    inb = {n: [[T_("in_%s_%d_%d" % (n, hp, b), ST) for b in range(2)] for hp in range(4)] for n in ("r", "k", "v", "ld", "a", "g")}
    prep = {n: [[T_("pp_%s_%d_%d" % (n, hp, b), ST) for b in range(2)] for hp in range(4)] for n in ("At", "Rt", "Bt", "Kt", "gam", "bonus")}
    S = [[T_("S_%d_%d" % (hp, b), 64) for b in range(2)] for hp in range(4)]
    for hp in range(4):
        P.op("dve", lambda e, hp=hp: e.memset(S[hp][0][:], 0.0), writes=["S%d_0" % hp])
    sc = {n: T_("sc_" + n, ST) for n in ("cs", "csm", "gm1", "ig", "kk", "kksq", "nrm", "rn", "kkn", "t", "kp", "bv", "rk",
                                          "y", "ysq", "mean", "msq", "var", "std", "rstd", "yc", "yn", "z", "z2")}
    osb = [T_("osb%d" % i, ST, BF16) for i in range(2)]
    Gm = [T_("Gm%d" % hp, 320) for hp in range(4)]
    TTs = [T_("TTs%d" % hp, 192) for hp in range(4)]
    Nn = [[T_("Nn%d_%d" % (hp, b), 128) for b in range(2)] for hp in range(4)]
    Pm = [[T_("Pm%d_%d" % (hp, b), 64) for b in range(2)] for hp in range(4)]
    Wsb = [T_("Wsb%d" % hp, 64) for hp in range(4)]
    Usb = [T_("Usb%d" % hp, 64) for hp in range(4)]
    Ysb = [T_("Ysb%d" % hp, 64) for hp in range(4)]
    ycm = [T_("ycm%d" % hp, ST) for hp in range(4)]
    cnt = dict(ev=0, o=0, g=0, sq=0, pp=0, y=0, s=0)
    HS = [slice(0, 64), slice(64, 128)]

    def evac(dst, src, reads, writes, eng=None):
        if eng is None:
            eng = "act" if cnt["ev"] % 2 == 0 else "dve"
            cnt["ev"] += 1
        if eng == "act":
            P.op("act", lambda e: e.copy(dst, src), reads=reads, writes=writes)
        else:
            P.op("dve", lambda e: e.tensor_copy(dst, src), reads=reads, writes=writes)

    def do_prep(hp, st):
        b = st % 2
        t0 = st * ST
        I = {n: inb[n][hp][b] for n in inb}
        Pp = {n: prep[n][hp][b] for n in prep}
        ik = lambda n: "in_%s_%d_%d" % (n, hp, b)
        pk = lambda n: "pp_%s_%d_%d" % (n, hp, b)
        for n in ("r", "k", "v", "ld", "a", "g"):
            P.op("sp", lambda e, n=n, t=I[n]: e.dma_start(out=t[:], in_=ins_d[n][hp * 128:(hp + 1) * 128, t0:t0 + ST]),
                 writes=[ik(n)], dma=True)
        col = lambda j: par[:, hp, j:j + 1]
        dve = lambda fn, r, w: P.op("dve", fn, reads=r, writes=w)
        act = lambda fn, r, w: P.op("act", fn, reads=r, writes=w)
        pool = lambda fn, r, w: P.op("dve", fn, reads=r, writes=w)
        dve(lambda e: e.tensor_tensor_scan(sc["cs"][:], resetm[:], I["ld"][:], 0.0, ALU.mult, ALU.add), [ik("ld"), "resetm"], ["cs"])
        act(lambda e: e.activation(Pp["gam"][:], sc["cs"][:], AF.Exp), ["cs"], [pk("gam")])
        pool(lambda e: e.tensor_tensor(sc["csm"][:], sc["cs"][:], I["ld"][:], ALU.subtract), ["cs", ik("ld")], ["csm"])
        act(lambda e: e.activation(sc["gm1"][:], sc["csm"][:], AF.Exp), ["csm"], ["gm1"])
        act(lambda e: e.activation(sc["ig"][:], sc["cs"][:], AF.Exp, scale=-1.0), ["cs"], ["ig"])
        pool(lambda e: e.tensor_scalar(sc["kk"][:], I["k"][:], col(0), None, ALU.mult), [ik("k"), "par"], ["kk"])
        act(lambda e: e.activation(sc["kksq"][:], sc["kk"][:], AF.Square), ["kk"], ["kksq"])
        P.op("pe", lambda e: e.matmul(bkA[hp][:, 0:256], onesbd[:], sc["kksq"][:], start=True, stop=True), reads=["onesbd", "kksq"], writes=[KA[hp]])
        act(lambda e: e.activation(sc["nrm"][:], bkA[hp][:, 0:256], AF.Sqrt), [KA[hp]], ["nrm", KA[hp]])
        dve(lambda e: e.tensor_scalar(sc["nrm"][:], sc["nrm"][:], 1e-12, None, ALU.max), ["nrm"], ["nrm"])
        dve(lambda e: e.reciprocal(sc["rn"][:], sc["nrm"][:]), ["nrm"], ["rn"])
        pool(lambda e: e.tensor_tensor(sc["kkn"][:], sc["kk"][:], sc["rn"][:], ALU.mult), ["kk", "rn"], ["kkn"])
        dve(lambda e: e.tensor_scalar(sc["t"][:], I["a"][:], col(1), col(1), ALU.mult, ALU.subtract), [ik("a"), "par"], ["t"])
        dve(lambda e: e.scalar_tensor_tensor(sc["kp"][:], sc["t"][:], 1.0, I["k"][:], ALU.add, ALU.mult), ["t", ik("k")], ["kp"])
        dve(lambda e: e.scalar_tensor_tensor(Pp["At"][:], sc["gm1"][:], -1.0, sc["kkn"][:], ALU.mult, ALU.mult), ["gm1", "kkn"], [pk("At")])
        pool(lambda e: e.tensor_tensor(Pp["Rt"][:], Pp["gam"][:], I["r"][:], ALU.mult), [pk("gam"), ik("r")], [pk("Rt")])
        pool(lambda e: e.tensor_tensor(sc["bv"][:], sc["kkn"][:], I["a"][:], ALU.mult), ["kkn", ik("a")], ["bv"])
        pool(lambda e: e.tensor_tensor(Pp["Bt"][:], sc["bv"][:], sc["ig"][:], ALU.mult), ["bv", "ig"], [pk("Bt")])
        dve(lambda e: e.tensor_tensor(Pp["Kt"][:], sc["kp"][:], sc["ig"][:], ALU.mult), ["kp", "ig"], [pk("Kt")])
        dve(lambda e: e.scalar_tensor_tensor(sc["rk"][:], I["r"][:], col(2), sc["kp"][:], ALU.mult, ALU.mult), [ik("r"), "par", "kp"], ["rk"])
        P.op("pe", lambda e: e.matmul(bkA[hp][:, 256:512], onesbd[:], sc["rk"][:], start=True, stop=True), reads=["onesbd", "rk"], writes=[KA[hp]])
        dve(lambda e: e.tensor_tensor(Pp["bonus"][:], bkA[hp][:, 256:512], I["v"][:], ALU.mult), [KA[hp], ik("v")], [pk("bonus"), KA[hp]])

    def chunk_stages(hp, st, ch):
        b = st % 2
        sl = slice(ch * 64, (ch + 1) * 64)
        Pp = {n: prep[n][hp][b] for n in prep}
        pk = lambda n: "pp_%s_%d_%d" % (n, hp, b)
        vt = inb["v"][hp][b]
        vk = "in_v_%d_%d" % (hp, b)
        cidx = st * NCH + ch
        sp_, sn_ = S[hp][cidx % 2], S[hp][(cidx + 1) % 2]
        skp, skn = "S%d_%d" % (hp, cidx % 2), "S%d_%d" % (hp, (cidx + 1) % 2)
        gi = hp
        gk, tk = KA[hp], KA[hp]
        stages = []

        def mm(out, lhsT, rhs, start, stop, reads, writes):
            P.op("pe", lambda e: e.matmul(out, lhsT, rhs, start=start, stop=stop), reads=reads, writes=writes)

        def st_gram():
            for h2 in range(2):
                hs = HS[h2]
                pairs = (("Bt", "At"), ("At", "Bt"), ("Kt", "At"), ("Bt", "Rt"), ("Kt", "Rt"))
                for i, (l, r) in enumerate(pairs):
                    mm(Gps[gi][hs, 64 * i:64 * i + 64], Pp[l][hs, sl], Pp[r][hs, sl], True, True, [pk(l), pk(r)], [gk])
                for i, (t, k) in enumerate(((Pp["Bt"], pk("Bt")), (Pp["Kt"], pk("Kt")), (vt, vk))):
                    P.op("pe", lambda e, i=i, t=t, hs=hs, h2=h2: e.matmul(Tps[gi][hs, 64 * i:64 * i + 64], t[hs, sl], ident[hs, 64 * h2:64 * h2 + 64], start=True, stop=True),
                         reads=[k, "ident"], writes=[tk])
        stages.append(st_gram)

        def st_gevac():
            P.op("dve", lambda e: e.tensor_tensor(Gm[hp][:], Gps[gi], mask[:], ALU.mult), reads=[gk, "mask"], writes=["Gm%d" % hp, gk])
            P.op("act", lambda e: e.copy(TTs[hp][:], Tps[gi]), reads=[tk], writes=["TTs%d" % hp, tk])
        stages.append(st_gevac)

        def st_p0():
            P.op("dve", lambda e: e.tensor_tensor(Pm[hp][0][:], Gm[hp][:, 0:64], identst[:], ALU.add), reads=["Gm%d" % hp, "identst"], writes=["Pm%d_0" % hp])
        stages.append(st_p0)
        state = dict(N=(Gm[hp][:, 0:64], "Gm%d" % hp), NT=(Gm[hp][:, 64:128], "Gm%d" % hp))
        for lvl in range(5):
            def st_sq(lvl=lvl):
                si = hp
                state["si"] = si
                (N, nk), (NT, ntk) = state["N"], state["NT"]
                for h2 in range(2):
                    hs = HS[h2]
                    mm(sqp[si][hs, 0:64], NT[hs], N[hs], True, True, [nk, ntk], [KB[hp]])
                    mm(sqp[si][hs, 64:128], N[hs], NT[hs], True, True, [nk, ntk], [KB[hp]])
            stages.append(st_sq)

            def st_sqev(lvl=lvl):
                si = state["si"]
                nb = Nn[hp][lvl % 2]
                nbk = "Nn%d_%d" % (hp, lvl % 2)
                evac(nb[:], sqp[si], [KB[hp]], [nbk, KB[hp]])
                state["N"] = (nb[:, 0:64], nbk)
                state["NT"] = (nb[:, 64:128], nbk)
            stages.append(st_sqev)

            def st_pu(lvl=lvl):
                pi = hp
                state["pi"] = pi
                (NT, ntk) = state["NT"]
                pc = Pm[hp][lvl % 2]
                pck = "Pm%d_%d" % (hp, lvl % 2)
                for h2 in range(2):
                    hs = HS[h2]
                    mm(Pps[pi][hs], NT[hs], pc[hs], True, False, [ntk, pck], [KB[hp]])
                    mm(Pps[pi][hs], ident[hs, 64 * h2:64 * h2 + 64], pc[hs], False, True, ["ident", pck], [KB[hp]])
            stages.append(st_pu)

            def st_puev(lvl=lvl):
                pi = state["pi"]
                pn = Pm[hp][(lvl + 1) % 2]
                evac(pn[:], Pps[pi], [KB[hp]], ["Pm%d_%d" % (hp, (lvl + 1) % 2), KB[hp]])
            stages.append(st_puev)
        Tm, Tk = Pm[hp][1], "Pm%d_1" % hp

        def st_w():
            for h2 in range(2):
                hs = HS[h2]
                mm(Wps[hp][hs], Pp["At"][hs, sl], sp_[hs], True, False, [pk("At"), skp], [KB[hp]])
                mm(Wps[hp][hs], Gm[hp][hs, 128:192], TTs[hp][hs, 128:192], False, True, ["Gm%d" % hp, "TTs%d" % hp], [KB[hp]])
        stages.append(st_w)
        stages.append(lambda: evac(Wsb[hp][:], Wps[hp], [KB[hp]], ["Wsb%d" % hp, KB[hp]], "act"))

        def st_u():
            for h2 in range(2):
                hs = HS[h2]
                mm(Ups[hp][hs], Tm[hs], Wsb[hp][hs], True, True, [Tk, "Wsb%d" % hp], [KB[hp]])
        stages.append(st_u)
        stages.append(lambda: evac(Usb[hp][:], Ups[hp], [KB[hp]], ["Usb%d" % hp, KB[hp]], "dve"))

        def st_ys():
            yi = hp
            state["yi"] = yi
            for h2 in range(2):
                hs = HS[h2]
                mm(Sps[yi][hs], TTs[hp][hs, 0:64], Usb[hp][hs], True, False, ["TTs%d" % hp, "Usb%d" % hp], [KB[hp]])
                mm(Sps[yi][hs], TTs[hp][hs, 64:128], TTs[hp][hs, 128:192], False, False, ["TTs%d" % hp], [KB[hp]])
                mm(Sps[yi][hs], ident[hs, 64 * h2:64 * h2 + 64], sp_[hs], False, True, ["ident", skp], [KB[hp]])
            for h2 in range(2):
                hs = HS[h2]
                mm(Yps[yi][hs], Pp["Rt"][hs, sl], sp_[hs], True, False, [pk("Rt"), skp], [KB[hp]])
                mm(Yps[yi][hs], Gm[hp][hs, 192:256], Usb[hp][hs], False, False, ["Gm%d" % hp, "Usb%d" % hp], [KB[hp]])
                mm(Yps[yi][hs], Gm[hp][hs, 256:320], TTs[hp][hs, 128:192], False, True, ["Gm%d" % hp, "TTs%d" % hp], [KB[hp]])
        stages.append(st_ys)

        def st_ysev():
            yi = state["yi"]
            ce = ch * 64 + 63
            P.op("act", lambda e: e.activation(sn_[:], Sps[yi], AF.Copy, scale=Pp["gam"][:, ce:ce + 1]),
                 reads=[KB[hp], pk("gam")], writes=[skn, KB[hp]])
            P.op("dve", lambda e: e.tensor_copy(Ysb[hp][:], Yps[yi]), reads=[KB[hp]], writes=["Ysb%d" % hp, KB[hp]])
        stages.append(st_ysev)

        def st_ytr():
            for h2 in range(2):
                hs = HS[h2]
                P.op("pe", lambda e, hs=hs, h2=h2: e.matmul(Ytr[hp][hs], Ysb[hp][hs], ident[hs, 64 * h2:64 * h2 + 64], start=True, stop=True),
                     reads=["Ysb%d" % hp, "ident"], writes=[KB[hp]])
        stages.append(st_ytr)
        stages.append(lambda: evac(ycm[hp][:, sl], Ytr[hp], [KB[hp]], ["ycm%d" % hp, KB[hp]], "act"))
        return stages

    def do_epilogue(hp, st):
        b = st % 2
        t0 = st * ST
        gt = inb["g"][hp][b]
        gk_ = "in_g_%d_%d" % (hp, b)
        bon = prep["bonus"][hp][b]
        bk = "pp_bonus_%d_%d" % (hp, b)
        col = lambda j: par[:, hp, j:j + 1]
        dve = lambda fn, r, w: P.op("dve", fn, reads=r, writes=w)
        act = lambda fn, r, w: P.op("act", fn, reads=r, writes=w)
        pool = lambda fn, r, w: P.op("dve", fn, reads=r, writes=w)
        yk = "ycm%d" % hp
        sum_ps = bkA[hp][:, 0:256]
        ssq_ps = bkA[hp][:, 256:512]
        act(lambda e: e.activation(sc["ysq"][:], ycm[hp][:], AF.Square), [yk], ["ysq"])
        P.op("pe", lambda e: e.matmul(sum_ps, onesbd[:], ycm[hp][:], start=True, stop=True), reads=["onesbd", yk], writes=[KA[hp]])
        P.op("pe", lambda e: e.matmul(ssq_ps, onesbd[:], sc["ysq"][:], start=True, stop=True), reads=["onesbd", "ysq"], writes=[KA[hp]])
        dve(lambda e: e.tensor_scalar(sc["mean"][:], sum_ps, 1.0 / 64, None, ALU.mult), [KA[hp]], ["mean", KA[hp]])
        pool(lambda e: e.tensor_tensor(sc["msq"][:], sc["mean"][:], sc["mean"][:], ALU.mult), ["mean"], ["msq"])
        dve(lambda e: e.scalar_tensor_tensor(sc["var"][:], ssq_ps, 1.0 / 64, sc["msq"][:], ALU.mult, ALU.subtract), [KA[hp], "msq"], ["var", KA[hp]])
        act(lambda e: e.activation(sc["std"][:], sc["var"][:], AF.Sqrt, bias=GN_EPS, scale=1.0), ["var"], ["std"])
        dve(lambda e: e.reciprocal(sc["rstd"][:], sc["std"][:]), ["std"], ["rstd"])
        pool(lambda e: e.tensor_tensor(sc["yc"][:], ycm[hp][:], sc["mean"][:], ALU.subtract), [yk, "mean"], ["yc"])
        pool(lambda e: e.tensor_tensor(sc["yn"][:], sc["yc"][:], sc["rstd"][:], ALU.mult), ["yc", "rstd"], ["yn"])
        dve(lambda e: e.tensor_scalar(sc["z"][:], sc["yn"][:], col(3), col(4), ALU.mult, ALU.add), ["yn", "par"], ["z"])
        pool(lambda e: e.tensor_tensor(sc["z2"][:], sc["z"][:], bon[:], ALU.add), ["z", bk], ["z2"])
        oi = cnt["o"] % 2
        cnt["o"] += 1
        pool(lambda e: e.tensor_tensor(osb[oi][:], sc["z2"][:], gt[:], ALU.mult), ["z2", gk_], ["osb%d" % oi])
        P.op("sp", lambda e: e.dma_start(out=o_d[hp * 128:(hp + 1) * 128, t0:t0 + ST], in_=osb[oi][:]), reads=["osb%d" % oi], dma=True)

    import os
    for st in range(int(os.environ.get('RW_NST', NST))):
        for hp in range(4):
            do_prep(hp, st)
        for ch in range(int(os.environ.get('RW_NCH', NCH))):
            sts = [chunk_stages(hp, st, ch) for hp in range(4)]
            for i in range(min(len(sts[0]), int(os.environ.get('RW_NSTAGE', 1000)))):
                for hp in range(4):
                    sts[hp][i]()
        if int(os.environ.get('RW_EPI', 1)):
            for hp in range(4):
                do_epilogue(hp, st)
    return P.finish()


def rwkv_consts():
    ident = np.eye(128, dtype=np.float32)
    onesbd = np.zeros((128, 128), np.float32)
    onesbd[:64, :64] = 1.0
    onesbd[64:, 64:] = 1.0
    i = (np.arange(128) % 64)[:, None]
    t = np.arange(64)[None, :]
    su = (i < t).astype(np.float32)
    sl_ = (t < i).astype(np.float32)
    ui = (i <= t).astype(np.float32)
    mask = np.concatenate([su, sl_, su, ui, ui], axis=1)
    identst = (i == t).astype(np.float32)
    resetm = np.ones((128, RW_ST), np.float32)
    resetm[:, ::64] = 0.0
    return dict(ident=ident, onesbd=onesbd, mask320=np.ascontiguousarray(mask), identst=identst, resetm=resetm)


def run_rwkv_core_stage(proj, inp, ic):
    if "rwkvc" not in _cache:
        _cache["rwkvc"] = build_rwkv_core_stage()
    consts = rwkv_consts()
    pv = np.stack([inp["rwkv_k_k"][ic], inp["rwkv_k_a"][ic], inp["rwkv_r_k"][ic].reshape(-1), inp["rwkv_ln_w"][ic], inp["rwkv_ln_b"][ic]], axis=1)
    maps = []
    for c in range(NCORES):
        seq, hq = divmod(c, 4)
        m = dict(consts)
        for n in ("r", "k", "v", "ld", "a", "g"):
            m[n] = np.ascontiguousarray(np.concatenate([proj[4 * seq + i][n][512 * hq:512 * hq + 512, :] for i in range(4)], axis=1))
        m["par"] = np.ascontiguousarray(pv[512 * hq:512 * hq + 512].reshape(4, 128, 5).transpose(1, 0, 2)).astype(np.float32)
        maps.append(m)
    res = _run(_cache["rwkvc"], maps)
    ys = [r["oT"] for r in res.results]
    out = []
    for c in range(NCORES):
        seq, i = divmod(c, 4)
        out.append(np.ascontiguousarray(np.concatenate([ys[4 * seq + hq][:, i * TOK:(i + 1) * TOK] for hq in range(4)], axis=0)))
    return out


def kernel(**inputs):
    inp = {k: np.asarray(v) for k, v in inputs.items()}
    x = inp["x"].reshape(NCORES * TOK, D)
    xT = [np.ascontiguousarray(x[c * TOK:(c + 1) * TOK].T) for c in range(NCORES)]
    ia = ib = ic = 0
    depth = inp["norm_mix"].shape[0]
    for layer in range(depth):
        kind = layer % 3
        gfin = inp["norm_f"] if layer == depth - 1 else None
        if kind == 0:
            qkv = run_qkv_stage(xT, inp["attn_w_qkv"][ia], inp["norm_mix"][layer])
            aT = run_attn_stage(qkv)
            wo = inp["attn_w_o"][ia]
            ia += 1
        elif kind == 1:
            uT = run_normproj_stage(xT, inp["ssm_w_in"][ib], inp["norm_mix"][layer])
            aT = run_s5_stage(uT, inp["ssm_log_dt"][ib], inp["ssm_a_re"][ib], inp["ssm_a_im"][ib], inp["ssm_b_re"][ib],
                              inp["ssm_b_im"][ib], inp["ssm_c_re"][ib], inp["ssm_c_im"][ib], inp["ssm_d"][ib])
            wo = inp["ssm_w_out"][ib]
            ib += 1
        else:
            proj = run_rwkv_proj_stage(xT, inp, layer, ic)
            aT = run_rwkv_core_stage(proj, inp, ic)
            wo = inp["rwkv_w_o"][ic]
            ic += 1
        xT = run_mlp_stage(xT, inp["mlp_w1"][layer], inp["mlp_w2"][layer], inp["norm_mlp"][layer], g_final=gfin,
                           aT_cores=aT, wo=wo)
    out = np.concatenate([np.asarray(t).T for t in xT], axis=0).reshape(inp["x"].shape)
    return np.ascontiguousarray(out.astype(np.float32))
```

```python
import numpy as np
from contextlib import ExitStack
import concourse.bass as bass
import concourse.mybir as mybir
from concourse.bass_utils import run_bass_kernel_spmd

F32 = mybir.dt.float32
BF16 = mybir.dt.bfloat16
ALU = mybir.AluOpType
AF = mybir.ActivationFunctionType

D = 2048
KC = 16
NCORES = 8
TOK = 2048
EPS = 1e-5

COMPUTE = ("pe", "dve", "act", "pool")
NDMA = 12


class Sched:
    def __init__(self, nc):
        self.nc = nc
        self.ins = {e: [] for e in ("pe", "dve", "act", "pool", "sp")}
        self.last_w = {}
        self.readers = {}

    def op(self, eng, fn, reads=(), writes=(), dma=False):
        lst = self.ins[eng]
        idx = len(lst)
        me = (eng, idx, dma)
        deps = set()
        for r in reads:
            w = self.last_w.get(r)
            if w is not None:
                deps.add(w)
        for r in writes:
            w = self.last_w.get(r)
            if w is not None:
                deps.add(w)
            for rd in self.readers.get(r, {}).values():
                for t in rd:
                    if t[0] == eng and not dma and not t[2]:
                        continue
                    deps.add(t)
        deps.discard(me)
        if eng == "pe":
            deps = {d for d in deps if d[0] != "pe"}
        lst.append(dict(fn=fn, deps=deps, dma=dma, marked=False))
        for d in deps:
            self.ins[d[0]][d[1]]["marked"] = True
        for r in writes:
            self.last_w[r] = me
            self.readers[r] = {}
        for r in reads:
            rd = self.readers.setdefault(r, {})
            if dma:
                rd.setdefault((eng, "dma"), []).append(me)
            else:
                rd[(eng, "c")] = [me]
        return me

    def emit(self):
        nc = self.nc
        ins = self.ins
        with ExitStack() as es:
            csem = {e: es.enter_context(nc.semaphore("cs_" + e)) for e in COMPUTE}
            dsem = {q: [es.enter_context(nc.semaphore("ds_%s%d" % (q, i))) for i in range(NDMA)]
                    for q in ("sp", "pool")}
            for e, lst in ins.items():
                c = 0
                nd = 0
                for it in lst:
                    if it["dma"]:
                        it["dsem"] = dsem[e][nd % NDMA]
                        it["dkey"] = (e, nd % NDMA)
                        it["dval"] = 16 * (nd // NDMA + 1)
                        it["dprev"] = 16 * (nd // NDMA)
                        nd += 1
                    else:
                        if it["marked"]:
                            c += 1
                        it["cval"] = c
            block = es.enter_context(nc.Block())

            def run(e, engobj):
                seen_c = {x: 0 for x in COMPUTE}
                seen_d = {}
                for it in ins[e]:
                    for d in sorted(it["deps"]):
                        tgt = ins[d[0]][d[1]]
                        if tgt["dma"]:
                            if seen_d.get(tgt["dkey"], 0) >= tgt["dval"]:
                                continue
                            engobj.wait_ge(tgt["dsem"], tgt["dval"])
                            seen_d[tgt["dkey"]] = tgt["dval"]
                        else:
                            v = tgt["cval"]
                            if seen_c[d[0]] >= v:
                                continue
                            engobj.wait_ge(csem[d[0]], v)
                            seen_c[d[0]] = v
                    if it["dma"]:
                        if it["dprev"] > 0 and seen_d.get(it["dkey"], 0) < it["dprev"]:
                            engobj.wait_ge(it["dsem"], it["dprev"])
                            seen_d[it["dkey"]] = it["dprev"]
                        it["fn"](engobj).then_inc(it["dsem"], 16)
                    else:
                        r = it["fn"](engobj)
                        if it["marked"]:
                            r.then_inc(csem[e], 1)
                if e in ("sp", "pool"):
                    last = {}
                    for it in ins[e]:
                        if it["dma"]:
                            last[it["dkey"]] = (it["dsem"], it["dval"])
                    for s, v in last.values():
                        engobj.wait_ge(s, v)

            @block.tensor
            def _(eng):
                run("pe", eng)

            @block.vector
            def _(eng):
                run("dve", eng)

            @block.scalar
            def _(eng):
                run("act", eng)

            @block.gpsimd
            def _(eng):
                run("pool", eng)

            @block.sync
            def _(eng):
                run("sp", eng)


class Prog:
    def __init__(self):
        self.nc = bass.Bass("TRN2", target_bir_lowering=False)
        self.S = Sched(self.nc)
        self.es = ExitStack()
        self.uid = 0

    def sb(self, name, shape, dt=F32):
        return self.es.enter_context(self.nc.sbuf_tensor("s_" + name, shape, dt))

    def ps(self, name, shape, dt=F32):
        return self.es.enter_context(self.nc.psum_tensor("p_" + name, shape, dt))

    def din(self, name, shape, dt=F32):
        return self.nc.dram_tensor(name, list(shape), dt, kind="ExternalInput").ap()

    def dout(self, name, shape, dt=F32):
        return self.nc.dram_tensor(name, list(shape), dt, kind="ExternalOutput").ap()

    def op(self, *a, **k):
        if getattr(self, "defer", None) is not None:
            self.defer.append((a, k))
            return None
        return self.S.op(*a, **k)

    def deferred(self, fn):
        self.defer = []
        fn()
        ops, self.defer = self.defer, None
        return ops

    def flush(self, ops, n):
        open_ = 0
        done = 0
        while ops and (done < n or open_ > 0):
            a, k = ops.pop(0)
            eng = a[0]
            if eng == "pe":
                open_ += sum(1 for w in k.get("writes", ()) if str(w).startswith("bk"))
            elif not k.get("dma"):
                if any(str(r).startswith("bk") for r in k.get("reads", ())):
                    open_ -= 1
            self.S.op(*a, **k)
            done += 1

    def finish(self):
        self.S.emit()
        self.es.close()
        return self.nc


class Common:
    def __init__(self, P):
        self.P = P
        self.ones = P.sb("ones_f", [128, 128], F32)
        P.op("dve", lambda e: e.memset(self.ones[:], 1.0), writes=["ones"])
        self.sq = [P.sb("sq%d" % i, [128, 512], F32) for i in range(2)]
        self.std = P.sb("std", [128, 512], F32)
        self.rstd = P.sb("rstd", [128, 512], F32)
        self.stat = P.ps("stat_ps", [128, 512])
        self.nsq = 0


def rmsnorm(P, C, x_sb, g_sb, gcol0, out_sb, TT, xkey, okey, out_dt_is_f32=False):
    for h in range(TT // 512):
        sl = slice(h * 512, (h + 1) * 512)
        for kc in range(KC):
            j = C.nsq % 2
            C.nsq += 1
            P.op("act", lambda e, j=j, kc=kc, sl=sl: e.activation(C.sq[j][:], x_sb[:, kc, sl], AF.Square),
                 reads=[xkey + "_%d" % kc], writes=["sq%d" % j])
            P.op("pe", lambda e, j=j, kc=kc: e.matmul(C.stat[:], C.ones[:], C.sq[j][:], start=(kc == 0), stop=(kc == KC - 1)),
                 reads=["ones", "sq%d" % j], writes=["stat"])
        P.op("act", lambda e: e.activation(C.std[:], C.stat[:], AF.Sqrt, bias=EPS, scale=1.0 / D),
             reads=["stat"], writes=["std"])
        P.op("dve", lambda e: e.reciprocal(C.rstd[:], C.std[:]), reads=["std"], writes=["rstd"])
        for kc in range(KC):
            P.op("dve", lambda e, kc=kc, sl=sl: e.scalar_tensor_tensor(
                out_sb[:, kc, sl], x_sb[:, kc, sl], g_sb[:, gcol0 + kc:gcol0 + kc + 1], C.rstd[:], ALU.mult, ALU.mult),
                reads=[xkey + "_%d" % kc, "rstd", "gvec"], writes=[okey + "_%d" % kc])


class MlpBufs:
    FB = 512

    def __init__(self, P):
        FB = self.FB
        self.w1 = [P.sb("w1b%d" % i, [128, KC, FB], BF16) for i in range(2)]
        self.w2 = [P.sb("w2b%d" % i, [128, FB // 128, D], BF16) for i in range(2)]
        self.g1 = [P.ps("g1_%d" % i, [128, 512]) for i in range(2)]
        self.acc = [P.ps("acc_%d" % i, [128, 512]) for i in range(4)]
        self.relu = [P.sb("relu%d" % i, [128, 512], F32) for i in range(2)]
        self.cnt_w = 0
        self.cnt_g1 = 0
        self.cnt_acc = 0
        self.cnt_relu = 0


def mlp_block(P, C, M, x_sb, xn_sb, hT, TT, w1_d, w2_d, xkey, xnkey):
    FB = M.FB
    FBC = FB // 128
    NB = w1_d.shape[1] // FB
    NH = TT // 512
    w1v = w1_d.rearrange("(kc p) f -> p kc f", p=128)
    w2v = w2_d.rearrange("(fc p) n -> p fc n", p=128)

    def gemm2(b):
        par = b % 2
        for h in range(NH):
            sl = slice(h * 512, (h + 1) * 512)
            for n in range(KC):
                a = M.cnt_acc % 4
                M.cnt_acc += 1
                for fc in range(FBC):
                    P.op("pe", lambda e, a=a, fc=fc, n=n, sl=sl, par=par, wp=M.wpar[b]: e.matmul(
                        M.acc[a][:], M.w2[wp][:, fc, n * 128:(n + 1) * 128], hT[par][:, fc, sl],
                        start=(fc == 0), stop=(fc == FBC - 1)),
                        reads=["w2b%d" % M.wpar[b], "hT%d_%d_%d" % (par, fc, h)], writes=["acc%d" % a])
                P.op("dve", lambda e, a=a, n=n, sl=sl: e.tensor_tensor(x_sb[:, n, sl], x_sb[:, n, sl], M.acc[a][:], ALU.add),
                     reads=["acc%d" % a, xkey + "_%d" % n], writes=[xkey + "_%d" % n])

    M.wpar = {}
    pend = []
    for b in range(NB):
        wp = M.cnt_w % 2
        M.cnt_w += 1
        M.wpar[b] = wp
        P.op("pool", lambda e, b=b, wp=wp: e.dma_start(out=M.w1[wp][:], in_=w1v[:, :, b * FB:(b + 1) * FB]),
             writes=["w1b%d" % wp], dma=True)
        P.op("pool", lambda e, b=b, wp=wp: e.dma_start(out=M.w2[wp][:], in_=w2v[:, b * FBC:(b + 1) * FBC, :]),
             writes=["w2b%d" % wp], dma=True)
        par = b % 2
        for h in range(NH):
            sl = slice(h * 512, (h + 1) * 512)
            for fc in range(FBC):
                g = M.cnt_g1 % 2
                M.cnt_g1 += 1
                for kc in range(KC):
                    P.op("pe", lambda e, g=g, kc=kc, fc=fc, sl=sl, wp=wp: e.matmul(
                        M.g1[g][:], M.w1[wp][:, kc, fc * 128:(fc + 1) * 128], xn_sb[:, kc, sl],
                        start=(kc == 0), stop=(kc == KC - 1)),
                        reads=["w1b%d" % wp, xnkey + "_%d" % kc], writes=["g1_%d" % g])
                r = M.cnt_relu % 2
                M.cnt_relu += 1
                P.op("act", lambda e, g=g, r=r: e.activation(M.relu[r][:], M.g1[g][:], AF.Relu),
                     reads=["g1_%d" % g], writes=["relu%d" % r])
                if pend:
                    pend.pop()()
                pend.append(lambda r=r, fc=fc, sl=sl, par=par, h=h: P.op(
                    "act", lambda e: e.activation(hT[par][:, fc, sl], M.relu[r][:], AF.Square),
                    reads=["relu%d" % r], writes=["hT%d_%d_%d" % (par, fc, h)]))
        if b >= 1:
            gemm2(b - 1)
    pend.pop()()
    gemm2(NB - 1)


def proj_block(P, C, M, x_sb, a_sb, TT, wo_d, glu, xkey, akey):
    NH = TT // 512
    wv = wo_d.rearrange("(kc p) f -> p kc f", p=128)
    for nt in range(4):
        wps = []
        for part in range(2 if glu else 1):
            wp = M.cnt_w % 2
            M.cnt_w += 1
            wps.append(wp)
            c0 = part * D + nt * 512
            P.op("pool", lambda e, wp=wp, c0=c0: e.dma_start(out=M.w1[wp][:], in_=wv[:, :, c0:c0 + 512]),
                 writes=["w1b%d" % wp], dma=True)
        for nn in range(4):
            n = nt * 4 + nn
            for h in range(NH):
                sl = slice(h * 512, (h + 1) * 512)
                accs = []
                for part in range(len(wps)):
                    a = M.cnt_acc % 4
                    M.cnt_acc += 1
                    accs.append(a)
                    wp = wps[part]
                    for kc in range(KC):
                        P.op("pe", lambda e, a=a, wp=wp, kc=kc, nn=nn, sl=sl: e.matmul(
                            M.acc[a][:], M.w1[wp][:, kc, nn * 128:(nn + 1) * 128], a_sb[:, kc, sl],
                            start=(kc == 0), stop=(kc == KC - 1)),
                            reads=["w1b%d" % wp, akey + "_%d" % kc], writes=["acc%d" % a])
                if not glu:
                    a = accs[0]
                    P.op("dve", lambda e, a=a, n=n, sl=sl: e.tensor_tensor(x_sb[:, n, sl], x_sb[:, n, sl], M.acc[a][:], ALU.add),
                         reads=["acc%d" % a, xkey + "_%d" % n], writes=[xkey + "_%d" % n])
                else:
                    a1, a2 = accs
                    P.op("act", lambda e, a2=a2: e.activation(M.relu[0][:], M.acc[a2][:], AF.Sigmoid),
                         reads=["acc%d" % a2], writes=["relu0"])
                    P.op("dve", lambda e, a1=a1: e.tensor_tensor(M.relu[1][:], M.acc[a1][:], M.relu[0][:], ALU.mult),
                         reads=["acc%d" % a1, "relu0"], writes=["relu1"])
                    P.op("dve", lambda e, n=n, sl=sl: e.tensor_tensor(x_sb[:, n, sl], x_sb[:, n, sl], M.relu[1][:], ALU.add),
                         reads=["relu1", xkey + "_%d" % n], writes=[xkey + "_%d" % n])


def build_mlp_stage(final_norm, proj=None):
    P = Prog()
    TT = 1024
    xT_d = P.din("xT", [D, TOK])
    w1_d = P.din("w1", [D, 4 * D])
    w2_d = P.din("w2", [4 * D, D])
    g_d = P.din("gv", [128, 2 * KC])
    if proj:
        a_d = P.din("aT", [D, TOK], BF16)
        wo_d = P.din("wo", [D, 2 * D if proj == "glu" else D])
    out_d = P.dout("yT", [D, TOK])
    C = Common(P)
    M = MlpBufs(P)
    x_sb = P.sb("x_sb", [128, KC, TT], F32)
    xn_sb = P.sb("xn_sb", [128, KC, TT], BF16)
    hT = [P.sb("hT%d" % i, [128, M.FB // 128, TT], BF16) for i in range(2)]
    g_sb = P.sb("g_sb", [128, 2 * KC], F32)
    P.op("sp", lambda e: e.dma_start(out=g_sb[:], in_=g_d), writes=["gvec"], dma=True)
    xv = xT_d.rearrange("(kc p) t -> p kc t", p=128)
    ov = out_d.rearrange("(kc p) t -> p kc t", p=128)
    for ps_ in range(TOK // TT):
        tsl = slice(ps_ * TT, (ps_ + 1) * TT)
        for q in range(4):
            P.op("sp", lambda e, q=q, tsl=tsl: e.dma_start(out=x_sb[:, 4 * q:4 * q + 4, :], in_=xv[:, 4 * q:4 * q + 4, tsl]),
                 writes=["x_%d" % k for k in range(4 * q, 4 * q + 4)], dma=True)
        if proj:
            av = a_d.rearrange("(kc p) t -> p kc t", p=128)
            for q in range(2):
                P.op("sp", lambda e, q=q, tsl=tsl: e.dma_start(out=xn_sb[:, 8 * q:8 * q + 8, :], in_=av[:, 8 * q:8 * q + 8, tsl]),
                     writes=["xn_%d" % k for k in range(8 * q, 8 * q + 8)], dma=True)
            proj_block(P, C, M, x_sb, xn_sb, TT, wo_d, proj == "glu", "x", "xn")
        rmsnorm(P, C, x_sb, g_sb, 0, xn_sb, TT, "x", "xn")
        mlp_block(P, C, M, x_sb, xn_sb, hT, TT, w1_d, w2_d, "x", "xn")
        if final_norm:
            rmsnorm(P, C, x_sb, g_sb, KC, x_sb, TT, "x", "x")
        for q in range(4):
            P.op("sp", lambda e, q=q, tsl=tsl: e.dma_start(out=ov[:, 4 * q:4 * q + 4, tsl], in_=x_sb[:, 4 * q:4 * q + 4, :]),
                 reads=["x_%d" % k for k in range(4 * q, 4 * q + 4)], dma=True)
    return P.finish()


def gvec_layout(*vecs):
    return np.ascontiguousarray(np.concatenate([v.reshape(KC, 128).T for v in vecs], axis=1)).astype(np.float32)


_cache = {}


_last = {}


def _run(nc, maps):
    import os
    if os.environ.get("KTRACE"):
        res = run_bass_kernel_spmd(nc, maps, core_ids=list(range(NCORES)), trace=True)
        print("KTRACE exec_time_ns", res.exec_time_ns, flush=True)
        _last["res"] = res
        return res
    return run_bass_kernel_spmd(nc, maps, core_ids=list(range(NCORES)))


def run_mlp_stage(xT_cores, w1, w2, g_mlp, g_final=None, aT_cores=None, wo=None):
    proj = None if wo is None else ("glu" if wo.shape[1] == 2 * D else "lin")
    key = ("mlp", g_final is not None, proj)
    if key not in _cache:
        _cache[key] = build_mlp_stage(g_final is not None, proj)
    nc = _cache[key]
    gv = gvec_layout(g_mlp, g_final if g_final is not None else g_mlp)
    in_maps = [{"xT": xT_cores[c], "w1": w1, "w2": w2, "gv": gv} for c in range(NCORES)]
    if proj:
        for c in range(NCORES):
            in_maps[c]["aT"] = aT_cores[c]
            in_maps[c]["wo"] = wo
    res = _run(nc, in_maps)
    return [r["yT"] for r in res.results]


def load_norm_full(P, C, xT_d, g_sb, gcol0, xn_sb, xq, QW=256):
    xv = xT_d.rearrange("(kc p) t -> p kc t", p=128)
    for q in range(TOK // QW):
        b = q % 2
        tsl = slice(q * QW, (q + 1) * QW)
        keys = ["xq%d_%d" % (b, k) for k in range(KC)]
        for s in range(2):
            P.op("sp", lambda e, b=b, s=s, tsl=tsl: e.dma_start(out=xq[b][:, 8 * s:8 * s + 8, :], in_=xv[:, 8 * s:8 * s + 8, tsl]),
                 writes=keys[8 * s:8 * s + 8], dma=True)
        for kc in range(KC):
            j = C.nsq % 2
            C.nsq += 1
            P.op("act", lambda e, j=j, kc=kc, b=b: e.activation(C.sq[j][:, :QW], xq[b][:, kc, :], AF.Square),
                 reads=[keys[kc]], writes=["sq%d" % j])
            P.op("pe", lambda e, j=j, kc=kc: e.matmul(C.stat[:, :QW], C.ones[:], C.sq[j][:, :QW], start=(kc == 0), stop=(kc == KC - 1)),
                 reads=["ones", "sq%d" % j], writes=["stat"])
        P.op("act", lambda e: e.activation(C.std[:, :QW], C.stat[:, :QW], AF.Sqrt, bias=EPS, scale=1.0 / D),
             reads=["stat"], writes=["std"])
        P.op("dve", lambda e: e.reciprocal(C.rstd[:, :QW], C.std[:, :QW]), reads=["std"], writes=["rstd"])
        for kc in range(KC):
            P.op("dve", lambda e, kc=kc, b=b, tsl=tsl: e.scalar_tensor_tensor(
                xn_sb[:, kc, tsl], xq[b][:, kc, :], g_sb[:, gcol0 + kc:gcol0 + kc + 1], C.rstd[:, :QW], ALU.mult, ALU.mult),
                reads=[keys[kc], "rstd", "gvec"], writes=["xn_%d" % kc])


DIL = (1, 4, 16)


def blk_tok_slice(d, blk):
    J = TOK // (128 * d)
    r, j = divmod(blk, J)
    start = j * 128 * d + r
    return slice(start, start + 127 * d + 1, d) if d > 1 else slice(start, start + 128)


def build_qkv_stage():
    P = Prog()
    xT_d = P.din("xT", [D, TOK])
    w_d = P.din("wqkv", [D, 9 * D])
    g_d = P.din("gv", [128, KC])
    q_d = P.dout("qT", [3, 16, 128, TOK], BF16)
    k_d = P.dout("kT", [3, 16, 128, TOK], BF16)
    v_d = P.dout("v", [3, 16, 128, 16, 128], BF16)
    C = Common(P)
    g_sb = P.sb("g_sb", [128, KC], F32)
    P.op("sp", lambda e: e.dma_start(out=g_sb[:], in_=g_d), writes=["gvec"], dma=True)
    xq = [P.sb("xq%d" % i, [128, KC, 256], F32) for i in range(2)]
    xn = P.sb("xn", [128, KC, TOK], BF16)
    load_norm_full(P, C, xT_d, g_sb, 0, xn, xq, 256)
    xnkeys = ["xn_%d" % k for k in range(KC)]
    NW = 3
    wb = [P.sb("wb%d" % i, [128, KC, 512], BF16) for i in range(NW)]
    gp = [P.ps("gp%d" % i, [128, 512]) for i in range(4)]
    qst = [P.sb("qst%d" % i, [128, TOK], BF16) for i in range(3)]
    vst = [P.sb("vst%d" % i, [128, 16, 512], BF16) for i in range(2)]
    wv = w_d.rearrange("(kc p) f -> p kc f", p=128)
    cnt = dict(w=0, gp=0, q=0, v=0, ev=0)
    scale = 128.0 ** -0.5
    for g in range(3):
        d = DIL[g]
        for t in range(3):
            for nt in range(4):
                col0 = g * 3 * D + t * D + nt * 512
                wi = cnt["w"] % NW
                cnt["w"] += 1
                P.op("pool", lambda e, wi=wi, col0=col0: e.dma_start(out=wb[wi][:], in_=wv[:, :, col0:col0 + 512]),
                     writes=["wb%d" % wi], dma=True)
                if t < 2:
                    for hh in range(4):
                        h = nt * 4 + hh
                        qi = cnt["q"] % 3
                        cnt["q"] += 1
                        for tq in range(4):
                            tsl = slice(tq * 512, (tq + 1) * 512)
                            pi = cnt["gp"] % 4
                            cnt["gp"] += 1
                            for kc in range(KC):
                                P.op("pe", lambda e, pi=pi, wi=wi, kc=kc, hh=hh, tsl=tsl: e.matmul(
                                    gp[pi][:], wb[wi][:, kc, hh * 128:(hh + 1) * 128], xn[:, kc, tsl],
                                    start=(kc == 0), stop=(kc == KC - 1)),
                                    reads=["wb%d" % wi, xnkeys[kc]], writes=["gp%d" % pi])
                            sc = scale if t == 0 else 1.0
                            if cnt["ev"] % 2 == 0:
                                P.op("act", lambda e, pi=pi, qi=qi, tsl=tsl, sc=sc: e.activation(qst[qi][:, tsl], gp[pi][:], AF.Copy, scale=sc),
                                     reads=["gp%d" % pi], writes=["qst%d_%d" % (qi, tq)])
                            else:
                                P.op("dve", lambda e, pi=pi, qi=qi, tsl=tsl, sc=sc: e.tensor_scalar(qst[qi][:, tsl], gp[pi][:], sc, None, ALU.mult),
                                     reads=["gp%d" % pi], writes=["qst%d_%d" % (qi, tq)])
                            cnt["ev"] += 1
                        dst = q_d if t == 0 else k_d
                        P.op("sp", lambda e, qi=qi, dst=dst, g=g, h=h: e.dma_start(out=dst[g, h], in_=qst[qi][:]),
                             reads=["qst%d_%d" % (qi, x) for x in range(4)], dma=True)
                else:
                    vi = cnt["v"] % 2
                    cnt["v"] += 1
                    for blk in range(16):
                        bsl = blk_tok_slice(d, blk)
                        pi = cnt["gp"] % 4
                        cnt["gp"] += 1
                        for kc in range(KC):
                            P.op("pe", lambda e, pi=pi, wi=wi, kc=kc, bsl=bsl: e.matmul(
                                gp[pi][:], xn[:, kc, bsl], wb[wi][:, kc, :],
                                start=(kc == 0), stop=(kc == KC - 1)),
                                reads=["wb%d" % wi, xnkeys[kc]], writes=["gp%d" % pi])
                        if cnt["ev"] % 2 == 0:
                            P.op("act", lambda e, pi=pi, vi=vi, blk=blk: e.copy(vst[vi][:, blk, :], gp[pi][:]),
                                 reads=["gp%d" % pi], writes=["vst%d_%d" % (vi, blk)])
                        else:
                            P.op("dve", lambda e, pi=pi, vi=vi, blk=blk: e.tensor_copy(vst[vi][:, blk, :], gp[pi][:]),
                                 reads=["gp%d" % pi], writes=["vst%d_%d" % (vi, blk)])
                        cnt["ev"] += 1
                    for hh in range(4):
                        h = nt * 4 + hh
                        P.op("sp", lambda e, vi=vi, g=g, h=h, hh=hh: e.dma_start(out=v_d[g, h], in_=vst[vi][:, :, hh * 128:(hh + 1) * 128]),
                             reads=["vst%d_%d" % (vi, x) for x in range(16)], dma=True)
    return P.finish()


def run_qkv_stage(xT_cores, wqkv, g_mix):
    if "qkv" not in _cache:
        _cache["qkv"] = build_qkv_stage()
    nc = _cache["qkv"]
    gv = gvec_layout(g_mix)
    in_maps = [{"xT": xT_cores[c], "wqkv": wqkv, "gv": gv} for c in range(NCORES)]
    res = _run(nc, in_maps)
    return [(r["qT"], r["kT"], r["v"]) for r in res.results]


NEG = -30000.0


def attn_bias_tables(first):
    out = np.zeros((128, 48, 3, 128), np.float32)
    k = np.arange(128)[:, None].astype(np.float64)
    q = np.arange(128)[None, :].astype(np.float64)
    for g in range(3):
        for h in range(16):
            slope = 2.0 ** (-8.0 * (g * 16 + h + 1) / 48.0)
            c = slope * DIL[g]
            cur = np.where(k <= q, -c * (q - k), NEG)
            prev = np.where(k >= q, -c * (128 + q - k), NEG)
            out[:, g * 16 + h, 0, :] = prev
            out[:, g * 16 + h, 1, :] = cur
            out[:, g * 16 + h, 2, :] = NEG if first else prev
    return out


def build_attn_stage():
    P = Prog()
    q_d = P.din("qT", [3, 16, 128, TOK], BF16)
    k_d = P.din("kT", [3, 16, 128, 2 * TOK], BF16)
    v_d = P.din("v", [3, 16, 128, 32, 128], BF16)
    b_d = P.din("bias", [128, 48, 3, 128])
    id_d = P.din("identf", [128, 128])
    o_d = P.dout("oT", [D, TOK], BF16)
    bias = P.sb("bias_sb", [128, 48, 3, 128], F32)
    for i in range(4):
        P.op("sp", lambda e, i=i: e.dma_start(out=bias[:, 12 * i:12 * i + 12], in_=b_d[:, 12 * i:12 * i + 12]),
             writes=["bias%d" % i], dma=True)
    identf = P.sb("identf", [128, 128], F32)
    P.op("sp", lambda e: e.dma_start(out=identf[:], in_=id_d), writes=["identf"], dma=True)
    ones = P.sb("ones_b", [128, 128], BF16)
    P.op("dve", lambda e: e.memset(ones[:], 1.0), writes=["ones"])
    qh = [P.sb("qh%d" % i, [128, TOK], BF16) for i in range(2)]
    kh = [P.sb("kh%d" % i, [128, 2 * TOK], BF16) for i in range(2)]
    vh = [P.sb("vh%d" % i, [128, 32, 128], BF16) for i in range(2)]
    sps = [P.ps("sps%d" % i, [128, 4, 2, 128]) for i in range(2)]
    ups = [P.ps("ups%d" % i, [128, 512]) for i in range(2)]
    dps = [P.ps("dps%d" % i, [128, 512]) for i in range(2)]
    sbb = [P.sb("sbb%d" % i, [128, 4, 2, 128], F32) for i in range(2)]
    pT = [P.sb("pT%d" % i, [128, 4, 2, 128], BF16) for i in range(2)]
    accU = [P.sb("accU%d" % i, [128, TOK], F32) for i in range(2)]
    accD = [P.sb("accD%d" % i, [128, TOK], F32) for i in range(2)]
    rec = P.sb("rec", [128, TOK], F32)
    osb = [P.sb("osb%d" % i, [128, TOK], BF16) for i in range(2)]
    cnt = dict(ld=0, b=0)
    for h in range(16):
        ai = h % 2
        for g in range(3):
            d = DIL[g]
            J = TOK // (128 * d)
            halo = 128 * d
            li = cnt["ld"] % 2
            cnt["ld"] += 1
            P.op("sp", lambda e, li=li, g=g, h=h: e.dma_start(out=qh[li][:], in_=q_d[g, h]), writes=["qh%d" % li], dma=True)
            P.op("sp", lambda e, li=li, g=g, h=h, halo=halo: e.dma_start(out=kh[li][:, TOK - halo:], in_=k_d[g, h, :, TOK - halo:]),
                 writes=["kh%d" % li], dma=True)
            nb = 16 + d
            P.op("sp", lambda e, li=li, g=g, h=h, nb=nb: e.dma_start(out=vh[li][:, :nb, :], in_=v_d[g, h, :, :nb, :]),
                 writes=["vh%d" % li], dma=True)
            gh = g * 16 + h
            for bt in range(4):
                bi = cnt["b"] % 2
                cnt["b"] += 1
                blks = [divmod(4 * bt + i, J) for i in range(4)]
                for i, (r, j) in enumerate(blks):
                    qs = blk_tok_slice(d, 4 * bt + i)
                    for kb in range(2):
                        st = TOK + (j - 1 + kb) * 128 * d + r
                        ks = slice(st, st + 127 * d + 1, d) if d > 1 else slice(st, st + 128)
                        slot = kb if j > 0 else (2 - kb) if kb == 0 else 1
                        P.op("pe", lambda e, bi=bi, i=i, kb=kb, li=li, ks=ks, qs=qs: e.matmul(
                            sps[bi][:, i, kb, :], kh[li][:, ks], qh[li][:, qs], start=True, stop=False),
                            reads=["kh%d" % li, "qh%d" % li], writes=["sps%d" % bi])
                        P.op("pe", lambda e, bi=bi, i=i, kb=kb, gh=gh, slot=slot: e.matmul(
                            sps[bi][:, i, kb, :], identf[:], bias[:, gh, slot, :], start=False, stop=True),
                            reads=["identf", "bias%d" % (gh // 12)], writes=["sps%d" % bi])
                P.op("act", lambda e, bi=bi: e.activation(pT[bi][:], sps[bi][:], AF.Exp),
                     reads=["sps%d" % bi], writes=["pT%d" % bi, "sps%d" % bi])
                for i, (r, j) in enumerate(blks):
                    for kb in range(2):
                        vb = r * (J + 1) + (j + kb)
                        P.op("pe", lambda e, bi=bi, i=i, kb=kb, li=li, vb=vb: e.matmul(
                            ups[bi][:, i * 128:(i + 1) * 128], vh[li][:, vb, :], pT[bi][:, i, kb, :],
                            start=(kb == 0), stop=(kb == 1)),
                            reads=["vh%d" % li, "pT%d" % bi], writes=["ups%d" % bi])
                for i in range(4):
                    for kb in range(2):
                        P.op("pe", lambda e, bi=bi, i=i, kb=kb: e.matmul(
                            dps[bi][:, i * 128:(i + 1) * 128], ones[:], pT[bi][:, i, kb, :],
                            start=(kb == 0), stop=(kb == 1)),
                            reads=["ones", "pT%d" % bi], writes=["dps%d" % bi])
                def views(acc, psum):
                    if d == 1:
                        return acc[:, 512 * bt:512 * bt + 512], psum[:]
                    av = acc[:].rearrange("p (l r) -> p r l", r=d)
                    if d == 4:
                        return av[:, bt, :], psum[:]
                    return av[:, 4 * bt:4 * bt + 4, :], psum[:].rearrange("p (i l) -> p i l", i=4)
                for acc, psum, nm in ((accU[ai], ups[bi], "U"), (accD[ai], dps[bi], "D")):
                    av, pv = views(acc, psum)
                    pkey = ("ups%d" if nm == "U" else "dps%d") % bi
                    akey = "acc%s%d" % (nm, ai)
                    if g == 0:
                        P.op("dve", lambda e, av=av, pv=pv: e.tensor_copy(av, pv), reads=[pkey], writes=[akey])
                    else:
                        P.op("dve", lambda e, av=av, pv=pv: e.tensor_tensor(av, av, pv, ALU.add), reads=[pkey, akey], writes=[akey])
        P.op("dve", lambda e, ai=ai: e.reciprocal(rec[:], accD[ai][:]), reads=["accD%d" % ai], writes=["rec"])
        P.op("pool", lambda e, ai=ai: e.tensor_tensor(osb[ai][:], accU[ai][:], rec[:], ALU.mult),
             reads=["accU%d" % ai, "rec"], writes=["osb%d" % ai])
        P.op("sp", lambda e, ai=ai, h=h: e.dma_start(out=o_d[h * 128:(h + 1) * 128, :], in_=osb[ai][:]),
             reads=["osb%d" % ai], dma=True)
    return P.finish()


def attn_host_layout(qkv):
    maps = []
    for c in range(NCORES):
        qT, kT, v = qkv[c]
        first = (c % 4 == 0)
        kext = np.zeros((3, 16, 128, 2 * TOK), dtype=kT.dtype)
        kext[..., TOK:] = kT
        vext = np.zeros((3, 16, 128, 32, 128), dtype=v.dtype)
        for g in range(3):
            d = DIL[g]
            J = TOK // (128 * d)
            vg = v[g].reshape(16, 128, d, J, 128)
            ve = vext[g, :, :, :d * (J + 1), :].reshape(16, 128, d, J + 1, 128)
            ve[:, :, :, 1:, :] = vg
            if not first:
                pk, pv = qkv[c - 1][1], qkv[c - 1][2]
                kext[g, :, :, TOK - 128 * d:TOK] = pk[g, :, :, TOK - 128 * d:]
                ve[:, :, :, 0, :] = pv[g].reshape(16, 128, d, J, 128)[:, :, :, J - 1, :]
            vext[g, :, :, :d * (J + 1), :] = ve.reshape(16, 128, d * (J + 1), 128)
        maps.append({"qT": qT, "kT": kext, "v": vext, "bias": attn_bias_tables(first), "identf": np.eye(128, dtype=np.float32)})
    return maps


def run_attn_stage(qkv):
    import time
    t0 = time.time()
    if "attn" not in _cache:
        _cache["attn"] = build_attn_stage()
    t1 = time.time()
    maps = attn_host_layout(qkv)
    t2 = time.time()
    res = _run(_cache["attn"], maps)
    print("attn stage: build %.1f layout %.1f run %.1f" % (t1 - t0, t2 - t1, time.time() - t2), flush=True)
    return [r["oT"] for r in res.results]


def build_normproj_stage(n_out):
    P = Prog()
    xT_d = P.din("xT", [D, TOK])
    w_d = P.din("w", [D, n_out])
    g_d = P.din("gv", [128, KC])
    o_d = P.dout("uT", [n_out, TOK])
    C = Common(P)
    g_sb = P.sb("g_sb", [128, KC], F32)
    P.op("sp", lambda e: e.dma_start(out=g_sb[:], in_=g_d), writes=["gvec"], dma=True)
    xq = [P.sb("xq%d" % i, [128, KC, 256], F32) for i in range(2)]
    xn = P.sb("xn", [128, KC, TOK], BF16)
    load_norm_full(P, C, xT_d, g_sb, 0, xn, xq, 256)
    wb = [P.sb("wb%d" % i, [128, KC, 512], BF16) for i in range(2)]
    gp = [P.ps("gp%d" % i, [128, 512]) for i in range(4)]
    ost = [P.sb("ost%d" % i, [128, TOK], F32) for i in range(2)]
    wv = w_d.rearrange("(kc p) f -> p kc f", p=128)
    cnt = dict(gp=0, ev=0, o=0)
    for nt in range(n_out // 512):
        wi = nt % 2
        P.op("pool", lambda e, wi=wi, nt=nt: e.dma_start(out=wb[wi][:], in_=wv[:, :, nt * 512:(nt + 1) * 512]),
             writes=["wb%d" % wi], dma=True)
        for hh in range(4):
            n = nt * 4 + hh
            oi = cnt["o"] % 2
            cnt["o"] += 1
            for tq in range(4):
                tsl = slice(tq * 512, (tq + 1) * 512)
                pi = cnt["gp"] % 4
                cnt["gp"] += 1
                for kc in range(KC):
                    P.op("pe", lambda e, pi=pi, wi=wi, kc=kc, hh=hh, tsl=tsl: e.matmul(
                        gp[pi][:], wb[wi][:, kc, hh * 128:(hh + 1) * 128], xn[:, kc, tsl],
                        start=(kc == 0), stop=(kc == KC - 1)),
                        reads=["wb%d" % wi, "xn_%d" % kc], writes=["gp%d" % pi])
                if cnt["ev"] % 2 == 0:
                    P.op("act", lambda e, pi=pi, oi=oi, tsl=tsl: e.copy(ost[oi][:, tsl], gp[pi][:]),
                         reads=["gp%d" % pi], writes=["ost%d_%d" % (oi, tq)])
                else:
                    P.op("dve", lambda e, pi=pi, oi=oi, tsl=tsl: e.tensor_copy(ost[oi][:, tsl], gp[pi][:]),
                         reads=["gp%d" % pi], writes=["ost%d_%d" % (oi, tq)])
                cnt["ev"] += 1
            P.op("sp", lambda e, oi=oi, n=n: e.dma_start(out=o_d[n * 128:(n + 1) * 128, :], in_=ost[oi][:]),
                 reads=["ost%d_%d" % (oi, x) for x in range(4)], dma=True)
    return P.finish()


def run_normproj_stage(xT_cores, w, g_mix):
    key = ("normproj", w.shape[1])
    if key not in _cache:
        _cache[key] = build_normproj_stage(w.shape[1])
    gv = gvec_layout(g_mix)
    in_maps = [{"xT": xT_cores[c], "w": w, "gv": gv} for c in range(NCORES)]
    res = _run(_cache[key], in_maps)
    return [r["uT"] for r in res.results]


SEQ = 8192
S5_NT = 512
MAGIC = 12582912.0
TWO_PI = 6.283185307179586
C1 = 6.28125
C2 = TWO_PI - C1


def build_s5_stage():
    P = Prog()
    NT = S5_NT
    NTILE = SEQ // NT
    u_d = P.din("u", [512, SEQ])
    are_d = P.din("are", [128, 16])
    aim_d = P.din("aim", [128, 16])
    ldt_d = P.din("ldt", [128, 16])
    bre_d = P.din("bre", [128, 16, 128])
    bim_d = P.din("bim", [128, 16, 128])
    cre_d = P.din("cre", [128, 16, 128])
    cim_d = P.din("cim", [128, 16, 128])
    dv_d = P.din("dv", [128, 4])
    al_d = P.din("aloc", [128, NT])
    bl_d = P.din("bloc", [128, NT])
    y_d = P.dout("yT", [512, SEQ], BF16)

    def small(name, shape=(128, 16)):
        return P.sb(name, list(shape), F32)

    are, aim, ldt = small("are"), small("aim"), small("ldt")
    bre = P.sb("bre", [128, 16, 128], F32)
    bim = P.sb("bim", [128, 16, 128], F32)
    cre = P.sb("cre", [128, 16, 128], F32)
    cim = P.sb("cim", [128, 16, 128], F32)
    dv = small("dv", (128, 4))
    aloc = P.sb("aloc", [128, NT], F32)
    bloc = P.sb("bloc", [128, NT], F32)
    for t, dd, k in ((are, are_d, "are"), (aim, aim_d, "aim"), (ldt, ldt_d, "ldt"), (bre, bre_d, "bre"), (bim, bim_d, "bim"),
                     (cre, cre_d, "cre"), (cim, cim_d, "cim"), (dv, dv_d, "dv"), (aloc, al_d, "aloc"), (bloc, bl_d, "bloc")):
        P.op("sp", lambda e, t=t, dd=dd: e.dma_start(out=t[:], in_=dd), writes=[k], dma=True)

    names = ["dt", "th", "lr", "m", "tmp", "n", "r1", "thr", "s", "sh", "sq", "cs", "abr", "abi", "den", "rden", "zr",
             "t1", "t2", "cr", "ci", "ncr", "phi", "phr", "nci"]
    T = {n: small("p_" + n) for n in names}

    def dve(fn, reads, writes):
        P.op("dve", fn, reads=reads, writes=writes)

    def act(fn, reads, writes):
        P.op("act", fn, reads=reads, writes=writes)

    act(lambda e: e.activation(T["dt"][:], ldt[:], AF.Exp), ["ldt"], ["dt"])
    dve(lambda e: e.tensor_tensor(T["th"][:], T["dt"][:], aim[:], ALU.mult), ["dt", "aim"], ["th"])
    dve(lambda e: e.tensor_tensor(T["lr"][:], T["dt"][:], are[:], ALU.mult), ["dt", "are"], ["lr"])
    act(lambda e: e.activation(T["m"][:], T["lr"][:], AF.Exp), ["lr"], ["m"])

    def reduce_angle(src, dst, ksrc, kdst):
        dve(lambda e: e.tensor_scalar(T["tmp"][:], src[:], 1.0 / TWO_PI, MAGIC, ALU.mult, ALU.add), [ksrc], ["tmp"])
        dve(lambda e: e.tensor_scalar(T["n"][:], T["tmp"][:], MAGIC, None, ALU.subtract), ["tmp"], ["n"])
        dve(lambda e: e.scalar_tensor_tensor(T["r1"][:], T["n"][:], -C1, src[:], ALU.mult, ALU.add), ["n", ksrc], ["r1"])
        dve(lambda e: e.scalar_tensor_tensor(dst[:], T["n"][:], -C2, T["r1"][:], ALU.mult, ALU.add), ["n", "r1"], [kdst])

    reduce_angle(T["th"], T["thr"], "th", "thr")
    act(lambda e: e.activation(T["s"][:], T["thr"][:], AF.Sin), ["thr"], ["s"])
    act(lambda e: e.activation(T["sh"][:], T["thr"][:], AF.Sin, scale=0.5), ["thr"], ["sh"])
    act(lambda e: e.activation(T["sq"][:], T["sh"][:], AF.Square), ["sh"], ["sq"])
    act(lambda e: e.activation(T["cs"][:], T["sq"][:], AF.Identity, bias=1.0, scale=-2.0), ["sq"], ["cs"])
    dve(lambda e: e.tensor_tensor(T["abr"][:], T["m"][:], T["cs"][:], ALU.mult), ["m", "cs"], ["abr"])
    dve(lambda e: e.tensor_tensor(T["abi"][:], T["m"][:], T["s"][:], ALU.mult), ["m", "s"], ["abi"])
    dve(lambda e: e.tensor_tensor(T["t1"][:], are[:], are[:], ALU.mult), ["are"], ["t1"])
    dve(lambda e: e.tensor_tensor(T["t2"][:], aim[:], aim[:], ALU.mult), ["aim"], ["t2"])
    dve(lambda e: e.tensor_tensor(T["den"][:], T["t1"][:], T["t2"][:], ALU.add), ["t1", "t2"], ["den"])
    dve(lambda e: e.reciprocal(T["rden"][:], T["den"][:]), ["den"], ["rden"])
    dve(lambda e: e.tensor_scalar(T["zr"][:], T["abr"][:], -1.0, None, ALU.add), ["abr"], ["zr"])
    dve(lambda e: e.tensor_tensor(T["t1"][:], T["zr"][:], are[:], ALU.mult), ["zr", "are"], ["t1"])
    dve(lambda e: e.tensor_tensor(T["t2"][:], T["abi"][:], aim[:], ALU.mult), ["abi", "aim"], ["t2"])
    dve(lambda e: e.tensor_tensor(T["cr"][:], T["t1"][:], T["t2"][:], ALU.add), ["t1", "t2"], ["cr0"])
    dve(lambda e: e.tensor_tensor(T["cr"][:], T["cr"][:], T["rden"][:], ALU.mult), ["cr0", "rden"], ["cr"])
    dve(lambda e: e.tensor_tensor(T["t1"][:], T["abi"][:], are[:], ALU.mult), ["abi", "are", "cr0"], ["t1"])
    dve(lambda e: e.tensor_tensor(T["t2"][:], T["zr"][:], aim[:], ALU.mult), ["zr", "aim", "cr0"], ["t2"])
    dve(lambda e: e.tensor_tensor(T["ci"][:], T["t1"][:], T["t2"][:], ALU.subtract), ["t1", "t2"], ["ci0"])
    dve(lambda e: e.tensor_tensor(T["ci"][:], T["ci"][:], T["rden"][:], ALU.mult), ["ci0", "rden"], ["ci"])
    dve(lambda e: e.tensor_scalar(T["ncr"][:], T["cr"][:], -1.0, None, ALU.mult), ["cr"], ["ncr"])
    dve(lambda e: e.tensor_scalar(T["nci"][:], T["ci"][:], -1.0, None, ALU.mult), ["ci"], ["nci"])
    dve(lambda e: e.tensor_scalar(T["phi"][:], T["thr"][:], 64.0, None, ALU.mult), ["thr"], ["phi"])
    reduce_angle(T["phi"], T["phr"], "phi", "phr")
    off = P.sb("off", [128, NTILE, 16], F32)
    for tt in range(NTILE):
        dve(lambda e, tt=tt: e.tensor_scalar(off[:, tt, :], T["phr"][:], float(tt * (NT // 64)), None, ALU.mult), ["phr"], ["off"])
    ctr, cti = cre, cim
    nctr = P.sb("nctr", [128, 16, 128], F32)
    ctmp = P.sb("ctmp", [128, 128], F32)
    ctA = P.sb("ctA", [128, 128], F32)
    ctB = P.sb("ctB", [128, 128], F32)
    for pr in range(16):
        dve(lambda e, pr=pr: e.tensor_scalar(ctmp[:], cim[:, pr, :], T["nci"][:, pr:pr + 1], None, ALU.mult), ["cim", "nci"], ["ctmp"])
        dve(lambda e, pr=pr: e.scalar_tensor_tensor(ctA[:], cre[:, pr, :], T["cr"][:, pr:pr + 1], ctmp[:], ALU.mult, ALU.add),
            ["cre", "cr", "ctmp"], ["ctA"])
        dve(lambda e, pr=pr: e.tensor_scalar(ctmp[:], cim[:, pr, :], T["ncr"][:, pr:pr + 1], None, ALU.mult), ["cim", "ncr", "ctA"], ["ctmp"])
        dve(lambda e, pr=pr: e.scalar_tensor_tensor(ctB[:], cre[:, pr, :], T["nci"][:, pr:pr + 1], ctmp[:], ALU.mult, ALU.add),
            ["cre", "nci", "ctmp"], ["ctB"])
        dve(lambda e, pr=pr: e.tensor_copy(cre[:, pr, :], ctA[:]), ["ctA", "ctB"], ["cre"])
        dve(lambda e, pr=pr: e.tensor_copy(cim[:, pr, :], ctB[:]), ["ctB"], ["cim"])
        dve(lambda e, pr=pr: e.tensor_scalar(nctr[:, pr, :], ctA[:], -1.0, None, ALU.mult), ["ctA"], ["nctr"])
    for n_ in ("dl", "dlr", "sD", "shD", "sqD", "cD"):
        T[n_] = small("p_" + n_)
    dve(lambda e: e.tensor_scalar(T["dl"][:], T["phr"][:], float(NT // 64), None, ALU.mult), ["phr"], ["dl"])
    reduce_angle(T["dl"], T["dlr"], "dl", "dlr")
    act(lambda e: e.activation(T["sD"][:], T["dlr"][:], AF.Sin), ["dlr"], ["sD"])
    act(lambda e: e.activation(T["shD"][:], T["dlr"][:], AF.Sin, scale=0.5), ["dlr"], ["shD"])
    act(lambda e: e.activation(T["sqD"][:], T["shD"][:], AF.Square), ["shD"], ["sqD"])
    act(lambda e: e.activation(T["cD"][:], T["sqD"][:], AF.Identity, bias=1.0, scale=-2.0), ["sqD"], ["cD"])

    carry = P.sb("carry", [128, 16, 2], F32)
    dve(lambda e: e.memset(carry[:], 0.0), [], ["carry%d" % i for i in range(16)])

    uch = P.sb("uch", [128, SEQ], F32)

    def big(name):
        return P.sb(name, [128, NT], F32)

    SC = {n: big("c_" + n) for n in ("base", "t1b", "tmpb", "nb", "r1b", "red", "shb", "sqb", "p1", "p2", "p3", "p4", "zhr", "zhi")}
    tabc = [[big("tabc_%d_%d" % (pq, i)) for i in range(2)] for pq in range(4)]
    tabs = [[big("tabs_%d_%d" % (pq, i)) for i in range(2)] for pq in range(4)]
    PB = {n: [big("%s_%d" % (n, i)) for i in range(2)] for n in ("wr", "wi", "q1", "q2", "q3", "q4", "t1", "t2")}
    zps = [P.ps("zps%d" % i, [128, 512]) for i in range(4)]
    yps = [P.ps("yps%d" % i, [128, 512]) for i in range(4)]
    gl = {n: P.sb("gl_" + n, [128, 512], F32) for n in ("ypre", "sq", "t", "inner", "sg")}
    ysb = [P.sb("ysb%d" % i, [128, NT], BF16) for i in range(2)]
    assert NT == 512
    cnt = dict(z=0, it=0, y=0, o=0)
    GC = 2.0 * 0.7978845608028654
    for ch in range(4):
        for q in range(4):
            P.op("sp", lambda e, ch=ch, q=q: e.dma_start(out=uch[:, q * 2048:(q + 1) * 2048],
                                                         in_=u_d[ch * 128:(ch + 1) * 128, q * 2048:(q + 1) * 2048]),
                 writes=["uch%d" % q], dma=True)
        for tt in range(NTILE):
            tsl0 = tt * NT
            usl = slice(tsl0, tsl0 + NT)
            ukey = "uch%d" % (tsl0 // 2048)
            yb = cnt["y"] % 4
            cnt["y"] += 1
            tp = tt % 2
            for pq in range(4):
                pr = ch * 4 + pq
                it = cnt["it"] % 2
                cnt["it"] += 1
                b = {n: PB[n][it] for n in PB}
                k = {n: "%s_%d" % (n, it) for n in PB}
                cs_, sn_ = tabc[pq][tp], tabs[pq][tp]
                kc_, ks_ = "tabc_%d_%d" % (pq, tp), "tabs_%d_%d" % (pq, tp)
                if tt == 0:
                    dve(lambda e, pr=pr: e.tensor_scalar(SC["t1b"][:], bloc[:], T["thr"][:, pr:pr + 1], None, ALU.mult), ["bloc", "thr"], ["c_t1b"])
                    dve(lambda e, pr=pr: e.scalar_tensor_tensor(SC["base"][:], aloc[:], T["phr"][:, pr:pr + 1], SC["t1b"][:], ALU.mult, ALU.add),
                        ["aloc", "phr", "c_t1b"], ["c_base"])
                    dve(lambda e: e.tensor_scalar(SC["tmpb"][:], SC["base"][:], 1.0 / TWO_PI, MAGIC, ALU.mult, ALU.add), ["c_base"], ["c_tmpb"])
                    dve(lambda e: e.tensor_scalar(SC["nb"][:], SC["tmpb"][:], MAGIC, None, ALU.subtract), ["c_tmpb"], ["c_nb"])
                    dve(lambda e: e.scalar_tensor_tensor(SC["r1b"][:], SC["nb"][:], -C1, SC["base"][:], ALU.mult, ALU.add), ["c_nb", "c_base"], ["c_r1b"])
                    dve(lambda e: e.scalar_tensor_tensor(SC["red"][:], SC["nb"][:], -C2, SC["r1b"][:], ALU.mult, ALU.add), ["c_nb", "c_r1b"], ["c_red"])
                    act(lambda e, sn_=sn_: e.activation(sn_[:], SC["red"][:], AF.Sin), ["c_red"], [ks_])
                    act(lambda e: e.activation(SC["shb"][:], SC["red"][:], AF.Sin, scale=0.5), ["c_red"], ["c_shb"])
                    act(lambda e: e.activation(SC["sqb"][:], SC["shb"][:], AF.Square), ["c_shb"], ["c_sqb"])
                    act(lambda e, cs_=cs_: e.activation(cs_[:], SC["sqb"][:], AF.Identity, bias=1.0, scale=-2.0), ["c_sqb"], [kc_])
                else:
                    co, so = tabc[pq][1 - tp], tabs[pq][1 - tp]
                    kco, kso = "tabc_%d_%d" % (pq, 1 - tp), "tabs_%d_%d" % (pq, 1 - tp)
                    act(lambda e, b=b, so=so, pr=pr: e.activation(b["t1"][:], so[:], AF.Copy, scale=T["sD"][:, pr:pr + 1]), [kso, "sD"], [k["t1"]])
                    act(lambda e, b=b, co=co, pr=pr: e.activation(b["t2"][:], co[:], AF.Copy, scale=T["sD"][:, pr:pr + 1]), [kco, "sD"], [k["t2"]])
                    dve(lambda e, b=b, co=co, cs_=cs_, pr=pr: e.scalar_tensor_tensor(cs_[:], co[:], T["cD"][:, pr:pr + 1], b["t1"][:], ALU.mult, ALU.subtract),
                        [kco, "cD", k["t1"]], [kc_])
                    dve(lambda e, b=b, so=so, sn_=sn_, pr=pr: e.scalar_tensor_tensor(sn_[:], so[:], T["cD"][:, pr:pr + 1], b["t2"][:], ALU.mult, ALU.add),
                        [kso, "cD", k["t2"]], [ks_])
                zr_i = cnt["z"] % 4
                zi_i = (cnt["z"] + 1) % 4
                cnt["z"] += 2
                P.op("pe", lambda e, zr_i=zr_i, pr=pr, usl=usl: e.matmul(zps[zr_i][:], bre[:, pr, :], uch[:, usl], start=True, stop=True),
                     reads=["bre", ukey], writes=["zps%d" % zr_i])
                P.op("pe", lambda e, zi_i=zi_i, pr=pr, usl=usl: e.matmul(zps[zi_i][:], bim[:, pr, :], uch[:, usl], start=True, stop=True),
                     reads=["bim", ukey], writes=["zps%d" % zi_i])
                kzr, kzi = "zps%d" % zr_i, "zps%d" % zi_i
                dve(lambda e, zr_i=zr_i, cs_=cs_: e.tensor_tensor(SC["p1"][:], zps[zr_i][:], cs_[:], ALU.mult), [kzr, kc_], ["c_p1", kzr])
                dve(lambda e, zi_i=zi_i, sn_=sn_: e.tensor_tensor(SC["p2"][:], zps[zi_i][:], sn_[:], ALU.mult), [kzi, ks_], ["c_p2", kzi])
                dve(lambda e: e.tensor_tensor(SC["zhr"][:], SC["p1"][:], SC["p2"][:], ALU.add), ["c_p1", "c_p2"], ["c_zhr"])
                dve(lambda e, zi_i=zi_i, cs_=cs_: e.tensor_tensor(SC["p3"][:], zps[zi_i][:], cs_[:], ALU.mult), [kzi, kc_], ["c_p3", kzi])
                dve(lambda e, zr_i=zr_i, sn_=sn_: e.tensor_tensor(SC["p4"][:], zps[zr_i][:], sn_[:], ALU.mult), [kzr, ks_], ["c_p4", kzr])
                dve(lambda e: e.tensor_tensor(SC["zhi"][:], SC["p3"][:], SC["p4"][:], ALU.subtract), ["c_p3", "c_p4"], ["c_zhi"])
                mcol = T["m"][:, pr:pr + 1].to_broadcast([128, NT])
                dve(lambda e, pr=pr, mcol=mcol, b=b: e.tensor_tensor_scan(b["wr"][:], mcol, SC["zhr"][:], carry[:, pr, 0:1], ALU.mult, ALU.add),
                    ["m", "c_zhr", "carry%d" % pr], [k["wr"]])
                dve(lambda e, pr=pr, mcol=mcol, b=b: e.tensor_tensor_scan(b["wi"][:], mcol, SC["zhi"][:], carry[:, pr, 1:2], ALU.mult, ALU.add),
                    ["m", "c_zhi", "carry%d" % pr], [k["wi"]])
                act(lambda e, pr=pr, b=b: e.copy(carry[:, pr, 0:1], b["wr"][:, NT - 1:NT]), [k["wr"]], ["carry%d" % pr])
                act(lambda e, pr=pr, b=b: e.copy(carry[:, pr, 1:2], b["wi"][:, NT - 1:NT]), [k["wi"]], ["carry%d" % pr])
                dve(lambda e, b=b, cs_=cs_: e.tensor_tensor(b["q1"][:], b["wr"][:], cs_[:], ALU.mult), [k["wr"], kc_], [k["q1"]])
                dve(lambda e, b=b, sn_=sn_: e.tensor_tensor(b["q2"][:], b["wi"][:], sn_[:], ALU.mult), [k["wi"], ks_], [k["q2"]])
                dve(lambda e, b=b, cs_=cs_: e.tensor_tensor(b["q3"][:], b["wi"][:], cs_[:], ALU.mult), [k["wi"], kc_], [k["q3"]])
                dve(lambda e, b=b, sn_=sn_: e.tensor_tensor(b["q4"][:], b["wr"][:], sn_[:], ALU.mult), [k["wr"], ks_], [k["q4"]])
                for qi, (ct, qn) in enumerate(((ctr, "q1"), (nctr, "q2"), (cti, "q3"), (cti, "q4"))):
                    P.op("pe", lambda e, ct=ct, qn=qn, pr=pr, b=b, yb=yb, qi=qi, pq=pq: e.matmul(
                        yps[yb][:], ct[:, pr, :], b[qn][:], start=(pq == 0 and qi == 0), stop=(pq == 3 and qi == 3)),
                        reads=["cre", "cim", "nctr", k[qn]], writes=["yps%d" % yb])
            oi = cnt["o"] % 2
            cnt["o"] += 1
            dve(lambda e, usl=usl, ch=ch, yb=yb: e.scalar_tensor_tensor(gl["ypre"][:], uch[:, usl], dv[:, ch:ch + 1], yps[yb][:], ALU.mult, ALU.add),
                [ukey, "dv", "yps%d" % yb], ["g_ypre", "yps%d" % yb])
            act(lambda e: e.activation(gl["sq"][:], gl["ypre"][:], AF.Square), ["g_ypre"], ["g_sq"])
            act(lambda e: e.activation(gl["t"][:], gl["sq"][:], AF.Identity, bias=1.0, scale=0.044715), ["g_sq"], ["g_t"])
            dve(lambda e: e.tensor_tensor(gl["inner"][:], gl["t"][:], gl["ypre"][:], ALU.mult), ["g_t", "g_ypre"], ["g_inner"])
            act(lambda e: e.activation(gl["sg"][:], gl["inner"][:], AF.Sigmoid, scale=GC), ["g_inner"], ["g_sg"])
            dve(lambda e, oi=oi: e.tensor_tensor(ysb[oi][:], gl["ypre"][:], gl["sg"][:], ALU.mult), ["g_ypre", "g_sg"], ["ysb%d" % oi])
            P.op("sp", lambda e, oi=oi, ch=ch, tsl0=tsl0: e.dma_start(out=y_d[ch * 128:(ch + 1) * 128, tsl0:tsl0 + NT], in_=ysb[oi][:]),
                 reads=["ysb%d" % oi], dma=True)
    return P.finish()


def s5_host_layout(uT_cores, log_dt, a_re, a_im, b_re, b_im, c_re, c_im, dskip):
    NT = S5_NT
    loc = np.arange(NT)
    aloc = np.broadcast_to((loc // 64).astype(np.float32), (128, NT)).copy()
    bloc = np.broadcast_to((loc % 64).astype(np.float32), (128, NT)).copy()
    maps = []
    for c in range(NCORES):
        seq, gq = divmod(c, 4)
        u = np.concatenate([uT_cores[4 * seq + i][512 * gq:512 * gq + 512, :] for i in range(4)], axis=1)
        gs = np.arange(32 * gq, 32 * gq + 32).reshape(16, 2)
        are = a_re[gs].transpose(1, 2, 0).reshape(128, 16)
        aim = a_im[gs].transpose(1, 2, 0).reshape(128, 16)
        ldt = np.repeat(log_dt[gs].transpose(1, 0)[:, None, :], 64, axis=1).reshape(128, 16)
        bre = np.zeros((128, 16, 128), np.float32)
        bim = np.zeros((128, 16, 128), np.float32)
        cre = np.zeros((128, 16, 128), np.float32)
        cim = np.zeros((128, 16, 128), np.float32)
        for pr in range(16):
            for g2 in range(2):
                g = gs[pr, g2]
                r0 = (pr % 4) * 32 + g2 * 16
                bre[r0:r0 + 16, pr, g2 * 64:(g2 + 1) * 64] = b_re[g].T
                bim[r0:r0 + 16, pr, g2 * 64:(g2 + 1) * 64] = b_im[g].T
                cre[g2 * 64:(g2 + 1) * 64, pr, r0:r0 + 16] = c_re[g].T
                cim[g2 * 64:(g2 + 1) * 64, pr, r0:r0 + 16] = c_im[g].T
        dv = dskip[32 * gq:32 * gq + 32].reshape(4, 128).T
        maps.append(dict(u=np.ascontiguousarray(u), are=np.ascontiguousarray(are), aim=np.ascontiguousarray(aim),
                         ldt=np.ascontiguousarray(ldt), bre=bre, bim=bim, cre=cre, cim=cim,
                         dv=np.ascontiguousarray(dv), aloc=aloc, bloc=bloc))
    return maps


def run_s5_stage(uT_cores, log_dt, a_re, a_im, b_re, b_im, c_re, c_im, dskip):
    if "s5" not in _cache:
        _cache["s5"] = build_s5_stage()
    maps = s5_host_layout(uT_cores, log_dt, a_re, a_im, b_re, b_im, c_re, c_im, dskip)
    res = _run(_cache["s5"], maps)
    ys = [r["yT"] for r in res.results]
    out = []
    for c in range(NCORES):
        seq, i = divmod(c, 4)
        out.append(np.ascontiguousarray(np.concatenate([ys[4 * seq + gq][:, i * TOK:(i + 1) * TOK] for gq in range(4)], axis=0)))
    return out


def build_rwkv_proj_stage():
    P = Prog()
    TT = 1024
    PW = 128
    xT_d = P.din("xT", [D, TOK])
    xp_d = P.din("xprev", [D, 1])
    g_d = P.din("gv", [128, KC])
    mu_d = P.din("mu", [128, 6 * KC])
    wrkv_d = P.din("wrkv", [3, D, D])
    w1_d = P.din("w1", [D, 96])
    w2_d = P.din("w2", [96, D])
    a1_d = P.din("a1", [D, 96])
    a2_d = P.din("a2", [96, D])
    g1_d = P.din("g1", [D, 256])
    g2_d = P.din("g2", [256, D])
    w0_d = P.din("w0", [128, KC])
    a0_d = P.din("a0", [128, KC])
    outs = {n: P.dout(n, [D, TOK]) for n in ("r", "k", "v", "ld", "a", "g")}
    C = Common(P)
    g_sb = P.sb("g_sb", [128, KC], F32)
    mu_sb = P.sb("mu_sb", [128, 6 * KC], F32)
    w0_sb = P.sb("w0_sb", [128, KC], F32)
    a0_sb = P.sb("a0_sb", [128, KC], F32)
    P.op("sp", lambda e: e.dma_start(out=g_sb[:], in_=g_d), writes=["gvec"], dma=True)
    P.op("sp", lambda e: e.dma_start(out=mu_sb[:], in_=mu_d), writes=["mu"], dma=True)
    P.op("sp", lambda e: e.dma_start(out=w0_sb[:], in_=w0_d), writes=["w0"], dma=True)
    P.op("sp", lambda e: e.dma_start(out=a0_sb[:], in_=a0_d), writes=["a0"], dma=True)
    w1_sb = P.sb("w1_sb", [128, KC, 96], BF16)
    a1_sb = P.sb("a1_sb", [128, KC, 96], BF16)
    g1_sb = P.sb("g1_sb", [128, KC, 256], BF16)
    w2_sb = P.sb("w2_sb", [96, D], BF16)
    a2_sb = P.sb("a2_sb", [96, D], BF16)
    g2_sb = P.sb("g2_sb", [128, 2, D], BF16)
    P.op("pool", lambda e: e.dma_start(out=w1_sb[:], in_=w1_d.rearrange("(kc p) f -> p kc f", p=128)), writes=["w1"], dma=True)
    P.op("pool", lambda e: e.dma_start(out=a1_sb[:], in_=a1_d.rearrange("(kc p) f -> p kc f", p=128)), writes=["a1"], dma=True)
    P.op("pool", lambda e: e.dma_start(out=g1_sb[:], in_=g1_d.rearrange("(kc p) f -> p kc f", p=128)), writes=["g1"], dma=True)
    P.op("pool", lambda e: e.dma_start(out=w2_sb[:], in_=w2_d), writes=["w2"], dma=True)
    P.op("pool", lambda e: e.dma_start(out=a2_sb[:], in_=a2_d), writes=["a2"], dma=True)
    P.op("pool", lambda e: e.dma_start(out=g2_sb[:], in_=g2_d.rearrange("(c p) f -> p c f", p=128)), writes=["g2"], dma=True)

    xq = [P.sb("xq%d" % i, [128, KC, PW + 1], F32) for i in range(2)]
    hf = P.sb("hf", [128, KC, PW + 1], F32)
    h_sb = P.sb("h_sb", [128, KC, TT], BF16)
    xx_sb = P.sb("xx_sb", [128, KC, TT], BF16)
    xj = [P.sb("xj0", [128, KC, TT], BF16)]
    wb = [P.sb("wb%d" % i, [128, KC, 512], BF16) for i in range(2)]
    gp = [P.ps("gp%d" % i, [128, 512]) for i in range(4)]
    lp = [P.ps("lp%d" % i, [128, 512]) for i in range(2)]
    ost = [P.sb("ost%d" % i, [128, TT], F32) for i in range(2)]
    lo = [P.sb("lo0", [128, 2, TT], BF16)]
    xv = xT_d.rearrange("(kc p) t -> p kc t", p=128)
    xpv = xp_d.rearrange("(kc p) t -> p kc t", p=128)
    cnt = dict(x=0, gp=0, lp=0, ev=0, o=0, w=0, j=0, lo=0)
    NPC = TT // PW
    for ps_ in range(TOK // TT):
        t0 = ps_ * TT
        for pc in range(NPC):
            tp = t0 + pc * PW
            b = cnt["x"] % 2
            cnt["x"] += 1
            keys = ["xq%d_%d" % (b, k) for k in range(KC)]
            if tp == 0:
                P.op("sp", lambda e, b=b: e.dma_start(out=xq[b][:, :, 0:1], in_=xpv, allow_slow_non_contiguous=True), writes=keys, dma=True)
                P.op("sp", lambda e, b=b: e.dma_start(out=xq[b][:, :, 1:], in_=xv[:, :, 0:PW]), writes=keys, dma=True)
            else:
                P.op("sp", lambda e, b=b, tp=tp: e.dma_start(out=xq[b][:], in_=xv[:, :, tp - 1:tp + PW]), writes=keys, dma=True)
            N1 = PW + 1
            for kc in range(KC):
                j = C.nsq % 2
                C.nsq += 1
                P.op("act", lambda e, j=j, kc=kc, b=b: e.activation(C.sq[j][:, :N1], xq[b][:, kc, :], AF.Square),
                     reads=[keys[kc]], writes=["sq%d" % j])
                P.op("pe", lambda e, j=j, kc=kc: e.matmul(C.stat[:, :N1], C.ones[:], C.sq[j][:, :N1], start=(kc == 0), stop=(kc == KC - 1)),
                     reads=["ones", "sq%d" % j], writes=["stat"])
            P.op("act", lambda e: e.activation(C.std[:, :N1], C.stat[:, :N1], AF.Sqrt, bias=EPS, scale=1.0 / D), reads=["stat"], writes=["std"])
            P.op("dve", lambda e: e.reciprocal(C.rstd[:, :N1], C.std[:, :N1]), reads=["std"], writes=["rstd"])
            psl = slice(pc * PW, (pc + 1) * PW)
            for kc in range(KC):
                P.op("dve", lambda e, kc=kc, b=b: e.scalar_tensor_tensor(
                    hf[:, kc, :], xq[b][:, kc, :], g_sb[:, kc:kc + 1], C.rstd[:, :N1], ALU.mult, ALU.mult),
                    reads=[keys[kc], "rstd", "gvec"], writes=["hf_%d" % kc])
                P.op("act", lambda e, kc=kc, psl=psl: e.copy(h_sb[:, kc, psl], hf[:, kc, 1:]), reads=["hf_%d" % kc], writes=["h_%d" % kc])
                P.op("dve", lambda e, kc=kc, psl=psl: e.tensor_tensor(xx_sb[:, kc, psl], hf[:, kc, 0:PW], hf[:, kc, 1:], ALU.subtract),
                     reads=["hf_%d" % kc], writes=["xx_%d" % kc])

        def make_xj(j):
            ji = 0
            cnt["j"] += 1
            for kc in range(KC):
                P.op("dve", lambda e, kc=kc, ji=ji, j=j: e.scalar_tensor_tensor(
                    xj[ji][:, kc, :], xx_sb[:, kc, :], mu_sb[:, j * KC + kc:j * KC + kc + 1], h_sb[:, kc, :], ALU.mult, ALU.add),
                    reads=["xx_%d" % kc, "h_%d" % kc, "mu"], writes=["xj%d_%d" % (ji, kc)])
            return ji

        def big_proj(ji, w_dram, name, post):
            wv = w_dram.rearrange("(kc p) f -> p kc f", p=128)
            for nt in range(4):
                wi = cnt["w"] % 2
                cnt["w"] += 1
                P.op("pool", lambda e, wi=wi, nt=nt: e.dma_start(out=wb[wi][:], in_=wv[:, :, nt * 512:(nt + 1) * 512]),
                     writes=["wb%d" % wi], dma=True)
                for hh in range(4):
                    n = nt * 4 + hh
                    oi = cnt["o"] % 2
                    cnt["o"] += 1
                    for th in range(TT // 512):
                        tsl = slice(th * 512, (th + 1) * 512)
                        pi = cnt["gp"] % 4
                        cnt["gp"] += 1
                        for kc in range(KC):
                            P.op("pe", lambda e, pi=pi, wi=wi, kc=kc, hh=hh, tsl=tsl, ji=ji: e.matmul(
                                gp[pi][:], wb[wi][:, kc, hh * 128:(hh + 1) * 128], xj[ji][:, kc, tsl],
                                start=(kc == 0), stop=(kc == KC - 1)),
                                reads=["wb%d" % wi, "xj%d_%d" % (ji, kc)], writes=["gp%d" % pi])
                        post(gp[pi], "gp%d" % pi, ost[oi][:, tsl], "ost%d_%d" % (oi, th), n)
                    P.op("sp", lambda e, oi=oi, n=n, name=name, t0=t0: e.dma_start(out=outs[name][n * 128:(n + 1) * 128, t0:t0 + TT], in_=ost[oi][:]),
                         reads=["ost%d_%d" % (oi, x) for x in range(TT // 512)], dma=True)

        def post_copy(ps, pkey, dst, dkey, n):
            if cnt["ev"] % 2 == 0:
                P.op("act", lambda e: e.copy(dst, ps[:]), reads=[pkey], writes=[dkey])
            else:
                P.op("dve", lambda e: e.tensor_copy(dst, ps[:]), reads=[pkey], writes=[dkey])
            cnt["ev"] += 1

        for j, name in ((0, "r"), (1, "k"), (2, "v")):
            ji = make_xj(j)
            big_proj(ji, wrkv_d[j], name, post_copy)

        def lora(ji, l1_sb, l1key, nl, func, l2, l2key, name, post):
            li = 0
            cnt["lo"] += 1
            nlc = (nl + 127) // 128
            for c in range(nlc):
                m = min(128, nl - c * 128)
                for th in range(TT // 512):
                    tsl = slice(th * 512, (th + 1) * 512)
                    pi = cnt["lp"] % 2
                    cnt["lp"] += 1
                    for kc in range(KC):
                        P.op("pe", lambda e, pi=pi, kc=kc, c=c, m=m, tsl=tsl, ji=ji: e.matmul(
                            lp[pi][:m, :], l1_sb[:, kc, c * 128:c * 128 + m], xj[ji][:, kc, tsl],
                            start=(kc == 0), stop=(kc == KC - 1)),
                            reads=[l1key, "xj%d_%d" % (ji, kc)], writes=["lp%d" % pi])
                    P.op("act", lambda e, pi=pi, li=li, c=c, m=m, tsl=tsl: e.activation(lo[li][:m, c, tsl], lp[pi][:m, :], func),
                         reads=["lp%d" % pi], writes=["lo%d" % li])
            for n in range(KC):
                oi = cnt["o"] % 2
                cnt["o"] += 1
                for th in range(TT // 512):
                    tsl = slice(th * 512, (th + 1) * 512)
                    pi = cnt["gp"] % 4
                    cnt["gp"] += 1
                    for c in range(nlc):
                        m = min(128, nl - c * 128)
                        lhs = l2[:m, c, n * 128:(n + 1) * 128] if nlc > 1 else l2[:m, n * 128:(n + 1) * 128]
                        P.op("pe", lambda e, pi=pi, lhs=lhs, li=li, c=c, m=m, tsl=tsl: e.matmul(
                            gp[pi][:], lhs, lo[li][:m, c, tsl], start=(c == 0), stop=(c == nlc - 1)),
                            reads=[l2key, "lo%d" % li], writes=["gp%d" % pi])
                    post(gp[pi], "gp%d" % pi, ost[oi][:, tsl], "ost%d_%d" % (oi, th), n)
                P.op("sp", lambda e, oi=oi, n=n, name=name, t0=t0: e.dma_start(out=outs[name][n * 128:(n + 1) * 128, t0:t0 + TT], in_=ost[oi][:]),
                     reads=["ost%d_%d" % (oi, x) for x in range(TT // 512)], dma=True)

        def post_ld(ps, pkey, dst, dkey, n):
            P.op("act", lambda e: e.activation(dst, ps[:], AF.Sigmoid, bias=w0_sb[:, n:n + 1], scale=1.0), reads=[pkey, "w0"], writes=[dkey])
            P.op("dve", lambda e: e.tensor_scalar(dst, dst, -0.6065306597126334, None, ALU.mult), reads=[dkey], writes=[dkey])

        def post_a(ps, pkey, dst, dkey, n):
            P.op("act", lambda e: e.activation(dst, ps[:], AF.Sigmoid, bias=a0_sb[:, n:n + 1], scale=1.0), reads=[pkey, "a0"], writes=[dkey])

        ji = make_xj(3)
        lora(ji, w1_sb, "w1", 96, AF.Tanh, w2_sb, "w2", "ld", post_ld)
        ji = make_xj(4)
        lora(ji, a1_sb, "a1", 96, AF.Copy, a2_sb, "a2", "a", post_a)
        ji = make_xj(5)
        lora(ji, g1_sb, "g1", 256, AF.Sigmoid, g2_sb, "g2", "g", post_copy)
    return P.finish()


def run_rwkv_proj_stage(xT_cores, inp, L, ic):
    if "rwkvp" not in _cache:
        _cache["rwkvp"] = build_rwkv_proj_stage()
    gv = gvec_layout(inp["norm_mix"][L])
    mu = gvec_layout(*[inp["rwkv_mu"][ic][j] for j in range(6)])
    maps = []
    for c in range(NCORES):
        if c % 4 == 0:
            xprev = np.zeros((D, 1), np.float32)
        else:
            xprev = np.ascontiguousarray(xT_cores[c - 1][:, TOK - 1:TOK])
        maps.append(dict(xT=xT_cores[c], xprev=xprev, gv=gv, mu=mu, wrkv=inp["rwkv_w_rkv"][ic], w1=inp["rwkv_w1"][ic],
                         w2=inp["rwkv_w2"][ic], a1=inp["rwkv_a1"][ic], a2=inp["rwkv_a2"][ic], g1=inp["rwkv_g1"][ic],
                         g2=inp["rwkv_g2"][ic], w0=gvec_layout(inp["rwkv_w0"][ic]), a0=gvec_layout(inp["rwkv_a0"][ic])))
    res = _run(_cache["rwkvp"], maps)
    return [{n: r[n] for n in ("r", "k", "v", "ld", "a", "g")} for r in res.results]


RW_ST = 256
GN_EPS = 64 * 1e-5


def build_rwkv_core_stage():
    import os
    P = Prog()
    ST = RW_ST
    NST = SEQ // ST
    NCH = ST // 64
    ins_d = {n: P.din(n, [512, SEQ]) for n in ("r", "k", "v", "ld", "a", "g")}
    par_d = P.din("par", [128, 4, 5])
    id_d = P.din("ident", [128, 128])
    ob_d = P.din("onesbd", [128, 128])
    mk_d = P.din("mask320", [128, 320])
    is_d = P.din("identst", [128, 64])
    rm_d = P.din("resetm", [128, ST])
    o_d = P.dout("oT", [512, SEQ], BF16)
    par = P.sb("par", [128, 4, 5])
    ident = P.sb("ident", [128, 128])
    onesbd = P.sb("onesbd", [128, 128])
    mask = P.sb("mask320", [128, 320])
    identst = P.sb("identst", [128, 64])
    resetm = P.sb("resetm", [128, ST])
    for t, dd, k in ((par, par_d, "par"), (ident, id_d, "ident"), (onesbd, ob_d, "onesbd"), (mask, mk_d, "mask"),
                     (identst, is_d, "identst"), (resetm, rm_d, "resetm")):
        P.op("sp", lambda e, t=t, dd=dd: e.dma_start(out=t[:], in_=dd), writes=[k], dma=True)
    banks = [P.ps("bk%d" % i, [128, 512]) for i in range(8)]
    bkA = [banks[2 * hp] for hp in range(4)]
    bkB = [banks[2 * hp + 1] for hp in range(4)]
    Gps = [bkA[hp][:, 0:320] for hp in range(4)]
    Tps = [bkA[hp][:, 320:512] for hp in range(4)]
    sqp = [bkB[hp][:, 0:128] for hp in range(4)]
    Pps = [bkB[hp][:, 128:192] for hp in range(4)]
    Wps = [bkB[hp][:, 192:256] for hp in range(4)]
    Ups = [bkB[hp][:, 256:320] for hp in range(4)]
    Yps = [bkB[hp][:, 320:384] for hp in range(4)]
    Sps = [bkB[hp][:, 384:448] for hp in range(4)]
    Ytr = [bkB[hp][:, 448:512] for hp in range(4)]
    KA = ["bkA%d" % hp for hp in range(4)]
    KB = ["bkB%d" % hp for hp in range(4)]

    def T_(name, w, dt=F32):
        return P.sb(name, [128, w], dt)

    inb = {n: [[T_("in_%s_%d_%d" % (n, hp, b), ST) for b in range(2)] for hp in range(4)] for n in ("r", "k", "v", "ld", "a", "g")}
    prep = {n: [[T_("pp_%s_%d_%d" % (n, hp, b), ST) for b in range(2)] for hp in range(4)] for n in ("At", "Rt", "Bt", "Kt", "gam", "bonus")}
    S = [[T_("S_%d_%d" % (hp, b), 64) for b in range(2)] for hp in range(4)]
    for hp in range(4):
        P.op("dve", lambda e, hp=hp: e.memset(S[hp][0][:], 0.0), writes=["S%d_0" % hp])
    sc = {n: T_("sc_" + n, ST) for n in ("cs", "csm", "gm1", "ig", "kk", "kksq", "nrm", "rn", "kkn", "t", "kp", "bv", "rk",
                                          "y", "ysq", "mean", "msq", "var", "std", "rstd", "yc", "yn", "z", "z2")}
    osb = [T_("osb%d" % i, ST, BF16) for i in range(2)]
    Gm = [T_("Gm%d" % hp, 320) for hp in range(4)]
    TTs = [T_("TTs%d" % hp, 192) for hp in range(4)]
    Nn = [[T_("Nn%d_%d" % (hp, b), 128) for b in range(2)] for hp in range(4)]
    Pm = [[T_("Pm%d_%d" % (hp, b), 64) for b in range(2)] for hp in range(4)]
    Wsb = [T_("Wsb%d" % hp, 64) for hp in range(4)]
    Usb = [T_("Usb%d" % hp, 64) for hp in range(4)]
    Ysb = [T_("Ysb%d" % hp, 64) for hp in range(4)]
    ycm = [[T_("ycm%d_%d" % (hp, i), ST) for i in range(2)] for hp in range(4)]
    cnt = dict(ev=0, o=0, g=0, sq=0, pp=0, y=0, s=0)
    HS = [slice(0, 64), slice(64, 128)]

    def evac(dst, src, reads, writes, eng=None):
        if eng is None:
            eng = "act" if cnt["ev"] % 2 == 0 else "dve"
            cnt["ev"] += 1
        if eng == "act":
            P.op("act", lambda e: e.copy(dst, src), reads=reads, writes=writes)
        else:
            P.op("dve", lambda e: e.tensor_copy(dst, src), reads=reads, writes=writes)

    def do_prep(hp, st):
        b = st % 2
        t0 = st * ST
        I = {n: inb[n][hp][b] for n in inb}
        Pp = {n: prep[n][hp][b] for n in prep}
        ik = lambda n: "in_%s_%d_%d" % (n, hp, b)
        pk = lambda n: "pp_%s_%d_%d" % (n, hp, b)
        for n in ("r", "k", "v", "ld", "a", "g"):
            P.op("sp", lambda e, n=n, t=I[n]: e.dma_start(out=t[:], in_=ins_d[n][hp * 128:(hp + 1) * 128, t0:t0 + ST]),
                 writes=[ik(n)], dma=True)
        col = lambda j: par[:, hp, j:j + 1]
        dve = lambda fn, r, w: P.op("dve", fn, reads=r, writes=w)
        act = lambda fn, r, w: P.op("act", fn, reads=r, writes=w)
        pool = lambda fn, r, w: P.op("dve", fn, reads=r, writes=w)
        dve(lambda e: e.tensor_tensor_scan(sc["cs"][:], resetm[:], I["ld"][:], 0.0, ALU.mult, ALU.add), [ik("ld"), "resetm"], ["cs"])
        act(lambda e: e.activation(Pp["gam"][:], sc["cs"][:], AF.Exp), ["cs"], [pk("gam")])
        pool(lambda e: e.tensor_tensor(sc["csm"][:], sc["cs"][:], I["ld"][:], ALU.subtract), ["cs", ik("ld")], ["csm"])
        act(lambda e: e.activation(sc["gm1"][:], sc["csm"][:], AF.Exp), ["csm"], ["gm1"])
        act(lambda e: e.activation(sc["ig"][:], sc["cs"][:], AF.Exp, scale=-1.0), ["cs"], ["ig"])
        pool(lambda e: e.tensor_scalar(sc["kk"][:], I["k"][:], col(0), None, ALU.mult), [ik("k"), "par"], ["kk"])
        act(lambda e: e.activation(sc["kksq"][:], sc["kk"][:], AF.Square), ["kk"], ["kksq"])
        P.op("pe", lambda e: e.matmul(bkA[hp][:, 0:256], onesbd[:], sc["kksq"][:], start=True, stop=True), reads=["onesbd", "kksq"], writes=[KA[hp]])
        act(lambda e: e.activation(sc["nrm"][:], bkA[hp][:, 0:256], AF.Sqrt), [KA[hp]], ["nrm", KA[hp]])
        dve(lambda e: e.tensor_scalar(sc["nrm"][:], sc["nrm"][:], 1e-12, None, ALU.max), ["nrm"], ["nrm"])
        dve(lambda e: e.reciprocal(sc["rn"][:], sc["nrm"][:]), ["nrm"], ["rn"])
        pool(lambda e: e.tensor_tensor(sc["kkn"][:], sc["kk"][:], sc["rn"][:], ALU.mult), ["kk", "rn"], ["kkn"])
        dve(lambda e: e.tensor_scalar(sc["t"][:], I["a"][:], col(1), col(1), ALU.mult, ALU.subtract), [ik("a"), "par"], ["t"])
        dve(lambda e: e.scalar_tensor_tensor(sc["kp"][:], sc["t"][:], 1.0, I["k"][:], ALU.add, ALU.mult), ["t", ik("k")], ["kp"])
        dve(lambda e: e.scalar_tensor_tensor(Pp["At"][:], sc["gm1"][:], -1.0, sc["kkn"][:], ALU.mult, ALU.mult), ["gm1", "kkn"], [pk("At")])
        pool(lambda e: e.tensor_tensor(Pp["Rt"][:], Pp["gam"][:], I["r"][:], ALU.mult), [pk("gam"), ik("r")], [pk("Rt")])
        pool(lambda e: e.tensor_tensor(sc["bv"][:], sc["kkn"][:], I["a"][:], ALU.mult), ["kkn", ik("a")], ["bv"])
        pool(lambda e: e.tensor_tensor(Pp["Bt"][:], sc["bv"][:], sc["ig"][:], ALU.mult), ["bv", "ig"], [pk("Bt")])
        dve(lambda e: e.tensor_tensor(Pp["Kt"][:], sc["kp"][:], sc["ig"][:], ALU.mult), ["kp", "ig"], [pk("Kt")])
        dve(lambda e: e.scalar_tensor_tensor(sc["rk"][:], I["r"][:], col(2), sc["kp"][:], ALU.mult, ALU.mult), [ik("r"), "par", "kp"], ["rk"])
        P.op("pe", lambda e: e.matmul(bkA[hp][:, 256:512], onesbd[:], sc["rk"][:], start=True, stop=True), reads=["onesbd", "rk"], writes=[KA[hp]])
        dve(lambda e: e.tensor_tensor(Pp["bonus"][:], bkA[hp][:, 256:512], I["v"][:], ALU.mult), [KA[hp], ik("v")], [pk("bonus"), KA[hp]])

    pending_y = {}

    def chunk_stages(hp, st, ch):
        b = st % 2
        sl = slice(ch * 64, (ch + 1) * 64)
        Pp = {n: prep[n][hp][b] for n in prep}
        pk = lambda n: "pp_%s_%d_%d" % (n, hp, b)
        vt = inb["v"][hp][b]
        vk = "in_v_%d_%d" % (hp, b)
        cidx = st * NCH + ch
        sp_, sn_ = S[hp][cidx % 2], S[hp][(cidx + 1) % 2]
        skp, skn = "S%d_%d" % (hp, cidx % 2), "S%d_%d" % (hp, (cidx + 1) % 2)
        gi = hp
        gk, tk = KA[hp], KA[hp]
        stages = []

        def both(f, g):
            def h():
                if f is not None:
                    f()
                if g is not None:
                    g()
            return h

        def mm(out, lhsT, rhs, start, stop, reads, writes):
            P.op("pe", lambda e: e.matmul(out, lhsT, rhs, start=start, stop=stop), reads=reads, writes=writes)

        def st_gram():
            for h2 in range(2):
                hs = HS[h2]
                pairs = (("Bt", "At"), ("At", "Bt"), ("Kt", "At"), ("Bt", "Rt"), ("Kt", "Rt"))
                for i, (l, r) in enumerate(pairs):
                    mm(Gps[gi][hs, 64 * i:64 * i + 64], Pp[l][hs, sl], Pp[r][hs, sl], True, True, [pk(l), pk(r)], [gk])
                for i, (t, k) in enumerate(((Pp["Bt"], pk("Bt")), (Pp["Kt"], pk("Kt")), (vt, vk))):
                    P.op("pe", lambda e, i=i, t=t, hs=hs, h2=h2: e.matmul(Tps[gi][hs, 64 * i:64 * i + 64], t[hs, sl], ident[hs, 64 * h2:64 * h2 + 64], start=True, stop=True),
                         reads=[k, "ident"], writes=[tk])
        pend_y = pending_y.get(hp)
        stages.append(both(st_gram, pend_y[0] if pend_y else None))

        def st_gevac():
            P.op("dve", lambda e: e.tensor_tensor(Gm[hp][:], Gps[gi], mask[:], ALU.mult), reads=[gk, "mask"], writes=["Gm%d" % hp, gk])
            P.op("act", lambda e: e.copy(TTs[hp][:], Tps[gi]), reads=[tk], writes=["TTs%d" % hp, tk])

        def st_p0():
            P.op("dve", lambda e: e.tensor_tensor(Pm[hp][0][:], Gm[hp][:, 0:64], identst[:], ALU.add), reads=["Gm%d" % hp, "identst"], writes=["Pm%d_0" % hp])
        stages.append(both(both(st_gevac, st_p0), pend_y[1] if pend_y else None))
        state = dict(N=(Gm[hp][:, 0:64], "Gm%d" % hp), NT=(Gm[hp][:, 64:128], "Gm%d" % hp))
        def st_w():
            for h2 in range(2):
                hs = HS[h2]
                mm(Wps[hp][hs], Pp["At"][hs, sl], sp_[hs], True, False, [pk("At"), skp], [KB[hp]])
                mm(Wps[hp][hs], Gm[hp][hs, 128:192], TTs[hp][hs, 128:192], False, True, ["Gm%d" % hp, "TTs%d" % hp], [KB[hp]])
        st_wev = lambda: evac(Wsb[hp][:], Wps[hp], [KB[hp]], ["Wsb%d" % hp, KB[hp]], "act")

        L_sq, L_sqev, L_pu, L_puev = [], [], [], []
        for lvl in range(5):
            def st_sq(lvl=lvl):
                si = hp
                state["si"] = si
                (N, nk), (NT, ntk) = state["N"], state["NT"]
                for h2 in range(2):
                    hs = HS[h2]
                    mm(sqp[si][hs, 0:64], NT[hs], N[hs], True, True, [nk, ntk], [KB[hp]])
                    mm(sqp[si][hs, 64:128], N[hs], NT[hs], True, True, [nk, ntk], [KB[hp]])
            L_sq.append(st_sq)

            def st_sqev(lvl=lvl):
                si = state["si"]
                nb = Nn[hp][lvl % 2]
                nbk = "Nn%d_%d" % (hp, lvl % 2)
                evac(nb[:], sqp[si], [KB[hp]], [nbk, KB[hp]])
                state["N"] = (nb[:, 0:64], nbk)
                state["NT"] = (nb[:, 64:128], nbk)
            L_sqev.append(st_sqev)

            def st_pu(lvl=lvl):
                pi = hp
                state["pi"] = pi
                (NT, ntk) = state["NT"]
                pc = Pm[hp][lvl % 2]
                pck = "Pm%d_%d" % (hp, lvl % 2)
                for h2 in range(2):
                    hs = HS[h2]
                    mm(Pps[pi][hs], NT[hs], pc[hs], True, False, [ntk, pck], [KB[hp]])
                    mm(Pps[pi][hs], ident[hs, 64 * h2:64 * h2 + 64], pc[hs], False, True, ["ident", pck], [KB[hp]])
            L_pu.append(st_pu)

            def st_puev(lvl=lvl):
                pi = state["pi"]
                pn = Pm[hp][(lvl + 1) % 2]
                evac(pn[:], Pps[pi], [KB[hp]], ["Pm%d_%d" % (hp, (lvl + 1) % 2), KB[hp]])
            L_puev.append(st_puev)
        for l in range(6):
            sa = both(L_pu[l - 1] if l >= 1 else None, L_sq[l] if l < 5 else None)
            se = both(L_puev[l - 1] if l >= 1 else None, L_sqev[l] if l < 5 else None)
            if l == 0:
                sa, se = both(sa, st_w), both(se, st_wev)
            stages.append(sa)
            stages.append(se)
        Tm, Tk = Pm[hp][1], "Pm%d_1" % hp

        def st_u():
            for h2 in range(2):
                hs = HS[h2]
                mm(Ups[hp][hs], Tm[hs], Wsb[hp][hs], True, True, [Tk, "Wsb%d" % hp], [KB[hp]])
        stages.append(st_u)
        stages.append(lambda: evac(Usb[hp][:], Ups[hp], [KB[hp]], ["Usb%d" % hp, KB[hp]], "dve"))

        def st_ys():
            yi = hp
            state["yi"] = yi
            for h2 in range(2):
                hs = HS[h2]
                mm(Sps[yi][hs], TTs[hp][hs, 0:64], Usb[hp][hs], True, False, ["TTs%d" % hp, "Usb%d" % hp], [KB[hp]])
                mm(Sps[yi][hs], TTs[hp][hs, 64:128], TTs[hp][hs, 128:192], False, False, ["TTs%d" % hp], [KB[hp]])
                mm(Sps[yi][hs], ident[hs, 64 * h2:64 * h2 + 64], sp_[hs], False, True, ["ident", skp], [KB[hp]])
            for h2 in range(2):
                hs = HS[h2]
                mm(Yps[yi][hs], Pp["Rt"][hs, sl], sp_[hs], True, False, [pk("Rt"), skp], [KB[hp]])
                mm(Yps[yi][hs], Gm[hp][hs, 192:256], Usb[hp][hs], False, False, ["Gm%d" % hp, "Usb%d" % hp], [KB[hp]])
                mm(Yps[yi][hs], Gm[hp][hs, 256:320], TTs[hp][hs, 128:192], False, True, ["Gm%d" % hp, "TTs%d" % hp], [KB[hp]])
        stages.append(st_ys)

        def st_ysev():
            yi = state["yi"]
            ce = ch * 64 + 63
            P.op("act", lambda e: e.activation(sn_[:], Sps[yi], AF.Copy, scale=Pp["gam"][:, ce:ce + 1]),
                 reads=[KB[hp], pk("gam")], writes=[skn, KB[hp]])
            P.op("dve", lambda e: e.tensor_copy(Ysb[hp][:], Yps[yi]), reads=[KB[hp]], writes=["Ysb%d" % hp, KB[hp]])
        stages.append(st_ysev)

        def st_ytr():
            for h2 in range(2):
                hs = HS[h2]
                P.op("pe", lambda e, hs=hs, h2=h2: e.matmul(Ytr[hp][hs], Ysb[hp][hs], ident[hs, 64 * h2:64 * h2 + 64], start=True, stop=True),
                     reads=["Ysb%d" % hp, "ident"], writes=[KB[hp]])
        pending_y[hp] = (st_ytr, lambda: evac(ycm[hp][b][:, sl], Ytr[hp], [KB[hp]], ["ycm%d_%d" % (hp, b), KB[hp]], "act"))
        return stages

    def do_epilogue(hp, st):
        b = st % 2
        t0 = st * ST
        gt = inb["g"][hp][b]
        gk_ = "in_g_%d_%d" % (hp, b)
        bon = prep["bonus"][hp][b]
        bk = "pp_bonus_%d_%d" % (hp, b)
        col = lambda j: par[:, hp, j:j + 1]
        dve = lambda fn, r, w: P.op("dve", fn, reads=r, writes=w)
        act = lambda fn, r, w: P.op("act", fn, reads=r, writes=w)
        pool = lambda fn, r, w: P.op("dve", fn, reads=r, writes=w)
        yk = "ycm%d_%d" % (hp, b)
        sum_ps = bkA[hp][:, 0:256]
        ssq_ps = bkA[hp][:, 256:512]
        act(lambda e: e.activation(sc["ysq"][:], ycm[hp][b][:], AF.Square), [yk], ["ysq"])
        P.op("pe", lambda e: e.matmul(sum_ps, onesbd[:], ycm[hp][b][:], start=True, stop=True), reads=["onesbd", yk], writes=[KA[hp]])
        P.op("pe", lambda e: e.matmul(ssq_ps, onesbd[:], sc["ysq"][:], start=True, stop=True), reads=["onesbd", "ysq"], writes=[KA[hp]])
        dve(lambda e: e.tensor_scalar(sc["mean"][:], sum_ps, 1.0 / 64, None, ALU.mult), [KA[hp]], ["mean", KA[hp]])
        pool(lambda e: e.tensor_tensor(sc["msq"][:], sc["mean"][:], sc["mean"][:], ALU.mult), ["mean"], ["msq"])
        dve(lambda e: e.scalar_tensor_tensor(sc["var"][:], ssq_ps, 1.0 / 64, sc["msq"][:], ALU.mult, ALU.subtract), [KA[hp], "msq"], ["var", KA[hp]])
        act(lambda e: e.activation(sc["std"][:], sc["var"][:], AF.Sqrt, bias=GN_EPS, scale=1.0), ["var"], ["std"])
        dve(lambda e: e.reciprocal(sc["rstd"][:], sc["std"][:]), ["std"], ["rstd"])
        pool(lambda e: e.tensor_tensor(sc["yc"][:], ycm[hp][b][:], sc["mean"][:], ALU.subtract), [yk, "mean"], ["yc"])
        pool(lambda e: e.tensor_tensor(sc["yn"][:], sc["yc"][:], sc["rstd"][:], ALU.mult), ["yc", "rstd"], ["yn"])
        dve(lambda e: e.tensor_scalar(sc["z"][:], sc["yn"][:], col(3), col(4), ALU.mult, ALU.add), ["yn", "par"], ["z"])
        pool(lambda e: e.tensor_tensor(sc["z2"][:], sc["z"][:], bon[:], ALU.add), ["z", bk], ["z2"])
        oi = cnt["o"] % 2
        cnt["o"] += 1
        pool(lambda e: e.tensor_tensor(osb[oi][:], sc["z2"][:], gt[:], ALU.mult), ["z2", gk_], ["osb%d" % oi])
        P.op("sp", lambda e: e.dma_start(out=o_d[hp * 128:(hp + 1) * 128, t0:t0 + ST], in_=osb[oi][:]), reads=["osb%d" % oi], dma=True)

    import os
    NSTR = int(os.environ.get('RW_NST', NST))
    for hp in range(4):
        do_prep(hp, 0)
    for st in range(NSTR):
        def side(st=st):
            if st >= 1:
                for hp in range(4):
                    do_epilogue(hp, st - 1)
            if st + 1 < NSTR:
                for hp in range(4):
                    do_prep(hp, st + 1)
        pend = P.deferred(side)
        nslots = NCH * 15
        per = (len(pend) + nslots - 1) // nslots + 1
        for ch in range(NCH):
            sts = [chunk_stages(hp, st, ch) for hp in range(4)]
            for i in range(len(sts[0])):
                for hp in range(4):
                    sts[hp][i]()
                if i >= 1:
                    P.flush(pend, per)
        P.flush(pend, len(pend))
    for hp in range(4):
        pending_y[hp][0]()
    for hp in range(4):
        pending_y[hp][1]()
    for hp in range(4):
        do_epilogue(hp, NSTR - 1)
    return P.finish()


def rwkv_consts():
    ident = np.eye(128, dtype=np.float32)
    onesbd = np.zeros((128, 128), np.float32)
    onesbd[:64, :64] = 1.0
    onesbd[64:, 64:] = 1.0
    i = (np.arange(128) % 64)[:, None]
    t = np.arange(64)[None, :]
    su = (i < t).astype(np.float32)
    sl_ = (t < i).astype(np.float32)
    ui = (i <= t).astype(np.float32)
    mask = np.concatenate([su, sl_, su, ui, ui], axis=1)
    identst = (i == t).astype(np.float32)
    resetm = np.ones((128, RW_ST), np.float32)
    resetm[:, ::64] = 0.0
    return dict(ident=ident, onesbd=onesbd, mask320=np.ascontiguousarray(mask), identst=identst, resetm=resetm)


def run_rwkv_core_stage(proj, inp, ic):
    if "rwkvc" not in _cache:
        _cache["rwkvc"] = build_rwkv_core_stage()
    consts = rwkv_consts()
    pv = np.stack([inp["rwkv_k_k"][ic], inp["rwkv_k_a"][ic], inp["rwkv_r_k"][ic].reshape(-1), inp["rwkv_ln_w"][ic], inp["rwkv_ln_b"][ic]], axis=1)
    maps = []
    for c in range(NCORES):
        seq, hq = divmod(c, 4)
        m = dict(consts)
        for n in ("r", "k", "v", "ld", "a", "g"):
            m[n] = np.ascontiguousarray(np.concatenate([proj[4 * seq + i][n][512 * hq:512 * hq + 512, :] for i in range(4)], axis=1))
        m["par"] = np.ascontiguousarray(pv[512 * hq:512 * hq + 512].reshape(4, 128, 5).transpose(1, 0, 2)).astype(np.float32)
        maps.append(m)
    res = _run(_cache["rwkvc"], maps)
    ys = [r["oT"] for r in res.results]
    out = []
    for c in range(NCORES):
        seq, i = divmod(c, 4)
        out.append(np.ascontiguousarray(np.concatenate([ys[4 * seq + hq][:, i * TOK:(i + 1) * TOK] for hq in range(4)], axis=0)))
    return out


def kernel(**inputs):
    inp = {k: np.asarray(v) for k, v in inputs.items()}
    x = inp["x"].reshape(NCORES * TOK, D)
    xT = [np.ascontiguousarray(x[c * TOK:(c + 1) * TOK].T) for c in range(NCORES)]
    ia = ib = ic = 0
    depth = inp["norm_mix"].shape[0]
    for layer in range(depth):
        kind = layer % 3
        gfin = inp["norm_f"] if layer == depth - 1 else None
        if kind == 0:
            qkv = run_qkv_stage(xT, inp["attn_w_qkv"][ia], inp["norm_mix"][layer])
            aT = run_attn_stage(qkv)
            wo = inp["attn_w_o"][ia]
            ia += 1
        elif kind == 1:
            uT = run_normproj_stage(xT, inp["ssm_w_in"][ib], inp["norm_mix"][layer])
            aT = run_s5_stage(uT, inp["ssm_log_dt"][ib], inp["ssm_a_re"][ib], inp["ssm_a_im"][ib], inp["ssm_b_re"][ib],
                              inp["ssm_b_im"][ib], inp["ssm_c_re"][ib], inp["ssm_c_im"][ib], inp["ssm_d"][ib])
            wo = inp["ssm_w_out"][ib]
            ib += 1
        else:
            proj = run_rwkv_proj_stage(xT, inp, layer, ic)
            aT = run_rwkv_core_stage(proj, inp, ic)
            wo = inp["rwkv_w_o"][ic]
            ic += 1
        xT = run_mlp_stage(xT, inp["mlp_w1"][layer], inp["mlp_w2"][layer], inp["norm_mlp"][layer], g_final=gfin,
                           aT_cores=aT, wo=wo)
    out = np.concatenate([np.asarray(t).T for t in xT], axis=0).reshape(inp["x"].shape)
    return np.ascontiguousarray(out.astype(np.float32))
```
